# Optimizing a Trainium2 kernel written in Bass

```python
import math
import jax, jax.numpy as jnp
from jax import lax
import numpy as np

D_MODEL = 1024
BATCH = 1
SEQ = 16384
DEPTH = 2

A_HEADS = 4
A_HEAD_DIM = 64
A_WIDTH = A_HEADS * 2 * A_HEAD_DIM
Q_BLOCK = 128
B_WIDTH = 512
B_GROUPS = 4
B_GROUP_DIM = B_WIDTH // B_GROUPS
B_CHUNK = 128
C_HEADS = 8
C_HEAD_DIM = 64
C_WIDTH = C_HEADS * C_HEAD_DIM
C_PATTERNS = ((128, 1), (512, 4), (2048, 16))
C_BLOCK = 128
N_BRANCH = 3
BRANCH_WIDTH = 512
IN_COLS = 4 * A_WIDTH + 3 * B_WIDTH + 4 * C_WIDTH + N_BRANCH * D_MODEL
ROPE_THETA = 500000.0
ROPE_FRAC = 4
EPS = 1e-6

kernel_name = 'hybrid_diffattn_sgu_dilated_gated_merge'


def rmsnorm(t, g):
    tf = t.astype(jnp.float32)
    y = tf * lax.rsqrt(jnp.mean(tf * tf, axis=-1, keepdims=True) + EPS)
    return (y * g.astype(jnp.float32)).astype(t.dtype)


def layernorm(t, g, b):
    tf = t.astype(jnp.float32)
    mu = jnp.mean(tf, axis=-1, keepdims=True)
    var = jnp.mean(jnp.square(tf - mu), axis=-1, keepdims=True)
    y = (tf - mu) * lax.rsqrt(var + EPS)
    return (y * g.astype(jnp.float32) + b.astype(jnp.float32)).astype(t.dtype)


def partial_rope(t, positions):
    dh = t.shape[-1]
    rot = dh // ROPE_FRAC
    half = rot // 2
    inv = jnp.power(jnp.float32(ROPE_THETA), -jnp.arange(half, dtype=jnp.float32) * 2.0 / rot)
    ang = positions.astype(jnp.float32)[:, :, None] * inv
    cos = jnp.cos(ang)[:, :, None, :]
    sin = jnp.sin(ang)[:, :, None, :]
    tf = t.astype(jnp.float32)
    x1 = tf[..., :half]
    x2 = tf[..., half:rot]
    out = jnp.concatenate([x1 * cos - x2 * sin, x2 * cos + x1 * sin, tf[..., rot:]], axis=-1)
    return out.astype(t.dtype)


def split_cols(proj):
    sizes = [A_WIDTH, A_WIDTH, A_WIDTH, A_WIDTH,
             2 * B_WIDTH, B_WIDTH,
             C_WIDTH, C_WIDTH, C_WIDTH, C_WIDTH,
             N_BRANCH * D_MODEL]
    idx = np.cumsum(sizes)[:-1].tolist()
    return jnp.split(proj, idx, axis=-1)


def diff_attention(q, k, v, lam):
    B, S, H, _, d = q.shape
    nb = S // Q_BLOCK
    scale = 1.0 / math.sqrt(d)
    qb = q.reshape(B, nb, Q_BLOCK, H, 2, d).transpose(1, 0, 2, 3, 4, 5)
    kpos = jnp.arange(S)

    def block(args):
        i, qblk = args
        s = jnp.einsum('bqhmd,bkhmd->bhmqk', qblk, k,
                       preferred_element_type=jnp.float32) * scale
        qpos = i * Q_BLOCK + jnp.arange(Q_BLOCK)
        mask = kpos[None, :] <= qpos[:, None]
        s = jnp.where(mask, s, -jnp.inf)
        p = jax.nn.softmax(s, axis=-1)
        pd = p[:, :, 0] - lam * p[:, :, 1]
        return jnp.einsum('bhqk,bkhe->bqhe', pd.astype(v.dtype), v)

    out = lax.map(block, (jnp.arange(nb), qb))
    return out.transpose(1, 0, 2, 3, 4).reshape(B, S, H, 2 * d)


def dilated_window_attention(q, k, v, window, dil):
    B, S, H, dh = q.shape
    L = S // dil
    n_back = window // dil
    nb = -(-L // C_BLOCK)
    Lp = nb * C_BLOCK
    BD = B * dil
    scale = 1.0 / math.sqrt(dh)

    def to_sub(t):
        t = t.reshape(B, L, dil, H, dh).transpose(0, 2, 1, 3, 4).reshape(BD, L, H, dh)
        return jnp.pad(t, ((0, 0), (0, Lp - L), (0, 0), (0, 0)))

    def windows(t):
        t = jnp.pad(t, ((0, 0), (C_BLOCK, 0), (0, 0), (0, 0))).reshape(BD, nb + 1, C_BLOCK, H, dh)
        return jnp.concatenate([t[:, :-1], t[:, 1:]], axis=2)

    qb = to_sub(q).reshape(BD, nb, C_BLOCK, H, dh)
    kw = windows(to_sub(k))
    vw = windows(to_sub(v))
    s = jnp.einsum('bnqhd,bnkhd->bnhqk', qb, kw,
                   preferred_element_type=jnp.float32) * scale
    qi = jnp.arange(nb)[:, None, None] * C_BLOCK + jnp.arange(C_BLOCK)[None, :, None]
    ki = jnp.arange(nb)[:, None, None] * C_BLOCK - C_BLOCK + jnp.arange(2 * C_BLOCK)[None, None, :]
    rel = qi - ki
    mask = (rel >= 0) & (rel <= n_back) & (ki >= 0)
    s = jnp.where(mask[None, :, None], s, -jnp.inf)
    m = jnp.max(s, axis=-1, keepdims=True)
    e = jnp.exp(s - m)
    den = jnp.sum(e, axis=-1)
    o = jnp.einsum('bnhqk,bnkhd->bnqhd', e, vw.astype(jnp.float32))
    den_q = den.transpose(0, 1, 3, 2)
    o = o / den_q[..., None]
    lse = m[..., 0].transpose(0, 1, 3, 2) + jnp.log(den_q)

    def from_sub(t):
        rest = t.shape[3:]
        t = t.reshape((B, dil, Lp) + rest)[:, :, :L]
        t = jnp.moveaxis(t, 1, 2)
        return t.reshape((B, S) + rest)

    return from_sub(o), from_sub(lse)


def hybrid_layer(x, positions, layer_idx, norm_g, w_in, lam_q1, lam_k1, lam_q2, lam_k2,
                 subln_g, sgu_ln_g, sgu_ln_b, sgu_w, sgu_b, w_branch, w_out):
    B, S, D = x.shape
    h = rmsnorm(x, norm_g)
    proj = jnp.einsum('bsd,dc->bsc', h, w_in)
    aq, ak, av, az, buv, bz, cq, ck, cv, cz, gl = split_cols(proj)

    lam_init = 0.8 - 0.6 * math.exp(-0.3 * layer_idx)
    lam = (jnp.exp(jnp.sum(lam_q1.astype(jnp.float32) * lam_k1.astype(jnp.float32)))
           - jnp.exp(jnp.sum(lam_q2.astype(jnp.float32) * lam_k2.astype(jnp.float32)))
           + lam_init)
    qa = partial_rope(aq.reshape(B, S, 2 * A_HEADS, A_HEAD_DIM), positions).reshape(B, S, A_HEADS, 2, A_HEAD_DIM)
    ka = partial_rope(ak.reshape(B, S, 2 * A_HEADS, A_HEAD_DIM), positions).reshape(B, S, A_HEADS, 2, A_HEAD_DIM)
    va = av.reshape(B, S, A_HEADS, 2 * A_HEAD_DIM)
    oa = diff_attention(qa, ka, va, lam)
    oa = rmsnorm(oa, subln_g) * (1.0 - lam_init)
    ya = oa.reshape(B, S, A_WIDTH) * jax.nn.silu(az)

    uv = jax.nn.gelu(buv)
    u, vb = uv[..., :B_WIDTH], uv[..., B_WIDTH:]
    vb = layernorm(vb, sgu_ln_g, sgu_ln_b).reshape(B, S // B_CHUNK, B_CHUNK, B_GROUPS, B_GROUP_DIM)
    w_causal = sgu_w * jnp.tril(jnp.ones((B_CHUNK, B_CHUNK), dtype=sgu_w.dtype))
    mixed = jnp.einsum('gts,bnsgc->bntgc', w_causal, vb) + sgu_b.T[None, None, :, :, None]
    yb = u * mixed.reshape(B, S, B_WIDTH) * jax.nn.silu(bz)

    qc = partial_rope(cq.reshape(B, S, C_HEADS, C_HEAD_DIM), positions)
    kc = partial_rope(ck.reshape(B, S, C_HEADS, C_HEAD_DIM), positions)
    vc = cv.reshape(B, S, C_HEADS, C_HEAD_DIM)
    outs = []
    lses = []
    for window, dil in C_PATTERNS:
        o_p, lse_p = dilated_window_attention(qc, kc, vc, window, dil)
        outs.append(o_p)
        lses.append(lse_p)
    wts = jax.nn.softmax(jnp.stack(lses, axis=0), axis=0)
    oc = jnp.sum(wts[..., None] * jnp.stack(outs, axis=0), axis=0).astype(x.dtype)
    yc = oc.reshape(B, S, C_WIDTH) * jax.nn.silu(cz)

    ys = jnp.stack([ya, yb, yc], axis=0)
    pb = jnp.einsum('nbsc,ncd->nbsd', ys, w_branch)
    gates = jax.nn.sigmoid(gl.reshape(B, S, N_BRANCH, D))
    merged = jnp.einsum('bsnd,nbsd->bsd', gates, pb)
    return x + jnp.einsum('bsd,de->bse', merged, w_out)


def setup_inputs(seed: int = 0) -> dict:
    key = jax.random.key(seed)
    ks = jax.random.split(key, 16)
    f32 = jnp.float32
    x = jax.random.normal(ks[0], (BATCH, SEQ, D_MODEL), f32)
    positions = jnp.broadcast_to(jnp.arange(SEQ, dtype=jnp.int32)[None, :], (BATCH, SEQ))
    norm_g = 1.0 + 0.05 * jax.random.normal(ks[1], (DEPTH, D_MODEL), f32)
    w_in = jax.random.normal(ks[2], (DEPTH, D_MODEL, IN_COLS), f32) * D_MODEL ** -0.5
    lam_q1 = 0.1 * jax.random.normal(ks[3], (DEPTH, A_HEAD_DIM), f32)
    lam_k1 = 0.1 * jax.random.normal(ks[4], (DEPTH, A_HEAD_DIM), f32)
    lam_q2 = 0.1 * jax.random.normal(ks[5], (DEPTH, A_HEAD_DIM), f32)
    lam_k2 = 0.1 * jax.random.normal(ks[6], (DEPTH, A_HEAD_DIM), f32)
    subln_g = 1.0 + 0.05 * jax.random.normal(ks[7], (DEPTH, 2 * A_HEAD_DIM), f32)
    sgu_ln_g = 1.0 + 0.05 * jax.random.normal(ks[8], (DEPTH, B_WIDTH), f32)
    sgu_ln_b = 0.02 * jax.random.normal(ks[9], (DEPTH, B_WIDTH), f32)
    sgu_w = jax.random.normal(ks[10], (DEPTH, B_GROUPS, B_CHUNK, B_CHUNK), f32) * B_CHUNK ** -0.5
    sgu_b = 1.0 + 0.1 * jax.random.normal(ks[11], (DEPTH, B_GROUPS, B_CHUNK), f32)
    w_branch = jax.random.normal(ks[12], (DEPTH, N_BRANCH, BRANCH_WIDTH, D_MODEL), f32) * BRANCH_WIDTH ** -0.5
    w_out = jax.random.normal(ks[13], (DEPTH, D_MODEL, D_MODEL), f32) * (2.0 * D_MODEL) ** -0.5
    final_g = 1.0 + 0.05 * jax.random.normal(ks[14], (D_MODEL,), f32)
    return {'x': x, 'positions': positions, 'norm_g': norm_g, 'w_in': w_in,
            'lam_q1': lam_q1, 'lam_k1': lam_k1, 'lam_q2': lam_q2, 'lam_k2': lam_k2,
            'subln_g': subln_g, 'sgu_ln_g': sgu_ln_g, 'sgu_ln_b': sgu_ln_b,
            'sgu_w': sgu_w, 'sgu_b': sgu_b, 'w_branch': w_branch, 'w_out': w_out,
            'final_g': final_g}


def reference(x, positions, norm_g, w_in, lam_q1, lam_k1, lam_q2, lam_k2, subln_g,
              sgu_ln_g, sgu_ln_b, sgu_w, sgu_b, w_branch, w_out, final_g):
    h = x
    for l in range(DEPTH):
        h = hybrid_layer(h, positions, l, norm_g[l], w_in[l], lam_q1[l], lam_k1[l],
                         lam_q2[l], lam_k2[l], subln_g[l], sgu_ln_g[l], sgu_ln_b[l],
                         sgu_w[l], sgu_b[l], w_branch[l], w_out[l])
    return rmsnorm(h, final_g)
```

```python
import contextlib
import math
import numpy as np
import ml_dtypes
import concourse.bass as bass
import concourse.mybir as mybir
from concourse.bass_utils import run_bass_kernel_spmd

F32 = mybir.dt.float32
BF16 = mybir.dt.bfloat16
I32 = mybir.dt.int32
AF = mybir.ActivationFunctionType
ALU = mybir.AluOpType
AX = mybir.AxisListType

NCORES = 8
D = 1024
S = 16384
SOWN = S // NCORES
NT = SOWN // 128
INC = 8704
EPS = 1e-6
C_AQ, C_AK, C_AV, C_AZ, C_BU, C_BV, C_BZ, C_CQ, C_CK, C_CV, C_CZ, C_GL = (
    0, 512, 1024, 1536, 2048, 2560, 3072, 3584, 4096, 4608, 5120, 5632)
R_KAT, R_QAT, R_VA, R_KCT, R_VC, R_TOT = 0, 512, 1024, 1536, 2048, 2560


class Prog:
    CE = ("pe", "act", "dve", "pool")

    def __init__(self):
        self.ops = []
        self.lastw = {}
        self.rd_c = {}
        self.rd_d = {}
        self.sp_init = None
        self.regs = {}
        self.base = set()
        self.base_pending = set()
        self.since_dma = []
        self.last_c = {}

    def barrier(self):
        self.base = set(self.last_c.values()) | set(self.since_dma)
        self.since_dma = []
        self.base_pending = {"pe", "act", "dve", "pool", "sp"}

    def _add(self, eng, fn, r, w, kind):
        w = list(w) + [k for k in r if k.startswith("pb") and k not in w]
        idx = len(self.ops)
        deps = set()
        for k in r:
            if k in self.lastw:
                deps.add(self.lastw[k])
        for k in w:
            if k in self.lastw:
                deps.add(self.lastw[k])
            deps.update(self.rd_c.get(k, {}).values())
            deps.update(self.rd_d.get(k, ()))
        if eng in self.base_pending:
            deps |= self.base
            self.base_pending.discard(eng)
        if kind == "c":
            self.last_c[eng] = idx
        else:
            self.since_dma.append(idx)
        import sys as _s
        self.ops.append(dict(eng=eng, fn=fn, deps=deps, kind=kind, line=_s._getframe(2).f_lineno))
        for k in r:
            if kind == "c":
                self.rd_c.setdefault(k, {})[eng] = idx
            else:
                self.rd_d.setdefault(k, []).append(idx)
        for k in w:
            self.lastw[k] = idx
            self.rd_c[k] = {}
            self.rd_d[k] = []
        return idx

    def op(self, eng, fn, r=(), w=()):
        return self._add(eng, fn, r, w, "c")

    def dma(self, q, fn, r=(), w=(), inc=16, semq=None):
        i = self._add(q, fn, r, w, "d")
        self.ops[i]["inc"] = inc
        self.ops[i]["semq"] = semq or q
        return i

    def emit(self, nc, st, ndma={"sp": 24, "pool": 8, "act": 4, "cc": 2}, limit=None):
        ops = self.ops if limit is None else self.ops[:limit]
        need = [False] * len(ops)
        for o in ops:
            for d in o["deps"]:
                dd = ops[d]
                if dd["kind"] == "c" and not (dd["eng"] == "pe" and o["eng"] == "pe" and o["kind"] == "c"):
                    need[d] = True
        csem = {e: st.enter_context(nc.semaphore("c_" + e)) for e in self.CE}
        dsem = {q: [st.enter_context(nc.semaphore(f"d_{q}{i}")) for i in range(n)] for q, n in ndma.items()}
        cnt = {e: 0 for e in self.CE}
        dcnt = {q: 0 for q in ndma}
        dval = {q: [0] * n for q, n in ndma.items()}
        for i, o in enumerate(ops):
            if o["kind"] == "c":
                if need[i]:
                    cnt[o["eng"]] += 1
                    o["sig"] = cnt[o["eng"]]
            else:
                q = o["semq"]
                slot = dcnt[q] % ndma[q]
                dcnt[q] += 1
                o["slot"] = slot
                o["prev"] = dval[q][slot]
                dval[q][slot] += o["inc"]
                o["val"] = dval[q][slot]
        per = {e: [] for e in ("pe", "act", "dve", "pool", "sp")}
        for i, o in enumerate(ops):
            per[o["eng"]].append(i)
        block = st.enter_context(nc.Block())

        def run(ename, e):
            waited = {}
            if ename == "sp" and self.sp_init is not None:
                self.sp_init(e)
            for i in per[ename]:
                o = ops[i]
                needw = {}
                for d in o["deps"]:
                    dd = ops[d]
                    if dd["kind"] == "c":
                        if dd["eng"] == "pe" and ename == "pe" and o["kind"] == "c":
                            continue
                        key = ("c", dd["eng"]); val = dd["sig"]
                    else:
                        key = ("d", dd["semq"], dd["slot"]); val = dd["val"]
                    if needw.get(key, 0) < val:
                        needw[key] = val
                if o["kind"] == "d" and o["prev"] > 0:
                    key = ("d", o["semq"], o["slot"])
                    if needw.get(key, 0) < o["prev"]:
                        needw[key] = o["prev"]
                for key, val in needw.items():
                    if waited.get(key, 0) < val:
                        sem = csem[key[1]] if key[0] == "c" else dsem[key[1]][key[2]]
                        e.wait_ge(sem, val)
                        waited[key] = val
                ins = o["fn"](e)
                if o["kind"] == "c":
                    if need[i]:
                        ins.then_inc(csem[ename], 1)
                else:
                    ins.then_inc(dsem[o["semq"]][o["slot"]], o["inc"])
            for qn in ([ename] + (["cc"] if ename == "pool" else [])):
                if qn in ndma:
                    for slot, v in enumerate(dval[qn]):
                        if v > 0 and waited.get(("d", qn, slot), 0) < v:
                            e.wait_ge(dsem[qn][slot], v)

        @block.tensor
        def _(e):
            run("pe", e)

        @block.scalar
        def _(e):
            run("act", e)

        @block.vector
        def _(e):
            run("dve", e)

        @block.gpsimd
        def _(e):
            run("pool", e)

        @block.sync
        def _(e):
            run("sp", e)


def bc_free(ap, pos, n):
    a = ap.unsqueeze(pos)
    shp = list(a.shape)
    shp[pos] = n
    return a.to_broadcast(shp)


class Builder:
    def __init__(self, layers=(0, 1), final_norm=True, dbg=None, stop_after=None):
        self.layers = layers
        self.final_norm = final_norm
        self.dbg = dbg
        self.stop_after = stop_after
        self.limit = None
        self.nc = bass.Bass("TRN2", target_bir_lowering=False)
        self.P = Prog()
        self.uid = 0

    def din(self, name, shape, dt):
        return self.nc.dram_tensor(name, list(shape), dt, kind="ExternalInput").ap()

    def dout(self, name, shape, dt):
        return self.nc.dram_tensor(name, list(shape), dt, kind="ExternalOutput").ap()

    def dint(self, name, shape, dt):
        return self.nc.dram_tensor(name, list(shape), dt).ap()

    def sb(self, name, shape, dt):
        return self.st.enter_context(self.nc.sbuf_tensor("sb_" + name, list(shape), dt))

    def psb(self, name, shape, dt):
        self.uid += 1
        return self.ph.enter_context(self.nc.sbuf_tensor(f"ph{self.uid}_" + name, list(shape), dt))

    def ps(self, name, shape, dt):
        return self.st.enter_context(self.nc.psum_tensor("ps_" + name, list(shape), dt))

    def build(self):
        nc = self.nc
        with contextlib.ExitStack() as st:
            self.st = st
            self.declare_io()
            self.alloc()

            def sp_init(e):
                pid = e.partition_id()
                self.P.regs["p128"] = e.snap(pid * 128)
                self.P.regs["p256"] = e.snap(pid * 256)
                self.P.regs["prow"] = e.snap(pid * R_TOT)
                self.P.regs["p1"] = e.snap(pid * 1)
            self.P.sp_init = sp_init
            self.setup_consts()
            for l in self.layers:
                self.layer(l)
            self.P.emit(nc, st, limit=self.limit)
        return nc

    def declare_io(self):
        L = 2
        self.x_in = self.din("x", [SOWN, D], F32)
        self.pos_in = self.din("pos", [128, NT], I32)
        self.cidf_in = self.din("cidf", [128, 1], F32)
        self.norm_g = self.din("norm_g", [L, D], F32)
        self.w_in = self.din("w_in", [L, D, INC], F32)
        self.lam4 = [self.din(n, [L, 64], F32) for n in ("lam_q1", "lam_k1", "lam_q2", "lam_k2")]
        self.subln_g = self.din("subln_g", [L, 128], F32)
        self.sgu_ln_g = self.din("sgu_ln_g", [L, 512], F32)
        self.sgu_ln_b = self.din("sgu_ln_b", [L, 512], F32)
        self.sgu_w = self.din("sgu_w", [L, 4, 128, 128], F32)
        self.sgu_b = self.din("sgu_b", [L, 4, 128], F32)
        self.w_branch = self.din("w_branch", [L, 3, 512, D], F32)
        self.w_out = self.din("w_out", [L, D, D], F32)
        self.final_g = self.din("final_g", [D], F32)
        self.out = self.dout("out", [SOWN, D], F32)
        self.contrib = [self.dint(f"contrib{l}", [R_TOT, 2048], BF16) for l in range(2)]
        self.kvall = [self.dint(f"kvall{l}", [(NCORES + 1) * R_TOT, 2048], BF16) for l in range(2)]
        self.contrib2 = [self.dint(f"contribb{l}", [512, 2048], BF16) for l in range(2)]
        self.yaall = [self.dint(f"yaall{l}", [NCORES * 512, 2048], BF16) for l in range(2)]
        self.x1 = self.dint("x1buf", [SOWN, D], F32)
        self.qown = [self.dint(f"qown{l}", [512, 2048], BF16) for l in range(2)]
        self.cown = [self.dint(f"cown{l}", [2, 1024, 2048], BF16) for l in range(2)]
        self.yaown = [self.dint(f"yaown{l}", [512, 2048], BF16) for l in range(2)]
        if self.dbg:
            self.dbg_out = {k: self.dout("dbg_" + k, shp, dt) for k, (shp, dt) in self.dbg.items()}

    def alloc(self):
        sb, ps = self.sb, self.ps
        self.ident = sb("ident", [128, 128], BF16)
        self.ones = sb("ones", [128, 128], BF16)
        self.onesf = sb("onesf", [128, 128], F32)
        self.zeros = sb("zeros", [128, 1024], BF16)
        self.invf = sb("invf", [128, 8], F32)
        self.posi = sb("posi", [128, NT], I32)
        self.posf = sb("posf", [128, NT], F32)
        self.cs = sb("cs", [128, NT, 16], F32)
        self.sn = sb("sn", [128, NT, 16], F32)
        self.ang = sb("ang", [128, NT, 16], F32)
        self.cidf = sb("cidf", [128, 1], F32)
        self.rk = sb("rk", [128, NT, 8], F32)
        self.rki = sb("rki", [128, NT, 8], I32)
        self.rm = sb("rm", [128, NT, 8], F32)
        self.gT = sb("gT", [128, 8], F32)
        self.hT = sb("hT", [128, 8, SOWN], BF16)
        self.ybT = sb("ybT", [128, 4, SOWN], BF16)
        self.ycT = sb("ycT", [128, 4, SOWN], BF16)
        self.maskC = sb("maskC", [128, 17, 128], BF16)
        self.relb = sb("relb", [128, 128], F32)
        self.maskA = sb("maskA", [128, 8, 128], BF16)
        self.vhalo = sb("vhalo", [128, 128], BF16)
        self.small = sb("small", [128, 64], F32)
        self.pbank = [ps(f"pb{i}", [128, 512], F32) for i in range(8)]

    def pbf(self, i):
        return self.pbank[i][:, :].bitcast(BF16)

    def setup_consts(self):
        P = self.P
        ident, ones, onesf, zeros = self.ident, self.ones, self.onesf, self.zeros
        P.op("pool", lambda e: e.memset(ones[:, :], 1.0), w=["ones"])
        P.op("pool", lambda e: e.memset(onesf[:, :], 1.0), w=["onesf"])
        P.op("pool", lambda e: e.memset(zeros[:, :], 0.0), w=["zeros"])
        P.op("pool", lambda e: e.affine_select(out=ident[:, :], in_=ones[:, :], pattern=[[1, 128]],
                                               compare_op=ALU.is_equal, fill=0.0, base=0,
                                               channel_multiplier=-1), r=["ones"], w=["ident"])
        for i in range(8):
            v = float(500000.0 ** (-i / 8.0))
            P.op("pool", lambda e, i=i, v=v: e.memset(self.invf[:, i:i + 1], v), w=[f"invf{i}"])
        invk = [f"invf{i}" for i in range(8)]
        P.dma("sp", lambda e: e.dma_start(out=self.posi[:, :], in_=self.pos_in[:, :]), w=["posi"])
        P.dma("sp", lambda e: e.dma_start(out=self.cidf[:, :], in_=self.cidf_in[:, :]), w=["cidf"])
        P.op("dve", lambda e: e.tensor_copy(out=self.posf[:, :], in_=self.posi[:, :]), r=["posi"], w=["posf"])
        for t in range(NT):
            P.op("dve", lambda e, t=t: e.tensor_scalar(out=self.ang[:, t, 0:8], in0=self.invf[:, :],
                                                      scalar1=self.posf[:, t:t + 1], scalar2=None,
                                                      op0=ALU.mult),
                 r=invk + ["posf"], w=[f"ang{t}"])
        angk = [f"ang{t}" for t in range(NT)]
        a8 = self.ang[:, :, 0:8]
        PI = math.pi
        C1 = 6.28125
        C2 = 2.0 * math.pi - C1

        def sin_of(dst, shift, tag):
            r_ = self.ang[:, :, 8:16]
            kf = self.rk[:, :, :]
            ki = self.rki[:, :, :]
            m = self.rm[:, :, :]
            P.op("dve", lambda e: e.tensor_scalar(out=r_, in0=a8, scalar1=shift, scalar2=None, op0=ALU.add),
                 r=angk + ["angs"], w=["angs"])
            P.op("dve", lambda e: e.tensor_scalar(out=kf, in0=r_, scalar1=1.0 / (2.0 * PI), scalar2=None, op0=ALU.mult),
                 r=["angs"], w=["rk"])
            P.op("dve", lambda e: e.tensor_copy(out=ki, in_=kf), r=["rk"], w=["rki"])
            P.op("dve", lambda e: e.tensor_copy(out=kf, in_=ki), r=["rki"], w=["rk"])
            P.op("dve", lambda e: e.scalar_tensor_tensor(out=r_, in0=kf, scalar=-C1, in1=r_, op0=ALU.mult, op1=ALU.add),
                 r=["rk", "angs"], w=["angs"])
            P.op("dve", lambda e: e.scalar_tensor_tensor(out=r_, in0=kf, scalar=-C2, in1=r_, op0=ALU.mult, op1=ALU.add),
                 r=["rk", "angs"], w=["angs"])
            P.op("dve", lambda e: e.tensor_scalar(out=m, in0=r_, scalar1=PI, scalar2=-2.0 * PI, op0=ALU.is_gt, op1=ALU.mult),
                 r=["angs"], w=["rm"])
            P.op("dve", lambda e: e.tensor_tensor(out=r_, in0=r_, in1=m, op=ALU.add), r=["angs", "rm"], w=["angs"])
            P.op("dve", lambda e: e.tensor_scalar(out=m, in0=r_, scalar1=-PI, scalar2=2.0 * PI, op0=ALU.is_lt, op1=ALU.mult),
                 r=["angs"], w=["rm"])
            P.op("dve", lambda e: e.tensor_tensor(out=r_, in0=r_, in1=m, op=ALU.add), r=["angs", "rm"], w=["angs"])
            P.op("dve", lambda e: e.tensor_scalar(out=r_, in0=r_, scalar1=-PI, scalar2=PI, op0=ALU.max, op1=ALU.min),
                 r=["angs"], w=["angs"])
            P.op("act", lambda e: e.activation(out=dst, in_=r_, func=AF.Sin), r=["angs"], w=[tag])

        sin_of(self.sn[:, :, 8:16], 0.0, "sn_hi")
        P.op("dve", lambda e: e.tensor_scalar(out=self.sn[:, :, 0:8], in0=self.sn[:, :, 8:16], scalar1=-1.0,
                                              scalar2=None, op0=ALU.mult), r=["sn_hi"], w=["sn_lo"])
        sin_of(self.cs[:, :, 0:8], 0.5 * PI, "cs_lo")
        P.op("dve", lambda e: e.tensor_copy(out=self.cs[:, :, 8:16], in_=self.cs[:, :, 0:8]), r=["cs_lo"], w=["cs_hi"])
        self.k_rope = ["sn_hi", "sn_lo", "cs_lo", "cs_hi"]
        self.setup_masks()

    def setup_masks(self):
        P = self.P
        relb, small = self.relb, self.small
        reli = self.sb("reli", [128, 128], I32)
        tmpa = self.sb("mtmpa", [128, 128], F32)
        tmpb = self.sb("mtmpb", [128, 128], F32)
        tmpc = self.sb("mtmpc", [128, 128], F32)
        tmpi = self.sb("mtmpi", [128, 128], I32)
        P.op("pool", lambda e: e.iota(out=reli[:, :], pattern=[[1, 128]], base=0, channel_multiplier=-1), w=["reli"])
        P.op("dve", lambda e: e.tensor_copy(out=relb[:, :], in_=reli[:, :]), r=["reli"], w=["relb"])
        def band(Dl, hi):
            c0 = float(128 * Dl)
            P.op("dve", lambda e: e.tensor_scalar(out=tmpa[:, :], in0=relb[:, :], scalar1=c0, scalar2=0.0,
                                                  op0=ALU.add, op1=ALU.is_ge), r=["relb"], w=["mta"])
            P.op("dve", lambda e: e.tensor_scalar(out=tmpb[:, :], in0=relb[:, :], scalar1=c0, scalar2=float(hi),
                                                  op0=ALU.add, op1=ALU.is_le), r=["relb"], w=["mtb"])
            P.op("dve", lambda e: e.tensor_tensor(out=tmpa[:, :], in0=tmpa[:, :], in1=tmpb[:, :], op=ALU.mult),
                 r=["mta", "mtb"], w=["mta"])

        def lattice(Dl, dil):
            c0 = float(128 * Dl + 4096)
            P.op("dve", lambda e: e.tensor_scalar(out=tmpc[:, :], in0=relb[:, :], scalar1=c0, scalar2=None,
                                                  op0=ALU.add), r=["relb"], w=["mtc"])
            P.op("dve", lambda e: e.tensor_copy(out=tmpi[:, :], in_=tmpc[:, :]), r=["mtc"], w=["mti"])
            P.op("dve", lambda e: e.tensor_single_scalar(out=tmpi[:, :], in_=tmpi[:, :], scalar=dil - 1, op=ALU.bitwise_and),
                 r=["mti"], w=["mti"])
            P.op("dve", lambda e: e.tensor_copy(out=tmpc[:, :], in_=tmpi[:, :]), r=["mti"], w=["mtc"])
            P.op("dve", lambda e: e.tensor_scalar(out=tmpb[:, :], in0=tmpc[:, :], scalar1=0.0, scalar2=None, op0=ALU.is_equal),
                 r=["mtc"], w=["mtb"])
            P.op("dve", lambda e: e.tensor_tensor(out=tmpa[:, :], in0=tmpa[:, :], in1=tmpb[:, :], op=ALU.mult),
                 r=["mta", "mtb"], w=["mta"])

        acc = self.sb("macc", [128, 128], F32)

        def one_mask(Dl):
            band(Dl, 128)
            P.op("dve", lambda e: e.tensor_copy(out=acc[:, :], in_=tmpa[:, :]), r=["mta"], w=["macc"])
            band(Dl, 512)
            lattice(Dl, 4)
            P.op("dve", lambda e: e.tensor_tensor(out=acc[:, :], in0=acc[:, :], in1=tmpa[:, :], op=ALU.add),
                 r=["mta", "macc"], w=["macc"])
            band(Dl, 2048)
            lattice(Dl, 16)
            P.op("dve", lambda e: e.tensor_tensor(out=self.maskC[:, Dl, :], in0=acc[:, :], in1=tmpa[:, :], op=ALU.add),
                 r=["mta", "macc"], w=["maskC"])

        for Dl in range(17):
            one_mask(Dl)
        for v in range(8):
            P.op("dve", lambda e, v=v: e.tensor_scalar(out=small[:, v:v + 1], in0=self.cidf[:, 0:1], scalar1=float(-v), scalar2=128.0,
                                                      op0=ALU.add, op1=ALU.mult), r=["cidf"], w=[f"cshift{v}"])
            P.op("dve", lambda e, v=v: e.tensor_scalar(out=self.maskA[:, v, :], in0=relb[:, :], scalar1=small[:, v:v + 1], scalar2=0.0,
                                                      op0=ALU.add, op1=ALU.is_ge), r=["relb", f"cshift{v}"], w=["maskA"])
        P.op("dve", lambda e: e.tensor_scalar(out=small[:, 8:9], in0=self.cidf[:, 0:1], scalar1=1.0, scalar2=None, op0=ALU.min),
             r=["cidf"], w=["hv"])
        P.op("dve", lambda e: e.tensor_scalar(out=self.vhalo[:, :], in0=self.onesf[:, :], scalar1=small[:, 8:9], scalar2=None,
                                              op0=ALU.mult), r=["onesf", "hv"], w=["vhalo"])
        for l in range(2):
            for i in range(8):
                r0 = R_KCT + i * 128
                P.dma("sp", lambda e, l=l, r0=r0: e.dma_start(out=self.kvall[l][r0:r0 + 128, :].rearrange("p (a c) -> p a c", a=2),
                                                             in_=bc_free(self.zeros[:, :], 1, 2)),
                      r=["zeros"], w=[f"kvpad{l}_{i}"])
        self.k_kvpad = [[f"kvpad{l}_{i}" for i in range(8)] for l in range(2)]

    def load_w(self, dst, key, src, ncols):
        P = self.P
        nk = dst.shape[1]
        for k in range(nk):
            for c0 in range(0, ncols, 1024):
                c1 = min(ncols, c0 + 1024)
                P.dma("pool", lambda e, k=k, c0=c0, c1=c1: e.dma_start(out=dst[:, k, c0:c1],
                                                                     in_=src[k * 128:(k + 1) * 128, c0:c1]),
                      w=[key])

    def layer(self, l):
        self.lam_init = 0.8 - 0.6 * math.exp(-0.3 * l)
        with contextlib.ExitStack() as lst:
            self.QCT = lst.enter_context(self.nc.sbuf_tensor(f"sb_QCT{l}", [128, 4, SOWN], BF16))
            done = self.layer_front(l)
        if done:
            self.phase4(l)

    def layer_front(self, l):
        self.phase1(l)
        if self.stop_after == "p1":
            return False
        self.phaseB(l)
        if self.stop_after == "pB":
            return False
        self.phaseA(l)
        if self.stop_after == "pA":
            return False
        self.phaseC(l)
        if self.stop_after == "pC":
            return False
        return True

    def allgather1(self, l):
        P = self.P
        kv = self.kvall[l]
        P.dma("pool", lambda e: e.collective_compute(
            "AllGather", ALU.bypass, replica_groups=[list(range(NCORES))],
            ins=[self.contrib[l].opt()], outs=[kv[R_TOT:, :].opt()]),
            r=self.k_contrib, w=[f"kvall{l}"], inc=1, semq="cc")

    def allgather2(self, l):
        P = self.P
        P.dma("pool", lambda e: e.collective_compute(
            "AllGather", ALU.bypass, replica_groups=[list(range(NCORES))],
            ins=[self.contrib2[l].opt()], outs=[self.yaall[l].opt()]),
            r=[f"ctb2_{h}_{b}" for h in range(4) for b in range(4)], w=[f"yaall{l}"], inc=1, semq="cc")

    def gelu_from_psum(self, src, dst, tmp1, tmp2, rkeys, wkeys, tag):
        P = self.P
        P.op("act", lambda e: e.activation(out=tmp1, in_=src, func=AF.Square), r=rkeys, w=[tag + "_t1"])
        P.op("dve", lambda e: e.tensor_scalar(out=tmp1, in0=tmp1, scalar1=0.044715, scalar2=1.0, op0=ALU.mult, op1=ALU.add),
             r=[tag + "_t1"], w=[tag + "_t1"])
        P.op("dve", lambda e: e.tensor_tensor(out=tmp2, in0=tmp1, in1=src, op=ALU.mult), r=[tag + "_t1"] + rkeys, w=[tag + "_t2"])
        P.op("act", lambda e: e.activation(out=tmp2, in_=tmp2, func=AF.Sigmoid, scale=1.5957691216057308),
             r=[tag + "_t2"], w=[tag + "_t2"])
        P.op("dve", lambda e: e.tensor_tensor(out=dst, in0=tmp2, in1=src, op=ALU.mult), r=[tag + "_t2"] + rkeys, w=wkeys)

    def phaseB(self, l):
        with contextlib.ExitStack() as ph:
            self.ph = ph
            self._phaseB(l)
            self.P.barrier()

    def _phaseB(self, l):
        P = self.P
        sb = self.psb
        wB = sb("wB", [128, 8, 1536], BF16)
        for gi, c in enumerate((C_BU, C_BV, C_BZ)):
            self.load_w(wB[:, :, gi * 512:(gi + 1) * 512], f"wB_{gi}", self.w_in[l][:, c:c + 512], 512)
        wcf = sb("wcf", [128, 4, 128], F32)
        wcb = sb("wcb", [128, 4, 128], BF16)
        wcT = sb("wcT", [128, 4, 128], BF16)
        sbb = sb("sbb", [128, 4, 128], F32)
        lng = sb("lng", [128, 512], F32)
        lnb = sb("lnb", [128, 512], F32)
        P.dma("sp", lambda e: e.dma_start(out=wcf[:, :, :], in_=self.sgu_w[l].rearrange("g t s -> t g s")), w=["wcf"])
        P.dma("sp", lambda e: e.dma_start(out=sbb[:, :, :].rearrange("p g t -> p (g t)"),
                                          in_=self.sgu_b[l].rearrange("g t -> (g t)").partition_broadcast(128)), w=["sbb"])
        P.dma("sp", lambda e: e.dma_start(out=lng[:, :], in_=self.sgu_ln_g[l].partition_broadcast(128)), w=["lng"])
        P.dma("sp", lambda e: e.dma_start(out=lnb[:, :], in_=self.sgu_ln_b[l].partition_broadcast(128)), w=["lnb"])
        for g in range(4):
            P.op("pool", lambda e, g=g: e.affine_select(out=wcb[:, g, :], in_=wcf[:, g, :], pattern=[[-1, 128]],
                                                        compare_op=ALU.is_ge, fill=0.0, base=0, channel_multiplier=1),
                 r=["wcf"], w=[f"wcb{g}"])
        p0 = self.pbf(0)
        for g in range(4):
            P.op("pe", lambda e, g=g: e.transpose(out=p0[:, g * 128:(g + 1) * 128], in_=wcb[:, g, :], identity=self.ident[:, :]),
                 r=[f"wcb{g}", "ident"], w=["pb0"])
        P.op("dve", lambda e: e.tensor_copy(out=wcT[:, :, :].rearrange("p g t -> p (g t)"), in_=p0[:, 0:512]), r=["pb0"], w=["wcT"])
        self.allgather1(l)
        uz = [sb(f"uz{i}", [128, 4, 512], BF16) for i in range(2)]
        ug = [sb(f"ug{i}", [128, 512], F32) for i in range(2)]
        t1 = [sb(f"bt1{i}", [128, 512], F32) for i in range(2)]
        t2 = [sb(f"bt2{i}", [128, 512], F32) for i in range(2)]
        t1v = [sb(f"bt1v{i}", [128, 512], F32) for i in range(2)]
        t2v = [sb(f"bt2v{i}", [128, 512], F32) for i in range(2)]
        zs = [sb(f"zs{i}", [128, 512], F32) for i in range(2)]
        vg = [sb(f"vg{i}", [128, 512], F32) for i in range(2)]
        vb = [sb(f"vb{i}", [128, 512], BF16) for i in range(2)]
        st6 = [sb(f"st6{i}", [128, 6], F32) for i in range(2)]
        mv = [sb(f"mv{i}", [128, 2], F32) for i in range(2)]
        mt = [sb(f"mt{i}", [128, 4, 128], F32) for i in range(2)]
        cnt = 0
        for q4 in range(4):
            cols = slice(q4 * 512, (q4 + 1) * 512)
            uzb = uz[q4 % 2]
            kz = f"uz_{q4 % 2}"
            for j in range(4):
                pb_u, pb_z = 1 + (j % 2) * 2, 2 + (j % 2) * 2
                i2 = cnt % 2
                cnt += 1
                for k in range(8):
                    P.op("pe", lambda e, k=k, j=j, pb_u=pb_u, cols=cols: e.matmul(out=self.pbank[pb_u][:, :], lhsT=wB[:, k, j * 128:(j + 1) * 128],
                                                                      rhs=self.hT[:, k, cols], start=(k == 0), stop=(k == 7)),
                         r=["wB_0"] + self.k_hT, w=[f"pb{pb_u}"])
                for k in range(8):
                    P.op("pe", lambda e, k=k, j=j, pb_z=pb_z, cols=cols: e.matmul(out=self.pbank[pb_z][:, :],
                                                                      lhsT=wB[:, k, 1024 + j * 128:1024 + (j + 1) * 128],
                                                                      rhs=self.hT[:, k, cols], start=(k == 0), stop=(k == 7)),
                         r=["wB_2"] + self.k_hT, w=[f"pb{pb_z}"])
                self.gelu_from_psum(self.pbank[pb_u][:, :], ug[i2][:, :], t1[i2][:, :], t2[i2][:, :], [f"pb{pb_u}"], [f"ug{i2}"], f"gu{i2}")
                P.op("act", lambda e, pb_z=pb_z, i2=i2: e.activation(out=zs[i2][:, :], in_=self.pbank[pb_z][:, :], func=AF.Silu),
                     r=[f"pb{pb_z}"], w=[f"zs{i2}"])
                P.op("dve", lambda e, j=j, i2=i2, uzb=uzb: e.tensor_tensor(out=uzb[:, j, :], in0=ug[i2][:, :], in1=zs[i2][:, :], op=ALU.mult),
                     r=[f"ug{i2}", f"zs{i2}"], w=[kz + f"_{j}"])
            if self.dbg and "B_uz" in self.dbg and q4 == 0:
                P.dma("sp", lambda e, uzb=uzb: e.dma_start(out=self.dbg_out["B_uz"][:, :, :], in_=uzb[:, :, :]),
                      r=[kz + f"_{j}" for j in range(4)], w=["dbg_uz"])
            for tt in range(4):
                t = q4 * 4 + tt
                i2 = t % 2
                pbv, pbm = 5 + i2, 7
                for k in range(8):
                    P.op("pe", lambda e, k=k, t=t, pbv=pbv: e.matmul(out=self.pbank[pbv][:, :], lhsT=self.hT[:, k, t * 128:(t + 1) * 128],
                                                                    rhs=wB[:, k, 512:1024], start=(k == 0), stop=(k == 7)),
                         r=["wB_1"] + self.k_hT, w=[f"pb{pbv}"])
                self.gelu_from_psum(self.pbank[pbv][:, :], vg[i2][:, :], t1v[i2][:, :], t2v[i2][:, :], [f"pb{pbv}"], [f"vg{i2}"], f"gv{i2}")
                P.op("dve", lambda e, i2=i2: e.bn_stats(out=st6[i2][:, :], in_=vg[i2][:, :]), r=[f"vg{i2}"], w=[f"st6{i2}"])
                P.op("dve", lambda e, i2=i2: e.bn_aggr(out=mv[i2][:, :], in_=st6[i2][:, :]), r=[f"st6{i2}"], w=[f"mv{i2}"])
                P.op("dve", lambda e, i2=i2: e.tensor_scalar(out=mv[i2][:, 1:2], in0=mv[i2][:, 1:2], scalar1=EPS, scalar2=None, op0=ALU.add),
                     r=[f"mv{i2}"], w=[f"mv{i2}"])
                P.op("act", lambda e, i2=i2: e.activation(out=mv[i2][:, 1:2], in_=mv[i2][:, 1:2], func=AF.Sqrt), r=[f"mv{i2}"], w=[f"mv{i2}"])
                P.op("dve", lambda e, i2=i2: e.reciprocal(out=mv[i2][:, 1:2], in_=mv[i2][:, 1:2]), r=[f"mv{i2}"], w=[f"mv{i2}"])
                P.op("dve", lambda e, i2=i2: e.tensor_scalar(out=vg[i2][:, :], in0=vg[i2][:, :], scalar1=mv[i2][:, 0:1], scalar2=mv[i2][:, 1:2],
                                                            op0=ALU.subtract, op1=ALU.mult), r=[f"vg{i2}", f"mv{i2}"], w=[f"vg{i2}"])
                P.op("dve", lambda e, i2=i2: e.tensor_tensor(out=vg[i2][:, :], in0=vg[i2][:, :], in1=lng[:, :], op=ALU.mult),
                     r=[f"vg{i2}", "lng"], w=[f"vg{i2}"])
                P.op("dve", lambda e, i2=i2: e.tensor_tensor(out=vb[i2][:, :], in0=vg[i2][:, :], in1=lnb[:, :], op=ALU.add),
                     r=[f"vg{i2}", "lnb"], w=[f"vb{i2}"])
                for g in range(4):
                    P.op("pe", lambda e, g=g, i2=i2: e.matmul(out=self.pbank[pbm][:, g * 128:(g + 1) * 128], lhsT=vb[i2][:, g * 128:(g + 1) * 128],
                                                             rhs=wcT[:, g, :], start=True, stop=True),
                         r=[f"vb{i2}", "wcT"], w=[f"pb{pbm}"])
                P.op("dve", lambda e, i2=i2: e.tensor_tensor(out=mt[i2][:, :, :], in0=self.pbank[pbm][:, :].rearrange("p (g t) -> p g t", g=4),
                                                            in1=sbb[:, :, :], op=ALU.add), r=[f"pb{pbm}", "sbb"], w=[f"mt{i2}"])
                if self.dbg and "B_mt" in self.dbg and t == 0:
                    P.dma("sp", lambda e, i2=i2: e.dma_start(out=self.dbg_out["B_mt"][:, :, :], in_=mt[i2][:, :, :]), r=[f"mt{i2}"], w=["dbg_mt"])
                    P.dma("sp", lambda e, i2=i2: e.dma_start(out=self.dbg_out["B_vb"][:, :], in_=vb[i2][:, :]), r=[f"vb{i2}"], w=["dbg_vb"])
                    P.dma("sp", lambda e: e.dma_start(out=self.dbg_out["B_wcT"][:, :, :], in_=wcT[:, :, :]), r=["wcT"], w=["dbg_wcT"])
                P.op("dve", lambda e, i2=i2, t=t, tt=tt, uzb=uzb: e.tensor_tensor(out=self.ybT[:, :, t * 128:(t + 1) * 128], in0=mt[i2][:, :, :],
                                                                                  in1=uzb[:, :, tt * 128:(tt + 1) * 128], op=ALU.mult),
                     r=[f"mt{i2}"] + [kz + f"_{j}" for j in range(4)], w=[f"ybT{t}"])
        if self.dbg and "ybT" in self.dbg:
            P.dma("sp", lambda e: e.dma_start(out=self.dbg_out["ybT"][:, :, :], in_=self.ybT[:, :, :]),
                  r=[f"ybT{t}" for t in range(NT)], w=["dbgyb"])

    def phaseA(self, l):
        with contextlib.ExitStack() as ph:
            self.ph = ph
            self._phaseA(l)
            self.P.barrier()

    def _phaseA(self, l):
        P = self.P
        sb = self.psb
        kv = self.kvall[l]
        kvk = [f"kvall{l}"]
        KT = sb("KT", [128, 8, 2048], BF16)
        VV = sb("VV", [128, 8, 2048], BF16)
        Q1 = [sb(f"Q1p{i}", [128, 2048], BF16) for i in range(1)] * 2
        Q2 = [sb(f"Q2p{i}", [128, 2048], BF16) for i in range(1)] * 2
        E = [sb(f"E{i}", [128, 2, 512], BF16) for i in range(4)]
        fin = [sb(f"fin{i}", [128, 512], F32) for i in range(4)]
        fin = [fin[0], fin[1], fin[0], fin[2], fin[3]]
        sqb = sb("sqb", [128, 512], BF16)
        yst = [sb(f"yst{i}", [128, 512], BF16) for i in range(2)]
        lamv = sb("lamv", [128, 4, 64], F32)
        lamt = sb("lamt", [128, 2, 64], F32)
        lams = sb("lams", [128, 4], F32)
        gcol = sb("gcol", [128, 1], F32)
        for i in range(4):
            P.dma("sp", lambda e, i=i: e.dma_start(out=lamv[:, i, :], in_=self.lam4[i][l].partition_broadcast(128)), w=[f"lamv{i}"])
        P.dma("sp", lambda e: e.dma_start(out=gcol[:, :], in_=self.subln_g[l].rearrange("(e o) -> e o", o=1)), w=["gcol"])
        for m in range(2):
            P.op("dve", lambda e, m=m: e.tensor_tensor(out=lamt[:, m, :], in0=lamv[:, 2 * m, :], in1=lamv[:, 2 * m + 1, :], op=ALU.mult),
                 r=[f"lamv{2 * m}", f"lamv{2 * m + 1}"], w=[f"lamt{m}"])
            P.op("dve", lambda e, m=m: e.reduce_sum(out=lams[:, m:m + 1], in_=lamt[:, m, :], axis=AX.X), r=[f"lamt{m}"], w=[f"lams{m}"])
            P.op("act", lambda e, m=m: e.activation(out=lams[:, m:m + 1], in_=lams[:, m:m + 1], func=AF.Exp), r=[f"lams{m}"], w=[f"lams{m}"])
        P.op("dve", lambda e: e.tensor_tensor(out=lams[:, 2:3], in0=lams[:, 1:2], in1=lams[:, 0:1], op=ALU.subtract),
             r=["lams0", "lams1"], w=["neglam"])
        li = self.lam_init
        P.op("dve", lambda e: e.tensor_scalar(out=lams[:, 2:3], in0=lams[:, 2:3], scalar1=-li, scalar2=None, op0=ALU.add),
             r=["neglam"], w=["neglam"])
        P.op("dve", lambda e: e.tensor_scalar(out=gcol[:, :], in0=gcol[:, :], scalar1=1.0 - li, scalar2=None, op0=ALU.mult),
             r=["gcol"], w=["gcol"])
        neglam = lams[:, 2:3]
        for i in range(1):
            P.op("pool", lambda e, i=i: e.memset(Q1[i][64:128, :], 0.0), w=[f"Q1z{i}"])
            P.op("pool", lambda e, i=i: e.memset(Q2[i][0:64, :], 0.0), w=[f"Q2z{i}"])
        kv4 = kv.rearrange("(o r) (a c) -> o r a c", o=NCORES + 1, a=2)
        for a in range(2):
            def qgather(e, a=a):
                src = kv4[1:NCORES + 1, R_QAT:R_QAT + 512, a, bass.ds(P.regs["p128"], 128)]
                dst = self.qown[l].rearrange("r (o a c) -> o r a c", o=8, a=2)[:, :, a, :]
                return e.dma_start(out=dst, in_=src)
            P.dma("sp", qgather, r=kvk, w=[f"qown_{a}"])
        S1b, S2b, O1b, O2b, R1b, R2b = (0, 1, 6), (2, 3, 7), 4, 5, 6, 7
        racc = [self.psb(f"racc{i}", [128, 2, 512], F32) for i in range(2)]
        tiles = []
        for h in range(4):
            for b in range(4):
                ntile = 32 * b + 32
                for kt in range(ntile):
                    tiles.append(dict(h=h, b=b, kt=kt, ntile=ntile))

        def head_loads(h):
            hb = 0
            for o in range(8):
                r0 = (o + 1) * R_TOT + R_KAT + h * 128
                P.dma("sp", lambda e, o=o, r0=r0: e.dma_start(out=KT[:, o, :], in_=kv[r0:r0 + 128, :]), r=kvk, w=[f"KT{o}"])
                r1 = (o + 1) * R_TOT + R_VA + h * 128
                P.dma("sp", lambda e, o=o, r1=r1: e.dma_start(out=VV[:, o, :], in_=kv[r1:r1 + 128, :]), r=kvk, w=[f"VV{o}"])
            P.dma("sp", lambda e, hb=hb, h=h: e.dma_start(out=Q1[hb][0:64, :], in_=self.qown[l][h * 128:h * 128 + 64, :]),
                  r=["qown_0", "qown_1"], w=[f"Q1d{hb}"])
            P.dma("sp", lambda e, hb=hb, h=h: e.dma_start(out=Q2[hb][64:128, :], in_=self.qown[l][h * 128 + 64:h * 128 + 128, :]),
                  r=["qown_0", "qown_1"], w=[f"Q2d{hb}"])

        def geom(T):
            b, kt = T["b"], T["kt"]
            u = kt - 32 * b
            n0 = 0 if u < 0 else 128 * (u // 8)
            return kt // 16, kt % 16, u, n0

        def stage1(i, T):
            h, b = T["h"], T["b"]
            hb = 0
            if T["kt"] == 0 and b == 0:
                head_loads(h)
            o, t, u, n0 = geom(T)
            sl = slice(n0, 512)
            qs = slice(b * 512 + n0, b * 512 + 512)
            s1, s2 = S1b[i % 3], S2b[i % 3]
            Eb = E[i % 4]
            ek = f"E{i % 4}"
            ksl = slice(t * 128, (t + 1) * 128)
            qk1 = [f"Q1d{hb}", f"Q1z{hb}"]
            qk2 = [f"Q2d{hb}", f"Q2z{hb}"]
            P.op("pe", lambda e: e.matmul(out=self.pbank[s1][:, sl], lhsT=KT[:, o, ksl], rhs=Q1[hb][:, qs], start=True, stop=True),
                 r=[f"KT{o}"] + qk1, w=[f"pb{s1}"])
            P.op("pe", lambda e: e.matmul(out=self.pbank[s2][:, sl], lhsT=KT[:, o, ksl], rhs=Q2[hb][:, qs], start=True, stop=True),
                 r=[f"KT{o}"] + qk2, w=[f"pb{s2}"])
            P.op("act", lambda e: e.activation(out=Eb[:, 0, sl], in_=self.pbank[s1][:, sl], func=AF.Exp, scale=0.125),
                 r=[f"pb{s1}"], w=[ek + "a"])
            P.op("act", lambda e: e.activation(out=Eb[:, 1, sl], in_=self.pbank[s2][:, sl], func=AF.Exp, scale=0.125),
                 r=[f"pb{s2}"], w=[ek + "b"])
            if u >= 0:
                v = u % 8
                P.op("pool", lambda e: e.tensor_tensor(out=Eb[:, :, n0:n0 + 128], in0=Eb[:, :, n0:n0 + 128],
                                                       in1=bc_free(self.maskA[:, v, :], 1, 2), op=ALU.mult),
                     r=[ek + "a", ek + "b", "maskA"], w=[ek + "a", ek + "b"])

        def stage2(i, T):
            h, b, kt, ntile = T["h"], T["b"], T["kt"], T["ntile"]
            o, t, u, n0 = geom(T)
            sl = slice(n0, 512)
            Eb = E[i % 4]
            ek = f"E{i % 4}"
            ksl = slice(t * 128, (t + 1) * 128)
            first, last = (kt == 0), (kt == ntile - 1)
            P.op("pe", lambda e: e.matmul(out=self.pbank[O1b][:, sl], lhsT=VV[:, o, ksl], rhs=Eb[:, 0, sl], start=first, stop=last),
                 r=[f"VV{o}", ek + "a"], w=[f"pb{O1b}"])
            P.op("pe", lambda e: e.matmul(out=self.pbank[O2b][:, sl], lhsT=VV[:, o, ksl], rhs=Eb[:, 1, sl], start=first, stop=last),
                 r=[f"VV{o}", ek + "b"], w=[f"pb{O2b}"])
            ab = (h * 4 + b) % 2
            ac = racc[ab]
            if first:
                P.op("dve", lambda e: e.memset(ac[:, 0, :], 0.0), w=[f"racc{ab}a"])
                P.op("pool", lambda e: e.memset(ac[:, 1, :], 0.0), w=[f"racc{ab}b"])
            P.op("dve", lambda e: e.tensor_tensor(out=ac[:, 0, sl], in0=ac[:, 0, sl], in1=Eb[:, 0, sl], op=ALU.add),
                 r=[ek + "a", f"racc{ab}a"], w=[f"racc{ab}a"])
            P.op("pool", lambda e: e.tensor_tensor(out=ac[:, 1, sl], in0=ac[:, 1, sl], in1=Eb[:, 1, sl], op=ALU.add),
                 r=[ek + "b", f"racc{ab}b"], w=[f"racc{ab}b"])
            if last:
                finalize(h, b)

        fsum = self.psb("fsum", [128, 2, 512], F32)

        def finalize(h, b):
            f0, f1, f2, f3, f4 = [x[:, :] for x in fin]
            ab = (h * 4 + b) % 2
            ac = racc[ab]
            P.op("dve", lambda e: e.tensor_copy(out=fsum[:, 0, :], in_=self.pbank[O1b][:, :]), r=[f"pb{O1b}"], w=["fs0"])
            P.op("dve", lambda e: e.tensor_copy(out=fsum[:, 1, :], in_=self.pbank[O2b][:, :]), r=[f"pb{O2b}"], w=["fs2"])
            P.op("pe", lambda e: e.matmul(out=self.pbank[R1b][:, :], lhsT=self.onesf[:, :], rhs=ac[:, 0, :], start=True, stop=True),
                 r=["onesf", f"racc{ab}a"], w=[f"pb{R1b}"])
            P.op("pe", lambda e: e.matmul(out=self.pbank[R2b][:, :], lhsT=self.onesf[:, :], rhs=ac[:, 1, :], start=True, stop=True),
                 r=["onesf", f"racc{ab}b"], w=[f"pb{R2b}"])
            P.op("dve", lambda e: e.reciprocal(out=f0, in_=self.pbank[R1b][:, :]), r=[f"pb{R1b}"], w=["f0"])
            P.op("dve", lambda e: e.tensor_tensor(out=f1, in0=fsum[:, 0, :], in1=f0, op=ALU.mult), r=["fs0", "f0"], w=["f1"])
            P.op("dve", lambda e: e.reciprocal(out=f0, in_=self.pbank[R2b][:, :]), r=[f"pb{R2b}", "f0"], w=["f0"])
            P.op("dve", lambda e: e.tensor_tensor(out=f3, in0=fsum[:, 1, :], in1=f0, op=ALU.mult), r=["fs2", "f0"], w=["f3"])
            P.op("dve", lambda e: e.scalar_tensor_tensor(out=f4, in0=f3, scalar=neglam, in1=f1, op0=ALU.mult, op1=ALU.add),
                 r=["f3", "f1", "neglam"], w=["f4"])
            P.op("pool", lambda e: e.tensor_tensor(out=sqb[:, :], in0=f4, in1=f4, op=ALU.mult), r=["f4"], w=["sqb"])
            P.op("pe", lambda e: e.matmul(out=self.pbank[R1b][:, :], lhsT=self.ones[:, :], rhs=sqb[:, :], start=True, stop=True),
                 r=["ones", "sqb"], w=[f"pb{R1b}"])
            P.op("dve", lambda e: e.tensor_scalar(out=f0, in0=self.pbank[R1b][:, :], scalar1=1.0 / 128.0, scalar2=EPS,
                                                  op0=ALU.mult, op1=ALU.add), r=[f"pb{R1b}"], w=["f0"])
            P.op("act", lambda e: e.activation(out=f0, in_=f0, func=AF.Sqrt), r=["f0"], w=["f0"])
            P.op("dve", lambda e: e.reciprocal(out=f0, in_=f0), r=["f0"], w=["f0"])
            P.op("dve", lambda e: e.tensor_tensor(out=f1, in0=f4, in1=f0, op=ALU.mult), r=["f4", "f0"], w=["f1"])
            ys = yst[(h * 4 + b) % 2]
            ysk = f"yst{(h * 4 + b) % 2}"
            P.op("dve", lambda e: e.tensor_scalar(out=ys[:, :], in0=f1, scalar1=gcol[:, 0:1], scalar2=None, op0=ALU.mult),
                 r=["f1", "gcol"], w=[ysk])
            P.dma("sp", lambda e: e.dma_start(out=self.contrib2[l][h * 128:(h + 1) * 128, b * 512:(b + 1) * 512], in_=ys[:, :]),
                  r=[ysk], w=[f"ctb2_{h}_{b}"])

        n = len(tiles)
        SK = 2
        done2 = 0
        for step in range(n + SK):
            newhead = step < n and tiles[step]["kt"] == 0 and tiles[step]["b"] == 0
            if newhead:
                while done2 < step:
                    stage2(done2, tiles[done2])
                    done2 += 1
            if step < n:
                stage1(step, tiles[step])
            if step >= SK and done2 <= step - SK:
                stage2(done2, tiles[done2])
                done2 += 1
        while done2 < n:
            stage2(done2, tiles[done2])
            done2 += 1
        if self.dbg and "ya2" in self.dbg:
            P.dma("sp", lambda e: e.dma_start(out=self.dbg_out["ya2"][:, :], in_=self.contrib2[l][:, :]),
                  r=[f"ctb2_{h}_{b}" for h in range(4) for b in range(4)], w=["dbgya"])

    def phaseC(self, l):
        with contextlib.ExitStack() as ph:
            self.ph = ph
            self._phaseC(l)
            self.P.barrier()

    def _phaseC(self, l):
        P = self.P
        sb = self.psb
        kv = self.kvall[l]
        kvk = [f"kvall{l}"]
        wcz = sb("wcz", [128, 8, 512], BF16)
        self.load_w(wcz[:, :, :], "wcz", self.w_in[l][:, C_CZ:C_CZ + 512], 512)
        self.allgather2(l)
        KW = [sb(f"KW{i}", [128, 2, 2048], BF16) for i in range(2)]
        VW = [sb(f"VW{i}", [128, 2, 16, 128], BF16) for i in range(2)]
        QA = [sb(f"QCa{i}", [128, 2048], BF16) for i in range(2)]
        QB = [sb(f"QCb{i}", [128, 2048], BF16) for i in range(2)]
        E = [sb(f"EC{i}", [128, 512], BF16) for i in range(4)]
        fin = [sb(f"finC{i}", [128, 512], F32) for i in range(3)]
        czs = sb("czs", [128, 512], F32)
        for i in range(2):
            P.op("dve", lambda e, i=i: e.memset(QA[i][64:128, :], 0.0), w=[f"QAz{i}"])
            P.op("dve", lambda e, i=i: e.memset(QB[i][0:64, :], 0.0), w=[f"QBz{i}"])
        def cgather(e):
            kv3 = kv.rearrange("(o r) c -> o r c", o=NCORES + 1)[:, R_KCT:R_TOT, :]
            return e.dma_start(out=self.cown[l][:, :, :], in_=kv3[bass.ds(P.regs["p1"], 2), :, :])
        P.dma("sp", cgather, r=kvk, w=["cown"])
        co = self.cown[l]
        it = 0
        for pr in range(4):
            pb = pr % 2
            for hf in range(2):
                P.dma("sp", lambda e, pb=pb, hf=hf, pr=pr: e.dma_start(out=KW[pb][:, hf, :], in_=co[hf, pr * 128:(pr + 1) * 128, :]),
                      r=["cown"], w=[f"KW{pb}_{hf}"])

                def vload(e, pb=pb, hf=hf, pr=pr):
                    src = co[hf, 512:1024, :].rearrange("(k a) c -> k (a c)", k=128)
                    src = src.rearrange("k (t c) -> k t c", t=16)[:, :, pr * 128:(pr + 1) * 128]
                    return e.dma_start(out=VW[pb][:, hf, :, :], in_=src)
                P.dma("sp", vload, r=["cown"], w=[f"VW{pb}_{hf}"])
            P.op("dve", lambda e, pb=pb, pr=pr, QCT=self.QCT: e.tensor_copy(out=QA[pb][0:64, :], in_=QCT[0:64, pr, :]),
                 r=[f"QCT{t}" for t in range(NT)], w=[f"QAd{pb}"])
            P.op("dve", lambda e, pb=pb, pr=pr, QCT=self.QCT: e.tensor_copy(out=QB[pb][64:128, :], in_=QCT[64:128, pr, :]),
                 r=[f"QCT{t}" for t in range(NT)], w=[f"QBd{pb}"])
            kq = [[f"QAd{pb}", f"QAz{pb}"], [f"QBd{pb}", f"QBz{pb}"]]
            Qp = [QA[pb], QB[pb]]
            for qg in range(4):
                kts = list(range(4 * qg, 4 * qg + 20))
                units = []
                for idx, kt in enumerate(kts):
                    hf, t = kt // 16, kt % 16
                    j0 = max(0, kt - (16 + 4 * qg))
                    j1 = min(3, kt - 4 * qg)
                    D0 = 16 + 4 * qg + j0 - kt
                    nj = j1 - j0 + 1
                    for hh in range(2):
                        units.append(dict(hf=hf, t=t, D0=D0, nj=nj, hh=hh, first=(idx == 0), last=(idx == len(kts) - 1),
                                          sl=slice(j0 * 128, (j1 + 1) * 128),
                                          qs=slice(qg * 512 + j0 * 128, qg * 512 + (j1 + 1) * 128)))

                def cst1(U, itn, pb=pb, Qp=Qp, kq=kq):
                    sbk = itn % 4
                    Eb = E[itn % 4]
                    ek = f"EC{itn % 4}"
                    hf, t, sl, qs, hh, D0, nj = U["hf"], U["t"], U["sl"], U["qs"], U["hh"], U["D0"], U["nj"]
                    P.op("pe", lambda e: e.matmul(out=self.pbank[sbk][:, sl], lhsT=KW[pb][:, hf, t * 128:(t + 1) * 128], rhs=Qp[hh][:, qs],
                                                  start=True, stop=True), r=[f"KW{pb}_{hf}"] + kq[hh], w=[f"pb{sbk}"])
                    P.op("act", lambda e: e.activation(out=Eb[:, sl], in_=self.pbank[sbk][:, sl], func=AF.Exp, scale=0.125),
                         r=[f"pb{sbk}"], w=[ek])
                    P.op("dve", lambda e: e.tensor_tensor(out=Eb[:, sl].rearrange("p (j q) -> p j q", j=nj),
                                                          in0=Eb[:, sl].rearrange("p (j q) -> p j q", j=nj),
                                                          in1=self.maskC[:, D0:D0 + nj, :], op=ALU.mult), r=[ek, "maskC"], w=[ek])

                def cst2(U, itn, pb=pb):
                    Eb = E[itn % 4]
                    ek = f"EC{itn % 4}"
                    hf, t, sl, hh, first, last = U["hf"], U["t"], U["sl"], U["hh"], U["first"], U["last"]
                    ob, rb = 4 + 2 * hh, 5 + 2 * hh
                    P.op("pe", lambda e: e.matmul(out=self.pbank[ob][:, sl], lhsT=VW[pb][:, hf, t, :], rhs=Eb[:, sl], start=first, stop=last,
                                                  skip_group_check=True), r=[f"VW{pb}_{hf}", ek], w=[f"pb{ob}"])
                    lh = self.vhalo if hf == 0 else self.ones
                    P.op("pe", lambda e: e.matmul(out=self.pbank[rb][:, sl], lhsT=lh[:, :], rhs=Eb[:, sl], start=first, stop=last,
                                                  skip_group_check=True), r=["vhalo", "ones", ek], w=[f"pb{rb}"])
                nu = len(units)
                SK = 2
                for step in range(nu + SK):
                    if step < nu:
                        cst1(units[step], it + step)
                    if step >= SK:
                        cst2(units[step - SK], it + step - SK)
                it += nu
                f0, f1, f2 = [x[:, :] for x in fin]
                P.op("dve", lambda e: e.reciprocal(out=fin[0][0:64, :], in_=self.pbank[5][0:64, :]), r=["pb5"], w=["fc0a"])
                P.op("dve", lambda e: e.reciprocal(out=fin[0][64:128, :], in_=self.pbank[7][64:128, :]), r=["pb7"], w=["fc0b"])
                P.op("dve", lambda e: e.tensor_tensor(out=fin[1][0:64, :], in0=self.pbank[4][0:64, :], in1=fin[0][0:64, :], op=ALU.mult),
                     r=["pb4", "fc0a"], w=["fc1a"])
                P.op("dve", lambda e: e.tensor_tensor(out=fin[1][64:128, :], in0=self.pbank[6][64:128, :], in1=fin[0][64:128, :], op=ALU.mult),
                     r=["pb6", "fc0b"], w=["fc1b"])
                sbk = it % 4
                it += 1
                gs = slice(qg * 512, (qg + 1) * 512)
                for k in range(8):
                    P.op("pe", lambda e, k=k, pr=pr, gs=gs, sbk=sbk: e.matmul(out=self.pbank[sbk][:, :], lhsT=wcz[:, k, pr * 128:(pr + 1) * 128],
                                                                           rhs=self.hT[:, k, gs], start=(k == 0), stop=(k == 7)),
                         r=["wcz"] + self.k_hT, w=[f"pb{sbk}"])
                P.op("act", lambda e, sbk=sbk: e.activation(out=czs[:, :], in_=self.pbank[sbk][:, :], func=AF.Silu), r=[f"pb{sbk}"], w=["czs"])
                P.op("dve", lambda e, pr=pr, gs=gs: e.tensor_tensor(out=self.ycT[:, pr, gs], in0=f1, in1=czs[:, :], op=ALU.mult),
                     r=["fc1a", "fc1b", "czs"], w=[f"ycT{pr}_{qg}"])
        if self.dbg and "ycT" in self.dbg:
            P.dma("sp", lambda e: e.dma_start(out=self.dbg_out["ycT"][:, :, :], in_=self.ycT[:, :, :]),
                  r=[f"ycT{pr}_{qg}" for pr in range(4) for qg in range(4)], w=["dbgyc"])

    def phase4(self, l):
        with contextlib.ExitStack() as ph:
            self.ph = ph
            self._phase4(l)
            self.P.barrier()

    def _phase4(self, l):
        P = self.P
        sb = self.psb
        last = (l == 1) and self.final_norm
        x_src = self.x_in if l == 0 else self.x1
        wG = sb("wG", [128, 8, 3072], BF16)
        wAZ = sb("wAZ", [128, 8, 512], BF16)
        wBR = sb("wBR", [128, 12, 1024], BF16)
        wO = sb("wO", [128, 8, 1024], BF16)
        self.load_w(wAZ[:, :, :], "wAZ", self.w_in[l][:, C_AZ:C_AZ + 512], 512)
        for n in range(3):
            self.load_w(wBR[:, 4 * n:4 * n + 4, :], f"wBR{n}", self.w_branch[l][n], 1024)
        for n in range(3):
            self.load_w(wG[:, :, n * 1024:(n + 1) * 1024], f"wG{n}", self.w_in[l][:, C_GL + n * 1024:C_GL + (n + 1) * 1024], 1024)
        self.load_w(wO[:, :, :], "wO", self.w_out[l], 1024)
        yaraw = [sb(f"yaraw{i}", [128, 4, 512], BF16) for i in range(1)] * 2
        yag = yaraw
        azs = [sb(f"azs{i}", [128, 512], BF16) for i in range(2)]
        sg = [sb(f"sg{i}", [128, 512], F32) for i in range(3)]
        mm = sg
        mT = [sb(f"mT{i}", [128, 8, 512], BF16) for i in range(1)] * 2
        xr_ = [sb(f"xres{i}", [128, D], F32) for i in range(1)] * 2
        xo = xr_
        if last:
            fg = sb("fg", [128, D], F32)
            P.dma("sp", lambda e: e.dma_start(out=fg[:, :], in_=self.final_g.partition_broadcast(128)), w=["fg"])
            junk = self.zeros
            ss = [sb(f"ss4{i}", [128, 1], F32) for i in range(2)]
        ya = self.yaall[l]
        for tb in range(2):
            def ygather(e, tb=tb):
                src = ya.rearrange("(w r) c -> w r c", w=8)[:, :, tb * 128:tb * 128 + 7 * 256 + 128][:, :, bass.ds(P.regs["p256"], 128)]
                dst = self.yaown[l].rearrange("r (tb w c) -> w r tb c", tb=2, w=8)[:, :, tb, :]
                return e.dma_start(out=dst, in_=src)
            P.dma("sp", ygather, r=[f"yaall{l}"], w=[f"yaown_{tb}"])
        cnt = 0
        for q4 in range(4):
            gb = q4 % 2
            gs = slice(q4 * 512, (q4 + 1) * 512)
            for tt in range(4):
                t = q4 * 4 + tt
                P.dma("sp", lambda e, gb=gb, tt=tt, t=t: e.dma_start(
                    out=yaraw[gb][:, :, tt * 128:(tt + 1) * 128],
                    in_=self.yaown[l][:, t * 128:(t + 1) * 128].rearrange("(h p) c -> p h c", h=4)),
                    r=["yaown_0", "yaown_1"], w=[f"yab_{h}" for h in range(4)])
            kyr = []
            for h in range(4):
                i2 = cnt % 2
                cnt += 1
                bk = i2
                for k in range(8):
                    P.op("pe", lambda e, k=k, h=h, bk=bk, gs=gs: e.matmul(out=self.pbank[bk][:, :], lhsT=wAZ[:, k, h * 128:(h + 1) * 128],
                                                                         rhs=self.hT[:, k, gs], start=(k == 0), stop=(k == 7)),
                         r=["wAZ"] + self.k_hT, w=[f"pb{bk}"])
                P.op("act", lambda e, bk=bk, i2=i2: e.activation(out=azs[i2][:, :], in_=self.pbank[bk][:, :], func=AF.Silu),
                     r=[f"pb{bk}"], w=[f"azs{i2}"])
                P.op("pool", lambda e, h=h, i2=i2, gb=gb: e.tensor_tensor(out=yag[gb][:, h, :], in0=yaraw[gb][:, h, :], in1=azs[i2][:, :], op=ALU.mult),
                     r=kyr + [f"azs{i2}", f"yab_{h}"], w=[f"yab_{h}"])
            kyg = [f"yab_{h}" for h in range(4)]
            ysrc = [(lambda cc, gb=gb: yag[gb][:, cc, :], kyg),
                    (lambda cc, gs=gs: self.ybT[:, cc, gs], [f"ybT{t}" for t in range(NT)]),
                    (lambda cc, gs=gs: self.ycT[:, cc, gs], [f"ycT{pr}_{qg}" for pr in range(4) for qg in range(4)])]
            for dc in range(8):
                for n in range(3):
                    getter, keys = ysrc[n]
                    for cc in range(4):
                        P.op("pe", lambda e, n=n, cc=cc, dc=dc, getter=getter: e.matmul(
                            out=self.pbank[2 + n][:, :], lhsT=wBR[:, 4 * n + cc, dc * 128:(dc + 1) * 128], rhs=getter(cc),
                            start=(cc == 0), stop=(cc == 3)), r=[f"wBR{n}"] + keys, w=[f"pb{2 + n}"])
                for n in range(3):
                    for k in range(8):
                        P.op("pe", lambda e, n=n, k=k, dc=dc, gs=gs: e.matmul(
                            out=self.pbank[5 + n][:, :], lhsT=wG[:, k, n * 1024 + dc * 128:n * 1024 + (dc + 1) * 128],
                            rhs=self.hT[:, k, gs], start=(k == 0), stop=(k == 7)), r=[f"wG{n}"] + self.k_hT, w=[f"pb{5 + n}"])
                for n in range(3):
                    P.op("act", lambda e, n=n: e.activation(out=sg[n][:, :], in_=self.pbank[5 + n][:, :], func=AF.Sigmoid),
                         r=[f"pb{5 + n}"], w=[f"sg{n}"])
                    P.op("dve", lambda e, n=n: e.tensor_tensor(out=mm[n][:, :], in0=self.pbank[2 + n][:, :], in1=sg[n][:, :], op=ALU.mult),
                         r=[f"pb{2 + n}", f"sg{n}"], w=[f"sg{n}"])
                P.op("pool", lambda e: e.tensor_tensor(out=mm[0][:, :], in0=mm[0][:, :], in1=mm[1][:, :], op=ALU.add), r=["sg0", "sg1"], w=["sg0"])
                P.op("pool", lambda e, gb=gb, dc=dc: e.tensor_tensor(out=mT[gb][:, dc, :], in0=mm[0][:, :], in1=mm[2][:, :], op=ALU.add),
                     r=["sg0", "sg2"], w=[f"mT_{dc}"])
            kmT = [f"mT_{dc}" for dc in range(8)]
            for tt in range(4):
                t = q4 * 4 + tt
                xb = 0
                P.dma("sp", lambda e, t=t, xb=xb: e.dma_start(out=xr_[xb][:, :], in_=x_src[t * 128:(t + 1) * 128, :]),
                      r=(["x1w"] if l == 1 else []), w=[f"xres{xb}"])
                for eg in range(2):
                    bk = eg
                    for dc in range(8):
                        P.op("pe", lambda e, dc=dc, eg=eg, bk=bk, gb=gb, tt=tt: e.matmul(
                            out=self.pbank[bk][:, :], lhsT=mT[gb][:, dc, tt * 128:(tt + 1) * 128], rhs=wO[:, dc, eg * 512:(eg + 1) * 512],
                            start=(dc == 0), stop=(dc == 7)), r=["wO"] + kmT, w=[f"pb{bk}"])
                    P.op("dve", lambda e, eg=eg, bk=bk, xb=xb: e.tensor_tensor(out=xo[xb][:, eg * 512:(eg + 1) * 512], in0=self.pbank[bk][:, :],
                                                                              in1=xr_[xb][:, eg * 512:(eg + 1) * 512], op=ALU.add),
                         r=[f"pb{bk}", f"xres{xb}"], w=[f"xres{xb}"])
                kxo = [f"xres{xb}"]
                if not last:
                    dst = self.x1 if (l == 0 and len(self.layers) > 1) else self.out
                    P.dma("sp", lambda e, t=t, xb=xb, dst=dst: e.dma_start(out=dst[t * 128:(t + 1) * 128, :], in_=xo[xb][:, :]),
                          r=kxo, w=["x1w"])
                else:
                    P.op("act", lambda e, xb=xb: e.activation(out=junk[:, :], in_=xo[xb][:, :], func=AF.Square, accum_out=ss[xb][:, :]),
                         r=kxo, w=["zeros", f"ss4{xb}"])
                    P.op("dve", lambda e, xb=xb: e.tensor_scalar(out=ss[xb][:, :], in0=ss[xb][:, :], scalar1=1.0 / D, scalar2=EPS,
                                                                op0=ALU.mult, op1=ALU.add), r=[f"ss4{xb}"], w=[f"ss4{xb}"])
                    P.op("act", lambda e, xb=xb: e.activation(out=ss[xb][:, :], in_=ss[xb][:, :], func=AF.Sqrt), r=[f"ss4{xb}"], w=[f"ss4{xb}"])
                    P.op("dve", lambda e, xb=xb: e.reciprocal(out=ss[xb][:, :], in_=ss[xb][:, :]), r=[f"ss4{xb}"], w=[f"ss4{xb}"])
                    P.op("dve", lambda e, xb=xb: e.scalar_tensor_tensor(out=xo[xb][:, :], in0=xo[xb][:, :], scalar=ss[xb][:, 0:1], in1=fg[:, :],
                                                                       op0=ALU.mult, op1=ALU.mult), r=kxo + [f"ss4{xb}", "fg"], w=kxo)
                    P.dma("sp", lambda e, t=t, xb=xb: e.dma_start(out=self.out[t * 128:(t + 1) * 128, :], in_=xo[xb][:, :]),
                          r=kxo, w=[f"outw{t}"])

    def phase1(self, l):
        with contextlib.ExitStack() as ph:
            self.ph = ph
            self._phase1(l)
            self.P.barrier()

    def _phase1(self, l):
        P = self.P
        nc = self.nc
        sb = self.psb
        self.xt = [sb(f"xt{i}", [128, D], F32) for i in range(2)]
        self.junk = sb("junk", [128, D], BF16)
        self.ss = [sb(f"ss{i}", [128, 1], F32) for i in range(2)]
        self.rstd = [sb(f"rstd{i}", [128, 1], F32) for i in range(2)]
        self.xs = [sb(f"xs{i}", [128, D], BF16) for i in range(2)]
        self.wP1 = sb("wP1", [128, 8, 3072], BF16)
        self.qk = [sb(f"qk{i}", [128, 4, 8, 64], BF16) for i in range(2)]
        self.xr = [sb(f"xr{i}", [128, 32, 16], F32) for i in range(2)]
        self.rt = [sb(f"rt{i}", [128, 32, 16], F32) for i in range(2)]
        self.ru = [sb(f"ru{i}", [128, 32, 16], F32) for i in range(2)]
        self.vsb = [sb(f"vsb{i}", [128, 2, 512], BF16) for i in range(2)]
        self.tsb = [sb(f"tsb{i}", [128, 16, 128], BF16) for i in range(2)]
        x_src = self.x_in if l == 0 else self.x1
        for gi, c in enumerate((C_AQ, C_AK, C_CQ, C_CK, C_AV, C_CV)):
            self.load_w(self.wP1[:, :, gi * 512:(gi + 1) * 512], f"wP1_{gi}", self.w_in[l][:, c:c + 512], 512)
        wk = [f"wP1_{g}" for g in range(6)]
        P.dma("sp", lambda e: e.dma_start(out=self.gT[:, :], in_=self.norm_g[l].rearrange("(k p) -> p k", p=128),
                                          allow_slow_non_contiguous=True), w=["gT"])
        ctb = self.contrib[l]
        self.k_contrib = [f"ctb_{n}{t}" for n in ("va", "vc", "qa", "ka", "kc") for t in range(NT)] + self.k_kvpad[l]
        self.k_hT = [f"hT{t}" for t in range(NT)]
        for t in range(NT):
            b = t % 2
            xt, ss, rstd, xs, qk, xr, rt, ru, vsb, tsb = (self.xt[b], self.ss[b], self.rstd[b], self.xs[b], self.qk[b],
                                                          self.xr[b], self.rt[b], self.ru[b], self.vsb[b], self.tsb[b])
            kb = f"_{b}"
            P.dma("sp", lambda e, t=t, xt=xt: e.dma_start(out=xt[:, :], in_=x_src[t * 128:(t + 1) * 128, :]),
                  w=["xt" + kb])
            P.op("act", lambda e, xt=xt, ss=ss, junk=self.junk: e.activation(out=junk[:, :], in_=xt[:, :], func=AF.Square,
                                                           accum_out=ss[:, :]),
                 r=["xt" + kb], w=["junk", "ss" + kb])
            P.op("dve", lambda e, ss=ss, rstd=rstd: e.tensor_scalar(out=rstd[:, :], in0=ss[:, :], scalar1=1.0 / D,
                                                                  scalar2=EPS, op0=ALU.mult, op1=ALU.add),
                 r=["ss" + kb], w=["rstd" + kb])
            P.op("act", lambda e, rstd=rstd: e.activation(out=rstd[:, :], in_=rstd[:, :], func=AF.Sqrt),
                 r=["rstd" + kb], w=["rstd" + kb])
            P.op("dve", lambda e, rstd=rstd: e.reciprocal(out=rstd[:, :], in_=rstd[:, :]),
                 r=["rstd" + kb], w=["rstd" + kb])
            P.op("dve", lambda e, xt=xt, xs=xs, rstd=rstd: e.tensor_scalar(out=xs[:, :], in0=xt[:, :],
                                                                         scalar1=rstd[:, 0:1], scalar2=None,
                                                                         op0=ALU.mult),
                 r=["xt" + kb, "rstd" + kb], w=["xs" + kb])
            pT = self.pbf(0)
            for k in range(8):
                P.op("pe", lambda e, k=k, xs=xs: e.transpose(out=pT[:, k * 128:(k + 1) * 128],
                                                             in_=xs[:, k * 128:(k + 1) * 128], identity=self.ident[:, :]),
                     r=["xs" + kb, "ident"], w=["pb0"])
            hTt = self.hT[:, :, t * 128:(t + 1) * 128]
            P.op("dve", lambda e, hTt=hTt: e.tensor_tensor(out=hTt, in0=pT.rearrange("p (k t) -> p k t", k=8),
                                                          in1=bc_free(self.gT[:, :], 2, 128), op=ALU.mult),
                 r=["pb0", "gT"], w=[f"hT{t}"])
            for g in range(6):
                bank = 1 + g
                for k in range(8):
                    P.op("pe", lambda e, g=g, k=k, bank=bank, t=t, wP1=self.wP1: e.matmul(
                        out=self.pbank[bank][:, :], lhsT=self.hT[:, k, t * 128:(t + 1) * 128],
                        rhs=wP1[:, k, g * 512:(g + 1) * 512], start=(k == 0), stop=(k == 7)),
                         r=[f"hT{t}", wk[g]], w=[f"pb{bank}"])
            for g in range(4):
                src = self.pbank[1 + g][:, :].rearrange("p (h d) -> p h d", d=64)
                eng = "act" if g % 2 == 0 else "dve"
                if eng == "act":
                    P.op("act", lambda e, g=g, src=src, qk=qk: e.copy(out=qk[:, g, :, 16:64], in_=src[:, :, 16:64]),
                         r=[f"pb{1 + g}"], w=[f"qkrest{g}" + kb])
                else:
                    P.op("dve", lambda e, g=g, src=src, qk=qk: e.tensor_copy(out=qk[:, g, :, 16:64], in_=src[:, :, 16:64]),
                         r=[f"pb{1 + g}"], w=[f"qkrest{g}" + kb])
                if eng == "act":
                    P.op("act", lambda e, g=g, src=src, xr=xr: e.copy(out=xr[:, g * 8:(g + 1) * 8, :], in_=src[:, :, 0:16]),
                         r=[f"pb{1 + g}"], w=[f"xr{g}" + kb])
                else:
                    P.op("dve", lambda e, g=g, src=src, xr=xr: e.tensor_copy(out=xr[:, g * 8:(g + 1) * 8, :], in_=src[:, :, 0:16]),
                         r=[f"pb{1 + g}"], w=[f"xr{g}" + kb])
            xrk = [f"xr{g}" + kb for g in range(4)]
            csb = bc_free(self.cs[:, t, :], 1, 32)
            P.op("pool", lambda e, xr=xr, rt=rt, csb=csb: e.tensor_tensor(out=rt[:, :, :], in0=xr[:, :, :], in1=csb, op=ALU.mult),
                 r=xrk + self.k_rope, w=["rt" + kb])
            P.op("pool", lambda e, xr=xr, ru=ru, t=t: e.tensor_tensor(out=ru[:, :, 0:8], in0=xr[:, :, 8:16],
                                                                    in1=bc_free(self.sn[:, t, 0:8], 1, 32), op=ALU.mult),
                 r=xrk + self.k_rope, w=["ru_lo" + kb])
            P.op("pool", lambda e, xr=xr, ru=ru, t=t: e.tensor_tensor(out=ru[:, :, 8:16], in0=xr[:, :, 0:8],
                                                                    in1=bc_free(self.sn[:, t, 8:16], 1, 32), op=ALU.mult),
                 r=xrk + self.k_rope, w=["ru_hi" + kb])
            P.op("pool", lambda e, qk=qk, rt=rt, ru=ru: e.tensor_tensor(
                out=qk[:, :, :, 0:16], in0=rt[:, :, :].rearrange("p (g h) d -> p g h d", g=4),
                in1=ru[:, :, :].rearrange("p (g h) d -> p g h d", g=4), op=ALU.add),
                 r=["rt" + kb, "ru_lo" + kb, "ru_hi" + kb], w=["qkrot" + kb])
            P.op("act", lambda e, vsb=vsb: e.copy(out=vsb[:, 0, :], in_=self.pbank[5][:, :]), r=["pb5"], w=["vsb0" + kb])
            P.op("act", lambda e, vsb=vsb: e.copy(out=vsb[:, 1, :], in_=self.pbank[6][:, :]), r=["pb6"], w=["vsb1" + kb])
            P.dma("sp", lambda e, vsb=vsb, t=t: e.dma_start(
                out=ctb[R_VA:R_VA + 512, t * 128:(t + 1) * 128].rearrange("(h k) e -> k h e", h=4),
                in_=vsb[:, 0, :].rearrange("k (h e) -> k h e", h=4)), r=["vsb0" + kb], w=[f"ctb_va{t}"])
            vc_dst = ctb[R_VC:R_VC + 512, :].rearrange("(k a) c -> k (a c)", k=128)[:, t * 512:(t + 1) * 512]
            P.dma("sp", lambda e, vsb=vsb, vc_dst=vc_dst: e.dma_start(out=vc_dst, in_=vsb[:, 1, :]),
                  r=["vsb1" + kb], w=[f"ctb_vc{t}"])
            qkf = qk[:, :, :, :].rearrange("p g h d -> p (g h d)")
            qkk = [f"qkrest{g}" + kb for g in range(4)] + ["qkrot" + kb]
            p7 = self.pbf(7)
            for rnd in range(2):
                for j in range(8):
                    blk = rnd * 8 + j
                    P.op("pe", lambda e, blk=blk, j=j, qkf=qkf: e.transpose(
                        out=p7[:, j * 128:(j + 1) * 128], in_=qkf[:, blk * 128:(blk + 1) * 128],
                        identity=self.ident[:, :]), r=qkk + ["ident"], w=["pb7"])
                eng = "act" if rnd == 0 else "dve"
                if eng == "act":
                    P.op("act", lambda e, rnd=rnd, tsb=tsb: e.copy(out=tsb[:, rnd * 8:(rnd + 1) * 8, :],
                                                                 in_=p7.rearrange("p (j t) -> p j t", j=8)),
                         r=["pb7"], w=[f"tsb{rnd}" + kb])
                else:
                    P.op("dve", lambda e, rnd=rnd, tsb=tsb: e.tensor_copy(out=tsb[:, rnd * 8:(rnd + 1) * 8, :],
                                                                        in_=p7.rearrange("p (j t) -> p j t", j=8)),
                         r=["pb7"], w=[f"tsb{rnd}" + kb])
            P.dma("sp", lambda e, tsb=tsb, t=t: e.dma_start(
                out=ctb[R_QAT:R_QAT + 512, t * 128:(t + 1) * 128].rearrange("(h p) c -> p h c", h=4),
                in_=tsb[:, 0:4, :]), r=["tsb0" + kb], w=[f"ctb_qa{t}"])
            P.dma("sp", lambda e, tsb=tsb, t=t: e.dma_start(
                out=ctb[R_KAT:R_KAT + 512, t * 128:(t + 1) * 128].rearrange("(h p) c -> p h c", h=4),
                in_=tsb[:, 4:8, :]), r=["tsb0" + kb], w=[f"ctb_ka{t}"])
            P.op("pool", lambda e, tsb=tsb, t=t, QCT=self.QCT: e.tensor_copy(out=QCT[:, :, t * 128:(t + 1) * 128], in_=tsb[:, 8:12, :]),
                 r=["tsb1" + kb], w=[f"QCT{t}"])
            P.dma("sp", lambda e, tsb=tsb, t=t: e.dma_start(
                out=ctb[R_KCT:R_KCT + 512, t * 128:(t + 1) * 128].rearrange("(h p) c -> p h c", h=4),
                in_=tsb[:, 12:16, :]), r=["tsb1" + kb], w=[f"ctb_kc{t}"])
        if self.dbg and "contrib" in self.dbg:
            P.dma("sp", lambda e: e.dma_start(out=self.dbg_out["contrib"][:, :], in_=ctb[:, :]), r=self.k_contrib, w=["dbgc"])
        if self.dbg and "hT" in self.dbg:
            P.dma("sp", lambda e: e.dma_start(out=self.dbg_out["hT"][:, :, :], in_=self.hT[:, :, :]),
                  r=[f"hT{t}" for t in range(NT)], w=["dbgh"])


def make_in_maps(inputs):
    x = np.ascontiguousarray(inputs["x"][0])
    pos = np.ascontiguousarray(inputs["positions"][0]).astype(np.int32)
    common = {k: np.ascontiguousarray(inputs[k]) for k in
              ("norm_g", "w_in", "lam_q1", "lam_k1", "lam_q2", "lam_k2", "subln_g", "sgu_ln_g", "sgu_ln_b",
               "sgu_w", "sgu_b", "w_branch", "w_out", "final_g")}
    maps = []
    for c in range(NCORES):
        m = dict(common)
        m["x"] = x[c * SOWN:(c + 1) * SOWN]
        m["pos"] = np.ascontiguousarray(pos[c * SOWN:(c + 1) * SOWN].reshape(NT, 128).T)
        m["cidf"] = np.full((128, 1), float(c), np.float32)
        maps.append(m)
    return maps


def kernel(**inputs):
    b = Builder()
    nc = b.build()
    res = run_bass_kernel_spmd(nc, make_in_maps(inputs), core_ids=list(range(NCORES)))
    out = np.concatenate([np.asarray(r["out"]) for r in res.results], axis=0)
    return out.reshape(1, S, D).astype(np.float32)
```

```python
import contextlib
import math
import numpy as np
import ml_dtypes
import concourse.bass as bass
import concourse.mybir as mybir
from concourse.bass_utils import run_bass_kernel_spmd

F32 = mybir.dt.float32
BF16 = mybir.dt.bfloat16
I32 = mybir.dt.int32
AF = mybir.ActivationFunctionType
ALU = mybir.AluOpType
AX = mybir.AxisListType

NCORES = 8
D = 1024
S = 16384
SOWN = S // NCORES
NT = SOWN // 128
INC = 8704
EPS = 1e-6
C_AQ, C_AK, C_AV, C_AZ, C_BU, C_BV, C_BZ, C_CQ, C_CK, C_CV, C_CZ, C_GL = (
    0, 512, 1024, 1536, 2048, 2560, 3072, 3584, 4096, 4608, 5120, 5632)
R_KAT, R_QAT, R_VA, R_KCT, R_VC, R_TOT = 0, 512, 1024, 1536, 2048, 2560


class Prog:
    CE = ("pe", "act", "dve", "pool")

    def __init__(self):
        self.ops = []
        self.lastw = {}
        self.rd_c = {}
        self.rd_d = {}
        self.sp_init = None
        self.regs = {}
        self.base = set()
        self.base_pending = set()
        self.since_dma = []
        self.last_c = {}

    def barrier(self):
        self.base = set(self.last_c.values()) | set(self.since_dma)
        self.since_dma = []
        self.base_pending = {"pe", "act", "dve", "pool", "sp"}

    def _add(self, eng, fn, r, w, kind):
        w = list(w) + [k for k in r if k.startswith("pb") and k not in w]
        idx = len(self.ops)
        deps = set()
        for k in r:
            if k in self.lastw:
                deps.add(self.lastw[k])
        for k in w:
            if k in self.lastw:
                deps.add(self.lastw[k])
            deps.update(self.rd_c.get(k, {}).values())
            deps.update(self.rd_d.get(k, ()))
        if eng in self.base_pending:
            deps |= self.base
            self.base_pending.discard(eng)
        if kind == "c":
            self.last_c[eng] = idx
        else:
            self.since_dma.append(idx)
        import sys as _s
        self.ops.append(dict(eng=eng, fn=fn, deps=deps, kind=kind, line=_s._getframe(2).f_lineno))
        for k in r:
            if kind == "c":
                self.rd_c.setdefault(k, {})[eng] = idx
            else:
                self.rd_d.setdefault(k, []).append(idx)
        for k in w:
            self.lastw[k] = idx
            self.rd_c[k] = {}
            self.rd_d[k] = []
        return idx

    def op(self, eng, fn, r=(), w=()):
        return self._add(eng, fn, r, w, "c")

    def dma(self, q, fn, r=(), w=(), inc=16, semq=None):
        i = self._add(q, fn, r, w, "d")
        self.ops[i]["inc"] = inc
        self.ops[i]["semq"] = semq or q
        return i

    def emit(self, nc, st, ndma={"sp": 24, "pool": 8, "act": 4, "cc": 2}, limit=None):
        ops = self.ops if limit is None else self.ops[:limit]
        need = [False] * len(ops)
        for o in ops:
            for d in o["deps"]:
                dd = ops[d]
                if dd["kind"] == "c" and not (dd["eng"] == "pe" and o["eng"] == "pe" and o["kind"] == "c"):
                    need[d] = True
        csem = {e: st.enter_context(nc.semaphore("c_" + e)) for e in self.CE}
        dsem = {q: [st.enter_context(nc.semaphore(f"d_{q}{i}")) for i in range(n)] for q, n in ndma.items()}
        cnt = {e: 0 for e in self.CE}
        dcnt = {q: 0 for q in ndma}
        dval = {q: [0] * n for q, n in ndma.items()}
        for i, o in enumerate(ops):
            if o["kind"] == "c":
                if need[i]:
                    cnt[o["eng"]] += 1
                    o["sig"] = cnt[o["eng"]]
            else:
                q = o["semq"]
                slot = dcnt[q] % ndma[q]
                dcnt[q] += 1
                o["slot"] = slot
                o["prev"] = dval[q][slot]
                dval[q][slot] += o["inc"]
                o["val"] = dval[q][slot]
        per = {e: [] for e in ("pe", "act", "dve", "pool", "sp")}
        for i, o in enumerate(ops):
            per[o["eng"]].append(i)
        block = st.enter_context(nc.Block())

        def run(ename, e):
            waited = {}
            if ename == "sp" and self.sp_init is not None:
                self.sp_init(e)
            for i in per[ename]:
                o = ops[i]
                needw = {}
                for d in o["deps"]:
                    dd = ops[d]
                    if dd["kind"] == "c":
                        if dd["eng"] == "pe" and ename == "pe" and o["kind"] == "c":
                            continue
                        key = ("c", dd["eng"]); val = dd["sig"]
                    else:
                        key = ("d", dd["semq"], dd["slot"]); val = dd["val"]
                    if needw.get(key, 0) < val:
                        needw[key] = val
                if o["kind"] == "d" and o["prev"] > 0:
                    key = ("d", o["semq"], o["slot"])
                    if needw.get(key, 0) < o["prev"]:
                        needw[key] = o["prev"]
                for key, val in needw.items():
                    if waited.get(key, 0) < val:
                        sem = csem[key[1]] if key[0] == "c" else dsem[key[1]][key[2]]
                        e.wait_ge(sem, val)
                        waited[key] = val
                ins = o["fn"](e)
                if o["kind"] == "c":
                    if need[i]:
                        ins.then_inc(csem[ename], 1)
                else:
                    ins.then_inc(dsem[o["semq"]][o["slot"]], o["inc"])
            for qn in ([ename] + (["cc"] if ename == "pool" else [])):
                if qn in ndma:
                    for slot, v in enumerate(dval[qn]):
                        if v > 0 and waited.get(("d", qn, slot), 0) < v:
                            e.wait_ge(dsem[qn][slot], v)

        @block.tensor
        def _(e):
            run("pe", e)

        @block.scalar
        def _(e):
            run("act", e)

        @block.vector
        def _(e):
            run("dve", e)

        @block.gpsimd
        def _(e):
            run("pool", e)

        @block.sync
        def _(e):
            run("sp", e)


def bc_free(ap, pos, n):
    a = ap.unsqueeze(pos)
    shp = list(a.shape)
    shp[pos] = n
    return a.to_broadcast(shp)


class Builder:
    def __init__(self, layers=(0, 1), final_norm=True, dbg=None, stop_after=None):
        self.layers = layers
        self.final_norm = final_norm
        self.dbg = dbg
        self.stop_after = stop_after
        self.limit = None
        self.nc = bass.Bass("TRN2", target_bir_lowering=False)
        self.P = Prog()
        self.uid = 0

    def din(self, name, shape, dt):
        return self.nc.dram_tensor(name, list(shape), dt, kind="ExternalInput").ap()

    def dout(self, name, shape, dt):
        return self.nc.dram_tensor(name, list(shape), dt, kind="ExternalOutput").ap()

    def dint(self, name, shape, dt):
        return self.nc.dram_tensor(name, list(shape), dt).ap()

    def sb(self, name, shape, dt):
        return self.st.enter_context(self.nc.sbuf_tensor("sb_" + name, list(shape), dt))

    def psb(self, name, shape, dt):
        self.uid += 1
        return self.ph.enter_context(self.nc.sbuf_tensor(f"ph{self.uid}_" + name, list(shape), dt))

    def ps(self, name, shape, dt):
        return self.st.enter_context(self.nc.psum_tensor("ps_" + name, list(shape), dt))

    def build(self):
        nc = self.nc
        with contextlib.ExitStack() as st:
            self.st = st
            self.declare_io()
            self.alloc()

            def sp_init(e):
                pid = e.partition_id()
                self.P.regs["p128"] = e.snap(pid * 128)
                self.P.regs["p256"] = e.snap(pid * 256)
                self.P.regs["prow"] = e.snap(pid * R_TOT)
                self.P.regs["p1"] = e.snap(pid * 1)
            self.P.sp_init = sp_init
            self.setup_consts()
            for l in self.layers:
                self.layer(l)
            self.P.emit(nc, st, limit=self.limit)
        return nc

    def declare_io(self):
        L = 2
        self.x_in = self.din("x", [SOWN, D], F32)
        self.pos_in = self.din("pos", [128, NT], I32)
        self.cidf_in = self.din("cidf", [128, 1], F32)
        self.norm_g = self.din("norm_g", [L, D], F32)
        self.w_in = self.din("w_in", [L, D, INC], F32)
        self.lam4 = [self.din(n, [L, 64], F32) for n in ("lam_q1", "lam_k1", "lam_q2", "lam_k2")]
        self.subln_g = self.din("subln_g", [L, 128], F32)
        self.sgu_ln_g = self.din("sgu_ln_g", [L, 512], F32)
        self.sgu_ln_b = self.din("sgu_ln_b", [L, 512], F32)
        self.sgu_w = self.din("sgu_w", [L, 4, 128, 128], F32)
        self.sgu_b = self.din("sgu_b", [L, 4, 128], F32)
        self.w_branch = self.din("w_branch", [L, 3, 512, D], F32)
        self.w_out = self.din("w_out", [L, D, D], F32)
        self.final_g = self.din("final_g", [D], F32)
        self.out = self.dout("out", [SOWN, D], F32)
        self.contrib = [self.dint(f"contrib{l}", [R_TOT, 2048], BF16) for l in range(2)]
        self.kvall = [self.dint(f"kvall{l}", [(NCORES + 1) * R_TOT, 2048], BF16) for l in range(2)]
        self.contrib2 = [self.dint(f"contribb{l}", [512, 2048], BF16) for l in range(2)]
        self.yaall = [self.dint(f"yaall{l}", [NCORES * 512, 2048], BF16) for l in range(2)]
        self.x1 = self.dint("x1buf", [SOWN, D], F32)
        self.qown = [self.dint(f"qown{l}", [512, 2048], BF16) for l in range(2)]
        self.cown = [self.dint(f"cown{l}", [2, 1024, 2048], BF16) for l in range(2)]
        self.yaown = [self.dint(f"yaown{l}", [512, 2048], BF16) for l in range(2)]
        if self.dbg:
            self.dbg_out = {k: self.dout("dbg_" + k, shp, dt) for k, (shp, dt) in self.dbg.items()}

    def alloc(self):
        sb, ps = self.sb, self.ps
        self.ident = sb("ident", [128, 128], BF16)
        self.ones = sb("ones", [128, 128], BF16)
        self.onesf = sb("onesf", [128, 128], F32)
        self.zeros = sb("zeros", [128, 1024], BF16)
        self.invf = sb("invf", [128, 8], F32)
        self.posi = sb("posi", [128, NT], I32)
        self.posf = sb("posf", [128, NT], F32)
        self.cs = sb("cs", [128, NT, 16], F32)
        self.sn = sb("sn", [128, NT, 16], F32)
        self.ang = sb("ang", [128, NT, 16], F32)
        self.cidf = sb("cidf", [128, 1], F32)
        self.rk = sb("rk", [128, NT, 8], F32)
        self.rki = sb("rki", [128, NT, 8], I32)
        self.rm = sb("rm", [128, NT, 8], F32)
        self.gT = sb("gT", [128, 8], F32)
        self.hT = sb("hT", [128, 8, SOWN], BF16)
        self.ybT = sb("ybT", [128, 4, SOWN], BF16)
        self.ycT = sb("ycT", [128, 4, SOWN], BF16)
        self.maskC = sb("maskC", [128, 17, 128], BF16)
        self.relb = sb("relb", [128, 128], F32)
        self.maskA = sb("maskA", [128, 8, 128], BF16)
        self.vhalo = sb("vhalo", [128, 128], BF16)
        self.small = sb("small", [128, 64], F32)
        self.pbank = [ps(f"pb{i}", [128, 512], F32) for i in range(8)]

    def pbf(self, i):
        return self.pbank[i][:, :].bitcast(BF16)

    def setup_consts(self):
        P = self.P
        ident, ones, onesf, zeros = self.ident, self.ones, self.onesf, self.zeros
        P.op("pool", lambda e: e.memset(ones[:, :], 1.0), w=["ones"])
        P.op("pool", lambda e: e.memset(onesf[:, :], 1.0), w=["onesf"])
        P.op("pool", lambda e: e.memset(zeros[:, :], 0.0), w=["zeros"])
        P.op("pool", lambda e: e.affine_select(out=ident[:, :], in_=ones[:, :], pattern=[[1, 128]],
                                               compare_op=ALU.is_equal, fill=0.0, base=0,
                                               channel_multiplier=-1), r=["ones"], w=["ident"])
        for i in range(8):
            v = float(500000.0 ** (-i / 8.0))
            P.op("pool", lambda e, i=i, v=v: e.memset(self.invf[:, i:i + 1], v), w=[f"invf{i}"])
        invk = [f"invf{i}" for i in range(8)]
        P.dma("sp", lambda e: e.dma_start(out=self.posi[:, :], in_=self.pos_in[:, :]), w=["posi"])
        P.dma("sp", lambda e: e.dma_start(out=self.cidf[:, :], in_=self.cidf_in[:, :]), w=["cidf"])
        P.op("dve", lambda e: e.tensor_copy(out=self.posf[:, :], in_=self.posi[:, :]), r=["posi"], w=["posf"])
        for t in range(NT):
            P.op("dve", lambda e, t=t: e.tensor_scalar(out=self.ang[:, t, 0:8], in0=self.invf[:, :],
                                                      scalar1=self.posf[:, t:t + 1], scalar2=None,
                                                      op0=ALU.mult),
                 r=invk + ["posf"], w=[f"ang{t}"])
        angk = [f"ang{t}" for t in range(NT)]
        a8 = self.ang[:, :, 0:8]
        PI = math.pi
        C1 = 6.28125
        C2 = 2.0 * math.pi - C1

        def sin_of(dst, shift, tag):
            r_ = self.ang[:, :, 8:16]
            kf = self.rk[:, :, :]
            ki = self.rki[:, :, :]
            m = self.rm[:, :, :]
            P.op("dve", lambda e: e.tensor_scalar(out=r_, in0=a8, scalar1=shift, scalar2=None, op0=ALU.add),
                 r=angk + ["angs"], w=["angs"])
            P.op("dve", lambda e: e.tensor_scalar(out=kf, in0=r_, scalar1=1.0 / (2.0 * PI), scalar2=None, op0=ALU.mult),
                 r=["angs"], w=["rk"])
            P.op("dve", lambda e: e.tensor_copy(out=ki, in_=kf), r=["rk"], w=["rki"])
            P.op("dve", lambda e: e.tensor_copy(out=kf, in_=ki), r=["rki"], w=["rk"])
            P.op("dve", lambda e: e.scalar_tensor_tensor(out=r_, in0=kf, scalar=-C1, in1=r_, op0=ALU.mult, op1=ALU.add),
                 r=["rk", "angs"], w=["angs"])
            P.op("dve", lambda e: e.scalar_tensor_tensor(out=r_, in0=kf, scalar=-C2, in1=r_, op0=ALU.mult, op1=ALU.add),
                 r=["rk", "angs"], w=["angs"])
            P.op("dve", lambda e: e.tensor_scalar(out=m, in0=r_, scalar1=PI, scalar2=-2.0 * PI, op0=ALU.is_gt, op1=ALU.mult),
                 r=["angs"], w=["rm"])
            P.op("dve", lambda e: e.tensor_tensor(out=r_, in0=r_, in1=m, op=ALU.add), r=["angs", "rm"], w=["angs"])
            P.op("dve", lambda e: e.tensor_scalar(out=m, in0=r_, scalar1=-PI, scalar2=2.0 * PI, op0=ALU.is_lt, op1=ALU.mult),
                 r=["angs"], w=["rm"])
            P.op("dve", lambda e: e.tensor_tensor(out=r_, in0=r_, in1=m, op=ALU.add), r=["angs", "rm"], w=["angs"])
            P.op("dve", lambda e: e.tensor_scalar(out=r_, in0=r_, scalar1=-PI, scalar2=PI, op0=ALU.max, op1=ALU.min),
                 r=["angs"], w=["angs"])
            P.op("act", lambda e: e.activation(out=dst, in_=r_, func=AF.Sin), r=["angs"], w=[tag])

        sin_of(self.sn[:, :, 8:16], 0.0, "sn_hi")
        P.op("dve", lambda e: e.tensor_scalar(out=self.sn[:, :, 0:8], in0=self.sn[:, :, 8:16], scalar1=-1.0,
                                              scalar2=None, op0=ALU.mult), r=["sn_hi"], w=["sn_lo"])
        sin_of(self.cs[:, :, 0:8], 0.5 * PI, "cs_lo")
        P.op("dve", lambda e: e.tensor_copy(out=self.cs[:, :, 8:16], in_=self.cs[:, :, 0:8]), r=["cs_lo"], w=["cs_hi"])
        self.k_rope = ["sn_hi", "sn_lo", "cs_lo", "cs_hi"]
        self.setup_masks()

    def setup_masks(self):
        P = self.P
        relb, small = self.relb, self.small
        reli = self.sb("reli", [128, 128], I32)
        tmpa = self.sb("mtmpa", [128, 128], F32)
        tmpb = self.sb("mtmpb", [128, 128], F32)
        tmpc = self.sb("mtmpc", [128, 128], F32)
        tmpi = self.sb("mtmpi", [128, 128], I32)
        P.op("pool", lambda e: e.iota(out=reli[:, :], pattern=[[1, 128]], base=0, channel_multiplier=-1), w=["reli"])
        P.op("dve", lambda e: e.tensor_copy(out=relb[:, :], in_=reli[:, :]), r=["reli"], w=["relb"])
        def band(Dl, hi):
            c0 = float(128 * Dl)
            P.op("dve", lambda e: e.tensor_scalar(out=tmpa[:, :], in0=relb[:, :], scalar1=c0, scalar2=0.0,
                                                  op0=ALU.add, op1=ALU.is_ge), r=["relb"], w=["mta"])
            P.op("dve", lambda e: e.tensor_scalar(out=tmpb[:, :], in0=relb[:, :], scalar1=c0, scalar2=float(hi),
                                                  op0=ALU.add, op1=ALU.is_le), r=["relb"], w=["mtb"])
            P.op("dve", lambda e: e.tensor_tensor(out=tmpa[:, :], in0=tmpa[:, :], in1=tmpb[:, :], op=ALU.mult),
                 r=["mta", "mtb"], w=["mta"])

        def lattice(Dl, dil):
            c0 = float(128 * Dl + 4096)
            P.op("dve", lambda e: e.tensor_scalar(out=tmpc[:, :], in0=relb[:, :], scalar1=c0, scalar2=None,
                                                  op0=ALU.add), r=["relb"], w=["mtc"])
            P.op("dve", lambda e: e.tensor_copy(out=tmpi[:, :], in_=tmpc[:, :]), r=["mtc"], w=["mti"])
            P.op("dve", lambda e: e.tensor_single_scalar(out=tmpi[:, :], in_=tmpi[:, :], scalar=dil - 1, op=ALU.bitwise_and),
                 r=["mti"], w=["mti"])
            P.op("dve", lambda e: e.tensor_copy(out=tmpc[:, :], in_=tmpi[:, :]), r=["mti"], w=["mtc"])
            P.op("dve", lambda e: e.tensor_scalar(out=tmpb[:, :], in0=tmpc[:, :], scalar1=0.0, scalar2=None, op0=ALU.is_equal),
                 r=["mtc"], w=["mtb"])
            P.op("dve", lambda e: e.tensor_tensor(out=tmpa[:, :], in0=tmpa[:, :], in1=tmpb[:, :], op=ALU.mult),
                 r=["mta", "mtb"], w=["mta"])

        acc = self.sb("macc", [128, 128], F32)

        def one_mask(Dl):
            band(Dl, 128)
            P.op("dve", lambda e: e.tensor_copy(out=acc[:, :], in_=tmpa[:, :]), r=["mta"], w=["macc"])
            band(Dl, 512)
            lattice(Dl, 4)
            P.op("dve", lambda e: e.tensor_tensor(out=acc[:, :], in0=acc[:, :], in1=tmpa[:, :], op=ALU.add),
                 r=["mta", "macc"], w=["macc"])
            band(Dl, 2048)
            lattice(Dl, 16)
            P.op("dve", lambda e: e.tensor_tensor(out=self.maskC[:, Dl, :], in0=acc[:, :], in1=tmpa[:, :], op=ALU.add),
                 r=["mta", "macc"], w=["maskC"])

        for Dl in range(17):
            one_mask(Dl)
        for v in range(8):
            P.op("dve", lambda e, v=v: e.tensor_scalar(out=small[:, v:v + 1], in0=self.cidf[:, 0:1], scalar1=float(-v), scalar2=128.0,
                                                      op0=ALU.add, op1=ALU.mult), r=["cidf"], w=[f"cshift{v}"])
            P.op("dve", lambda e, v=v: e.tensor_scalar(out=self.maskA[:, v, :], in0=relb[:, :], scalar1=small[:, v:v + 1], scalar2=0.0,
                                                      op0=ALU.add, op1=ALU.is_ge), r=["relb", f"cshift{v}"], w=["maskA"])
        P.op("dve", lambda e: e.tensor_scalar(out=small[:, 8:9], in0=self.cidf[:, 0:1], scalar1=1.0, scalar2=None, op0=ALU.min),
             r=["cidf"], w=["hv"])
        P.op("dve", lambda e: e.tensor_scalar(out=self.vhalo[:, :], in0=self.onesf[:, :], scalar1=small[:, 8:9], scalar2=None,
                                              op0=ALU.mult), r=["onesf", "hv"], w=["vhalo"])
        for l in range(2):
            for i in range(8):
                r0 = R_KCT + i * 128
                P.dma("sp", lambda e, l=l, r0=r0: e.dma_start(out=self.kvall[l][r0:r0 + 128, :].rearrange("p (a c) -> p a c", a=2),
                                                             in_=bc_free(self.zeros[:, :], 1, 2)),
                      r=["zeros"], w=[f"kvpad{l}_{i}"])
        self.k_kvpad = [[f"kvpad{l}_{i}" for i in range(8)] for l in range(2)]

    def load_w(self, dst, key, src, ncols):
        P = self.P
        nk = dst.shape[1]
        for k in range(nk):
            for c0 in range(0, ncols, 1024):
                c1 = min(ncols, c0 + 1024)
                P.dma("pool", lambda e, k=k, c0=c0, c1=c1: e.dma_start(out=dst[:, k, c0:c1],
                                                                     in_=src[k * 128:(k + 1) * 128, c0:c1]),
                      w=[key])

    def layer(self, l):
        self.lam_init = 0.8 - 0.6 * math.exp(-0.3 * l)
        with contextlib.ExitStack() as lst:
            self.QCT = lst.enter_context(self.nc.sbuf_tensor(f"sb_QCT{l}", [128, 4, SOWN], BF16))
            done = self.layer_front(l)
        if done:
            self.phase4(l)

    def layer_front(self, l):
        self.phase1(l)
        if self.stop_after == "p1":
            return False
        self.phaseB(l)
        if self.stop_after == "pB":
            return False
        self.phaseA(l)
        if self.stop_after == "pA":
            return False
        self.phaseC(l)
        if self.stop_after == "pC":
            return False
        return True

    def allgather1(self, l):
        P = self.P
        kv = self.kvall[l]
        P.dma("pool", lambda e: e.collective_compute(
            "AllGather", ALU.bypass, replica_groups=[list(range(NCORES))],
            ins=[self.contrib[l].opt()], outs=[kv[R_TOT:, :].opt()]),
            r=self.k_contrib, w=[f"kvall{l}"], inc=1, semq="cc")

    def allgather2(self, l):
        P = self.P
        P.dma("pool", lambda e: e.collective_compute(
            "AllGather", ALU.bypass, replica_groups=[list(range(NCORES))],
            ins=[self.contrib2[l].opt()], outs=[self.yaall[l].opt()]),
            r=[f"ctb2_{h}_{b}" for h in range(4) for b in range(4)], w=[f"yaall{l}"], inc=1, semq="cc")

    def gelu_from_psum(self, src, dst, tmp1, tmp2, rkeys, wkeys, tag):
        P = self.P
        P.op("act", lambda e: e.activation(out=tmp1, in_=src, func=AF.Square), r=rkeys, w=[tag + "_t1"])
        P.op("dve", lambda e: e.tensor_scalar(out=tmp1, in0=tmp1, scalar1=0.044715, scalar2=1.0, op0=ALU.mult, op1=ALU.add),
             r=[tag + "_t1"], w=[tag + "_t1"])
        P.op("dve", lambda e: e.tensor_tensor(out=tmp2, in0=tmp1, in1=src, op=ALU.mult), r=[tag + "_t1"] + rkeys, w=[tag + "_t2"])
        P.op("act", lambda e: e.activation(out=tmp2, in_=tmp2, func=AF.Sigmoid, scale=1.5957691216057308),
             r=[tag + "_t2"], w=[tag + "_t2"])
        P.op("dve", lambda e: e.tensor_tensor(out=dst, in0=tmp2, in1=src, op=ALU.mult), r=[tag + "_t2"] + rkeys, w=wkeys)

    def phaseB(self, l):
        with contextlib.ExitStack() as ph:
            self.ph = ph
            self._phaseB(l)
            self.P.barrier()

    def _phaseB(self, l):
        P = self.P
        sb = self.psb
        wB = sb("wB", [128, 8, 1536], BF16)
        for gi, c in enumerate((C_BU, C_BV, C_BZ)):
            self.load_w(wB[:, :, gi * 512:(gi + 1) * 512], f"wB_{gi}", self.w_in[l][:, c:c + 512], 512)
        wcf = sb("wcf", [128, 4, 128], F32)
        wcb = sb("wcb", [128, 4, 128], BF16)
        wcT = sb("wcT", [128, 4, 128], BF16)
        sbb = sb("sbb", [128, 4, 128], F32)
        lng = sb("lng", [128, 512], F32)
        lnb = sb("lnb", [128, 512], F32)
        P.dma("sp", lambda e: e.dma_start(out=wcf[:, :, :], in_=self.sgu_w[l].rearrange("g t s -> t g s")), w=["wcf"])
        P.dma("sp", lambda e: e.dma_start(out=sbb[:, :, :].rearrange("p g t -> p (g t)"),
                                          in_=self.sgu_b[l].rearrange("g t -> (g t)").partition_broadcast(128)), w=["sbb"])
        P.dma("sp", lambda e: e.dma_start(out=lng[:, :], in_=self.sgu_ln_g[l].partition_broadcast(128)), w=["lng"])
        P.dma("sp", lambda e: e.dma_start(out=lnb[:, :], in_=self.sgu_ln_b[l].partition_broadcast(128)), w=["lnb"])
        for g in range(4):
            P.op("pool", lambda e, g=g: e.affine_select(out=wcb[:, g, :], in_=wcf[:, g, :], pattern=[[-1, 128]],
                                                        compare_op=ALU.is_ge, fill=0.0, base=0, channel_multiplier=1),
                 r=["wcf"], w=[f"wcb{g}"])
        p0 = self.pbf(0)
        for g in range(4):
            P.op("pe", lambda e, g=g: e.transpose(out=p0[:, g * 128:(g + 1) * 128], in_=wcb[:, g, :], identity=self.ident[:, :]),
                 r=[f"wcb{g}", "ident"], w=["pb0"])
        P.op("dve", lambda e: e.tensor_copy(out=wcT[:, :, :].rearrange("p g t -> p (g t)"), in_=p0[:, 0:512]), r=["pb0"], w=["wcT"])
        self.allgather1(l)
        uz = [sb(f"uz{i}", [128, 4, 512], BF16) for i in range(2)]
        ug = [sb(f"ug{i}", [128, 512], F32) for i in range(2)]
        t1 = [sb(f"bt1{i}", [128, 512], F32) for i in range(2)]
        t2 = [sb(f"bt2{i}", [128, 512], F32) for i in range(2)]
        t1v = [sb(f"bt1v{i}", [128, 512], F32) for i in range(2)]
        t2v = [sb(f"bt2v{i}", [128, 512], F32) for i in range(2)]
        zs = [sb(f"zs{i}", [128, 512], F32) for i in range(2)]
        vg = [sb(f"vg{i}", [128, 512], F32) for i in range(2)]
        vb = [sb(f"vb{i}", [128, 512], BF16) for i in range(2)]
        st6 = [sb(f"st6{i}", [128, 6], F32) for i in range(2)]
        mv = [sb(f"mv{i}", [128, 2], F32) for i in range(2)]
        mt = [sb(f"mt{i}", [128, 4, 128], F32) for i in range(2)]
        cnt = 0
        for q4 in range(4):
            cols = slice(q4 * 512, (q4 + 1) * 512)
            uzb = uz[q4 % 2]
            kz = f"uz_{q4 % 2}"
            for j in range(4):
                pb_u, pb_z = 1 + (j % 2) * 2, 2 + (j % 2) * 2
                i2 = cnt % 2
                cnt += 1
                for k in range(8):
                    P.op("pe", lambda e, k=k, j=j, pb_u=pb_u, cols=cols: e.matmul(out=self.pbank[pb_u][:, :], lhsT=wB[:, k, j * 128:(j + 1) * 128],
                                                                      rhs=self.hT[:, k, cols], start=(k == 0), stop=(k == 7)),
                         r=["wB_0"] + self.k_hT, w=[f"pb{pb_u}"])
                for k in range(8):
                    P.op("pe", lambda e, k=k, j=j, pb_z=pb_z, cols=cols: e.matmul(out=self.pbank[pb_z][:, :],
                                                                      lhsT=wB[:, k, 1024 + j * 128:1024 + (j + 1) * 128],
                                                                      rhs=self.hT[:, k, cols], start=(k == 0), stop=(k == 7)),
                         r=["wB_2"] + self.k_hT, w=[f"pb{pb_z}"])
                self.gelu_from_psum(self.pbank[pb_u][:, :], ug[i2][:, :], t1[i2][:, :], t2[i2][:, :], [f"pb{pb_u}"], [f"ug{i2}"], f"gu{i2}")
                P.op("act", lambda e, pb_z=pb_z, i2=i2: e.activation(out=zs[i2][:, :], in_=self.pbank[pb_z][:, :], func=AF.Silu),
                     r=[f"pb{pb_z}"], w=[f"zs{i2}"])
                P.op("dve", lambda e, j=j, i2=i2, uzb=uzb: e.tensor_tensor(out=uzb[:, j, :], in0=ug[i2][:, :], in1=zs[i2][:, :], op=ALU.mult),
                     r=[f"ug{i2}", f"zs{i2}"], w=[kz + f"_{j}"])
            if self.dbg and "B_uz" in self.dbg and q4 == 0:
                P.dma("sp", lambda e, uzb=uzb: e.dma_start(out=self.dbg_out["B_uz"][:, :, :], in_=uzb[:, :, :]),
                      r=[kz + f"_{j}" for j in range(4)], w=["dbg_uz"])
            for tt in range(4):
                t = q4 * 4 + tt
                i2 = t % 2
                pbv, pbm = 5 + i2, 7
                for k in range(8):
                    P.op("pe", lambda e, k=k, t=t, pbv=pbv: e.matmul(out=self.pbank[pbv][:, :], lhsT=self.hT[:, k, t * 128:(t + 1) * 128],
                                                                    rhs=wB[:, k, 512:1024], start=(k == 0), stop=(k == 7)),
                         r=["wB_1"] + self.k_hT, w=[f"pb{pbv}"])
                self.gelu_from_psum(self.pbank[pbv][:, :], vg[i2][:, :], t1v[i2][:, :], t2v[i2][:, :], [f"pb{pbv}"], [f"vg{i2}"], f"gv{i2}")
                P.op("dve", lambda e, i2=i2: e.bn_stats(out=st6[i2][:, :], in_=vg[i2][:, :]), r=[f"vg{i2}"], w=[f"st6{i2}"])
                P.op("dve", lambda e, i2=i2: e.bn_aggr(out=mv[i2][:, :], in_=st6[i2][:, :]), r=[f"st6{i2}"], w=[f"mv{i2}"])
                P.op("dve", lambda e, i2=i2: e.tensor_scalar(out=mv[i2][:, 1:2], in0=mv[i2][:, 1:2], scalar1=EPS, scalar2=None, op0=ALU.add),
                     r=[f"mv{i2}"], w=[f"mv{i2}"])
                P.op("act", lambda e, i2=i2: e.activation(out=mv[i2][:, 1:2], in_=mv[i2][:, 1:2], func=AF.Sqrt), r=[f"mv{i2}"], w=[f"mv{i2}"])
                P.op("dve", lambda e, i2=i2: e.reciprocal(out=mv[i2][:, 1:2], in_=mv[i2][:, 1:2]), r=[f"mv{i2}"], w=[f"mv{i2}"])
                P.op("dve", lambda e, i2=i2: e.tensor_scalar(out=vg[i2][:, :], in0=vg[i2][:, :], scalar1=mv[i2][:, 0:1], scalar2=mv[i2][:, 1:2],
                                                            op0=ALU.subtract, op1=ALU.mult), r=[f"vg{i2}", f"mv{i2}"], w=[f"vg{i2}"])
                P.op("dve", lambda e, i2=i2: e.tensor_tensor(out=vg[i2][:, :], in0=vg[i2][:, :], in1=lng[:, :], op=ALU.mult),
                     r=[f"vg{i2}", "lng"], w=[f"vg{i2}"])
                P.op("dve", lambda e, i2=i2: e.tensor_tensor(out=vb[i2][:, :], in0=vg[i2][:, :], in1=lnb[:, :], op=ALU.add),
                     r=[f"vg{i2}", "lnb"], w=[f"vb{i2}"])
                for g in range(4):
                    P.op("pe", lambda e, g=g, i2=i2: e.matmul(out=self.pbank[pbm][:, g * 128:(g + 1) * 128], lhsT=vb[i2][:, g * 128:(g + 1) * 128],
                                                             rhs=wcT[:, g, :], start=True, stop=True),
                         r=[f"vb{i2}", "wcT"], w=[f"pb{pbm}"])
                P.op("dve", lambda e, i2=i2: e.tensor_tensor(out=mt[i2][:, :, :], in0=self.pbank[pbm][:, :].rearrange("p (g t) -> p g t", g=4),
                                                            in1=sbb[:, :, :], op=ALU.add), r=[f"pb{pbm}", "sbb"], w=[f"mt{i2}"])
                if self.dbg and "B_mt" in self.dbg and t == 0:
                    P.dma("sp", lambda e, i2=i2: e.dma_start(out=self.dbg_out["B_mt"][:, :, :], in_=mt[i2][:, :, :]), r=[f"mt{i2}"], w=["dbg_mt"])
                    P.dma("sp", lambda e, i2=i2: e.dma_start(out=self.dbg_out["B_vb"][:, :], in_=vb[i2][:, :]), r=[f"vb{i2}"], w=["dbg_vb"])
                    P.dma("sp", lambda e: e.dma_start(out=self.dbg_out["B_wcT"][:, :, :], in_=wcT[:, :, :]), r=["wcT"], w=["dbg_wcT"])
                P.op("dve", lambda e, i2=i2, t=t, tt=tt, uzb=uzb: e.tensor_tensor(out=self.ybT[:, :, t * 128:(t + 1) * 128], in0=mt[i2][:, :, :],
                                                                                  in1=uzb[:, :, tt * 128:(tt + 1) * 128], op=ALU.mult),
                     r=[f"mt{i2}"] + [kz + f"_{j}" for j in range(4)], w=[f"ybT{t}"])
        if self.dbg and "ybT" in self.dbg:
            P.dma("sp", lambda e: e.dma_start(out=self.dbg_out["ybT"][:, :, :], in_=self.ybT[:, :, :]),
                  r=[f"ybT{t}" for t in range(NT)], w=["dbgyb"])

    def phaseA(self, l):
        with contextlib.ExitStack() as ph:
            self.ph = ph
            self._phaseA(l)
            self.P.barrier()

    def _phaseA(self, l):
        P = self.P
        sb = self.psb
        kv = self.kvall[l]
        kvk = [f"kvall{l}"]
        KT = sb("KT", [128, 8, 2048], BF16)
        VV = sb("VV", [128, 8, 2048], BF16)
        Q1 = [sb(f"Q1p{i}", [128, 2048], BF16) for i in range(1)] * 2
        Q2 = [sb(f"Q2p{i}", [128, 2048], BF16) for i in range(1)] * 2
        E = [sb(f"E{i}", [128, 2, 512], BF16) for i in range(4)]
        fin = [sb(f"fin{i}", [128, 512], F32) for i in range(4)]
        fin = [fin[0], fin[1], fin[0], fin[2], fin[3]]
        sqb = sb("sqb", [128, 512], BF16)
        yst = [sb(f"yst{i}", [128, 512], BF16) for i in range(2)]
        lamv = sb("lamv", [128, 4, 64], F32)
        lamt = sb("lamt", [128, 2, 64], F32)
        lams = sb("lams", [128, 4], F32)
        gcol = sb("gcol", [128, 1], F32)
        for i in range(4):
            P.dma("sp", lambda e, i=i: e.dma_start(out=lamv[:, i, :], in_=self.lam4[i][l].partition_broadcast(128)), w=[f"lamv{i}"])
        P.dma("sp", lambda e: e.dma_start(out=gcol[:, :], in_=self.subln_g[l].rearrange("(e o) -> e o", o=1)), w=["gcol"])
        for m in range(2):
            P.op("dve", lambda e, m=m: e.tensor_tensor(out=lamt[:, m, :], in0=lamv[:, 2 * m, :], in1=lamv[:, 2 * m + 1, :], op=ALU.mult),
                 r=[f"lamv{2 * m}", f"lamv{2 * m + 1}"], w=[f"lamt{m}"])
            P.op("dve", lambda e, m=m: e.reduce_sum(out=lams[:, m:m + 1], in_=lamt[:, m, :], axis=AX.X), r=[f"lamt{m}"], w=[f"lams{m}"])
            P.op("act", lambda e, m=m: e.activation(out=lams[:, m:m + 1], in_=lams[:, m:m + 1], func=AF.Exp), r=[f"lams{m}"], w=[f"lams{m}"])
        P.op("dve", lambda e: e.tensor_tensor(out=lams[:, 2:3], in0=lams[:, 1:2], in1=lams[:, 0:1], op=ALU.subtract),
             r=["lams0", "lams1"], w=["neglam"])
        li = self.lam_init
        P.op("dve", lambda e: e.tensor_scalar(out=lams[:, 2:3], in0=lams[:, 2:3], scalar1=-li, scalar2=None, op0=ALU.add),
             r=["neglam"], w=["neglam"])
        P.op("dve", lambda e: e.tensor_scalar(out=gcol[:, :], in0=gcol[:, :], scalar1=1.0 - li, scalar2=None, op0=ALU.mult),
             r=["gcol"], w=["gcol"])
        neglam = lams[:, 2:3]
        for i in range(1):
            P.op("pool", lambda e, i=i: e.memset(Q1[i][64:128, :], 0.0), w=[f"Q1z{i}"])
            P.op("pool", lambda e, i=i: e.memset(Q2[i][0:64, :], 0.0), w=[f"Q2z{i}"])
        kv4 = kv.rearrange("(o r) (a c) -> o r a c", o=NCORES + 1, a=2)
        for a in range(2):
            def qgather(e, a=a):
                src = kv4[1:NCORES + 1, R_QAT:R_QAT + 512, a, bass.ds(P.regs["p128"], 128)]
                dst = self.qown[l].rearrange("r (o a c) -> o r a c", o=8, a=2)[:, :, a, :]
                return e.dma_start(out=dst, in_=src)
            P.dma("sp", qgather, r=kvk, w=[f"qown_{a}"])
        S1b, S2b, O1b, O2b, R1b, R2b = (0, 1, 6), (2, 3, 7), 4, 5, 6, 7
        racc = [self.psb(f"racc{i}", [128, 2, 512], F32) for i in range(2)]
        tiles = []
        for h in range(4):
            for b in range(4):
                ntile = 32 * b + 32
                for kt in range(ntile):
                    tiles.append(dict(h=h, b=b, kt=kt, ntile=ntile))

        def head_loads(h):
            hb = 0
            for o in range(8):
                r0 = (o + 1) * R_TOT + R_KAT + h * 128
                P.dma("sp", lambda e, o=o, r0=r0: e.dma_start(out=KT[:, o, :], in_=kv[r0:r0 + 128, :]), r=kvk, w=[f"KT{o}"])
                r1 = (o + 1) * R_TOT + R_VA + h * 128
                P.dma("sp", lambda e, o=o, r1=r1: e.dma_start(out=VV[:, o, :], in_=kv[r1:r1 + 128, :]), r=kvk, w=[f"VV{o}"])
            P.dma("sp", lambda e, hb=hb, h=h: e.dma_start(out=Q1[hb][0:64, :], in_=self.qown[l][h * 128:h * 128 + 64, :]),
                  r=["qown_0", "qown_1"], w=[f"Q1d{hb}"])
            P.dma("sp", lambda e, hb=hb, h=h: e.dma_start(out=Q2[hb][64:128, :], in_=self.qown[l][h * 128 + 64:h * 128 + 128, :]),
                  r=["qown_0", "qown_1"], w=[f"Q2d{hb}"])

        def geom(T):
            b, kt = T["b"], T["kt"]
            u = kt - 32 * b
            n0 = 0 if u < 0 else 128 * (u // 8)
            return kt // 16, kt % 16, u, n0

        def stage1(i, T):
            h, b = T["h"], T["b"]
            hb = 0
            if T["kt"] == 0 and b == 0:
                head_loads(h)
            o, t, u, n0 = geom(T)
            sl = slice(n0, 512)
            qs = slice(b * 512 + n0, b * 512 + 512)
            s1, s2 = S1b[i % 3], S2b[i % 3]
            Eb = E[i % 4]
            ek = f"E{i % 4}"
            ksl = slice(t * 128, (t + 1) * 128)
            qk1 = [f"Q1d{hb}", f"Q1z{hb}"]
            qk2 = [f"Q2d{hb}", f"Q2z{hb}"]
            P.op("pe", lambda e: e.matmul(out=self.pbank[s1][:, sl], lhsT=KT[:, o, ksl], rhs=Q1[hb][:, qs], start=True, stop=True),
                 r=[f"KT{o}"] + qk1, w=[f"pb{s1}"])
            P.op("pe", lambda e: e.matmul(out=self.pbank[s2][:, sl], lhsT=KT[:, o, ksl], rhs=Q2[hb][:, qs], start=True, stop=True),
                 r=[f"KT{o}"] + qk2, w=[f"pb{s2}"])
            P.op("act", lambda e: e.activation(out=Eb[:, 0, sl], in_=self.pbank[s1][:, sl], func=AF.Exp, scale=0.125),
                 r=[f"pb{s1}"], w=[ek + "a"])
            P.op("act", lambda e: e.activation(out=Eb[:, 1, sl], in_=self.pbank[s2][:, sl], func=AF.Exp, scale=0.125),
                 r=[f"pb{s2}"], w=[ek + "b"])
            if u >= 0:
                v = u % 8
                P.op("pool", lambda e: e.tensor_tensor(out=Eb[:, :, n0:n0 + 128], in0=Eb[:, :, n0:n0 + 128],
                                                       in1=bc_free(self.maskA[:, v, :], 1, 2), op=ALU.mult),
                     r=[ek + "a", ek + "b", "maskA"], w=[ek + "a", ek + "b"])

        def stage2(i, T):
            h, b, kt, ntile = T["h"], T["b"], T["kt"], T["ntile"]
            o, t, u, n0 = geom(T)
            sl = slice(n0, 512)
            Eb = E[i % 4]
            ek = f"E{i % 4}"
            ksl = slice(t * 128, (t + 1) * 128)
            first, last = (kt == 0), (kt == ntile - 1)
            P.op("pe", lambda e: e.matmul(out=self.pbank[O1b][:, sl], lhsT=VV[:, o, ksl], rhs=Eb[:, 0, sl], start=first, stop=last),
                 r=[f"VV{o}", ek + "a"], w=[f"pb{O1b}"])
            P.op("pe", lambda e: e.matmul(out=self.pbank[O2b][:, sl], lhsT=VV[:, o, ksl], rhs=Eb[:, 1, sl], start=first, stop=last),
                 r=[f"VV{o}", ek + "b"], w=[f"pb{O2b}"])
            ab = (h * 4 + b) % 2
            ac = racc[ab]
            if first:
                P.op("dve", lambda e: e.memset(ac[:, 0, :], 0.0), w=[f"racc{ab}a"])
                P.op("dve", lambda e: e.memset(ac[:, 1, 0:256], 0.0), w=[f"racc{ab}b"])
                P.op("pool", lambda e: e.memset(ac[:, 1, 256:512], 0.0), w=[f"racc{ab}c"])
            P.op("dve", lambda e: e.tensor_tensor(out=ac[:, 0, sl], in0=ac[:, 0, sl], in1=Eb[:, 0, sl], op=ALU.add),
                 r=[ek + "a", f"racc{ab}a"], w=[f"racc{ab}a"])
            if n0 < 256:
                sla = slice(n0, 256)
                P.op("dve", lambda e: e.tensor_tensor(out=ac[:, 1, sla], in0=ac[:, 1, sla], in1=Eb[:, 1, sla], op=ALU.add),
                     r=[ek + "b", f"racc{ab}b"], w=[f"racc{ab}b"])
            slb = slice(max(n0, 256), 512)
            P.op("pool", lambda e: e.tensor_tensor(out=ac[:, 1, slb], in0=ac[:, 1, slb], in1=Eb[:, 1, slb], op=ALU.add),
                 r=[ek + "b", f"racc{ab}c"], w=[f"racc{ab}c"])
            if last:
                finalize(h, b)

        fsum = self.psb("fsum", [128, 2, 512], F32)

        def finalize(h, b):
            f0, f1, f2, f3, f4 = [x[:, :] for x in fin]
            ab = (h * 4 + b) % 2
            ac = racc[ab]
            P.op("dve", lambda e: e.tensor_copy(out=fsum[:, 0, :], in_=self.pbank[O1b][:, :]), r=[f"pb{O1b}"], w=["fs0"])
            P.op("dve", lambda e: e.tensor_copy(out=fsum[:, 1, :], in_=self.pbank[O2b][:, :]), r=[f"pb{O2b}"], w=["fs2"])
            P.op("pe", lambda e: e.matmul(out=self.pbank[R1b][:, :], lhsT=self.onesf[:, :], rhs=ac[:, 0, :], start=True, stop=True),
                 r=["onesf", f"racc{ab}a"], w=[f"pb{R1b}"])
            P.op("pe", lambda e: e.matmul(out=self.pbank[R2b][:, :], lhsT=self.onesf[:, :], rhs=ac[:, 1, :], start=True, stop=True),
                 r=["onesf", f"racc{ab}b", f"racc{ab}c"], w=[f"pb{R2b}"])
            P.op("dve", lambda e: e.reciprocal(out=f0, in_=self.pbank[R1b][:, :]), r=[f"pb{R1b}"], w=["f0"])
            P.op("dve", lambda e: e.tensor_tensor(out=f1, in0=fsum[:, 0, :], in1=f0, op=ALU.mult), r=["fs0", "f0"], w=["f1"])
            P.op("dve", lambda e: e.reciprocal(out=f0, in_=self.pbank[R2b][:, :]), r=[f"pb{R2b}", "f0"], w=["f0"])
            P.op("dve", lambda e: e.tensor_tensor(out=f3, in0=fsum[:, 1, :], in1=f0, op=ALU.mult), r=["fs2", "f0"], w=["f3"])
            P.op("dve", lambda e: e.scalar_tensor_tensor(out=f4, in0=f3, scalar=neglam, in1=f1, op0=ALU.mult, op1=ALU.add),
                 r=["f3", "f1", "neglam"], w=["f4"])
            P.op("pool", lambda e: e.tensor_tensor(out=sqb[:, :], in0=f4, in1=f4, op=ALU.mult), r=["f4"], w=["sqb"])
            P.op("pe", lambda e: e.matmul(out=self.pbank[R1b][:, :], lhsT=self.ones[:, :], rhs=sqb[:, :], start=True, stop=True),
                 r=["ones", "sqb"], w=[f"pb{R1b}"])
            P.op("dve", lambda e: e.tensor_scalar(out=f0, in0=self.pbank[R1b][:, :], scalar1=1.0 / 128.0, scalar2=EPS,
                                                  op0=ALU.mult, op1=ALU.add), r=[f"pb{R1b}"], w=["f0"])
            P.op("act", lambda e: e.activation(out=f0, in_=f0, func=AF.Sqrt), r=["f0"], w=["f0"])
            P.op("dve", lambda e: e.reciprocal(out=f0, in_=f0), r=["f0"], w=["f0"])
            P.op("dve", lambda e: e.tensor_tensor(out=f1, in0=f4, in1=f0, op=ALU.mult), r=["f4", "f0"], w=["f1"])
            ys = yst[(h * 4 + b) % 2]
            ysk = f"yst{(h * 4 + b) % 2}"
            P.op("dve", lambda e: e.tensor_scalar(out=ys[:, :], in0=f1, scalar1=gcol[:, 0:1], scalar2=None, op0=ALU.mult),
                 r=["f1", "gcol"], w=[ysk])
            P.dma("sp", lambda e: e.dma_start(out=self.contrib2[l][h * 128:(h + 1) * 128, b * 512:(b + 1) * 512], in_=ys[:, :]),
                  r=[ysk], w=[f"ctb2_{h}_{b}"])

        n = len(tiles)
        SK = 2
        done2 = 0
        for step in range(n + SK):
            newhead = step < n and tiles[step]["kt"] == 0 and tiles[step]["b"] == 0
            if newhead:
                while done2 < step:
                    stage2(done2, tiles[done2])
                    done2 += 1
            if step < n:
                stage1(step, tiles[step])
            if step >= SK and done2 <= step - SK:
                stage2(done2, tiles[done2])
                done2 += 1
        while done2 < n:
            stage2(done2, tiles[done2])
            done2 += 1
        if self.dbg and "ya2" in self.dbg:
            P.dma("sp", lambda e: e.dma_start(out=self.dbg_out["ya2"][:, :], in_=self.contrib2[l][:, :]),
                  r=[f"ctb2_{h}_{b}" for h in range(4) for b in range(4)], w=["dbgya"])

    def phaseC(self, l):
        with contextlib.ExitStack() as ph:
            self.ph = ph
            self._phaseC(l)
            self.P.barrier()

    def _phaseC(self, l):
        P = self.P
        sb = self.psb
        kv = self.kvall[l]
        kvk = [f"kvall{l}"]
        wcz = sb("wcz", [128, 8, 512], BF16)
        self.load_w(wcz[:, :, :], "wcz", self.w_in[l][:, C_CZ:C_CZ + 512], 512)
        self.allgather2(l)
        KW = [sb(f"KW{i}", [128, 2, 2048], BF16) for i in range(2)]
        VW = [sb(f"VW{i}", [128, 2, 16, 128], BF16) for i in range(2)]
        QA = [sb(f"QCa{i}", [128, 2048], BF16) for i in range(2)]
        QB = [sb(f"QCb{i}", [128, 2048], BF16) for i in range(2)]
        E = [sb(f"EC{i}", [128, 512], BF16) for i in range(4)]
        fin = [sb(f"finC{i}", [128, 512], F32) for i in range(3)]
        czs = sb("czs", [128, 512], F32)
        for i in range(2):
            P.op("dve", lambda e, i=i: e.memset(QA[i][64:128, :], 0.0), w=[f"QAz{i}"])
            P.op("dve", lambda e, i=i: e.memset(QB[i][0:64, :], 0.0), w=[f"QBz{i}"])
        def cgather(e):
            kv3 = kv.rearrange("(o r) c -> o r c", o=NCORES + 1)[:, R_KCT:R_TOT, :]
            return e.dma_start(out=self.cown[l][:, :, :], in_=kv3[bass.ds(P.regs["p1"], 2), :, :])
        P.dma("sp", cgather, r=kvk, w=["cown"])
        co = self.cown[l]
        it = 0
        for pr in range(4):
            pb = pr % 2
            for hf in range(2):
                P.dma("sp", lambda e, pb=pb, hf=hf, pr=pr: e.dma_start(out=KW[pb][:, hf, :], in_=co[hf, pr * 128:(pr + 1) * 128, :]),
                      r=["cown"], w=[f"KW{pb}_{hf}"])

                def vload(e, pb=pb, hf=hf, pr=pr):
                    src = co[hf, 512:1024, :].rearrange("(k a) c -> k (a c)", k=128)
                    src = src.rearrange("k (t c) -> k t c", t=16)[:, :, pr * 128:(pr + 1) * 128]
                    return e.dma_start(out=VW[pb][:, hf, :, :], in_=src)
                P.dma("sp", vload, r=["cown"], w=[f"VW{pb}_{hf}"])
            P.op("dve", lambda e, pb=pb, pr=pr, QCT=self.QCT: e.tensor_copy(out=QA[pb][0:64, :], in_=QCT[0:64, pr, :]),
                 r=[f"QCT{t}" for t in range(NT)], w=[f"QAd{pb}"])
            P.op("dve", lambda e, pb=pb, pr=pr, QCT=self.QCT: e.tensor_copy(out=QB[pb][64:128, :], in_=QCT[64:128, pr, :]),
                 r=[f"QCT{t}" for t in range(NT)], w=[f"QBd{pb}"])
            kq = [[f"QAd{pb}", f"QAz{pb}"], [f"QBd{pb}", f"QBz{pb}"]]
            Qp = [QA[pb], QB[pb]]
            for qg in range(4):
                kts = list(range(4 * qg, 4 * qg + 20))
                units = []
                for idx, kt in enumerate(kts):
                    hf, t = kt // 16, kt % 16
                    j0 = max(0, kt - (16 + 4 * qg))
                    j1 = min(3, kt - 4 * qg)
                    D0 = 16 + 4 * qg + j0 - kt
                    nj = j1 - j0 + 1
                    for hh in range(2):
                        units.append(dict(hf=hf, t=t, D0=D0, nj=nj, hh=hh, first=(idx == 0), last=(idx == len(kts) - 1),
                                          sl=slice(j0 * 128, (j1 + 1) * 128),
                                          qs=slice(qg * 512 + j0 * 128, qg * 512 + (j1 + 1) * 128)))

                def cst1(U, itn, pb=pb, Qp=Qp, kq=kq):
                    sbk = itn % 4
                    Eb = E[itn % 4]
                    ek = f"EC{itn % 4}"
                    hf, t, sl, qs, hh, D0, nj = U["hf"], U["t"], U["sl"], U["qs"], U["hh"], U["D0"], U["nj"]
                    P.op("pe", lambda e: e.matmul(out=self.pbank[sbk][:, sl], lhsT=KW[pb][:, hf, t * 128:(t + 1) * 128], rhs=Qp[hh][:, qs],
                                                  start=True, stop=True), r=[f"KW{pb}_{hf}"] + kq[hh], w=[f"pb{sbk}"])
                    P.op("act", lambda e: e.activation(out=Eb[:, sl], in_=self.pbank[sbk][:, sl], func=AF.Exp, scale=0.125),
                         r=[f"pb{sbk}"], w=[ek])
                    P.op("dve", lambda e: e.tensor_tensor(out=Eb[:, sl].rearrange("p (j q) -> p j q", j=nj),
                                                          in0=Eb[:, sl].rearrange("p (j q) -> p j q", j=nj),
                                                          in1=self.maskC[:, D0:D0 + nj, :], op=ALU.mult), r=[ek, "maskC"], w=[ek])

                def cst2(U, itn, pb=pb):
                    Eb = E[itn % 4]
                    ek = f"EC{itn % 4}"
                    hf, t, sl, hh, first, last = U["hf"], U["t"], U["sl"], U["hh"], U["first"], U["last"]
                    ob, rb = 4 + 2 * hh, 5 + 2 * hh
                    P.op("pe", lambda e: e.matmul(out=self.pbank[ob][:, sl], lhsT=VW[pb][:, hf, t, :], rhs=Eb[:, sl], start=first, stop=last,
                                                  skip_group_check=True), r=[f"VW{pb}_{hf}", ek], w=[f"pb{ob}"])
                    lh = self.vhalo if hf == 0 else self.ones
                    P.op("pe", lambda e: e.matmul(out=self.pbank[rb][:, sl], lhsT=lh[:, :], rhs=Eb[:, sl], start=first, stop=last,
                                                  skip_group_check=True), r=["vhalo", "ones", ek], w=[f"pb{rb}"])
                nu = len(units)
                SK = 2
                for step in range(nu + SK):
                    if step < nu:
                        cst1(units[step], it + step)
                    if step >= SK:
                        cst2(units[step - SK], it + step - SK)
                it += nu
                f0, f1, f2 = [x[:, :] for x in fin]
                P.op("dve", lambda e: e.reciprocal(out=fin[0][0:64, :], in_=self.pbank[5][0:64, :]), r=["pb5"], w=["fc0a"])
                P.op("dve", lambda e: e.reciprocal(out=fin[0][64:128, :], in_=self.pbank[7][64:128, :]), r=["pb7"], w=["fc0b"])
                P.op("dve", lambda e: e.tensor_tensor(out=fin[1][0:64, :], in0=self.pbank[4][0:64, :], in1=fin[0][0:64, :], op=ALU.mult),
                     r=["pb4", "fc0a"], w=["fc1a"])
                P.op("dve", lambda e: e.tensor_tensor(out=fin[1][64:128, :], in0=self.pbank[6][64:128, :], in1=fin[0][64:128, :], op=ALU.mult),
                     r=["pb6", "fc0b"], w=["fc1b"])
                sbk = it % 4
                it += 1
                gs = slice(qg * 512, (qg + 1) * 512)
                for k in range(8):
                    P.op("pe", lambda e, k=k, pr=pr, gs=gs, sbk=sbk: e.matmul(out=self.pbank[sbk][:, :], lhsT=wcz[:, k, pr * 128:(pr + 1) * 128],
                                                                           rhs=self.hT[:, k, gs], start=(k == 0), stop=(k == 7)),
                         r=["wcz"] + self.k_hT, w=[f"pb{sbk}"])
                P.op("act", lambda e, sbk=sbk: e.activation(out=czs[:, :], in_=self.pbank[sbk][:, :], func=AF.Silu), r=[f"pb{sbk}"], w=["czs"])
                P.op("dve", lambda e, pr=pr, gs=gs: e.tensor_tensor(out=self.ycT[:, pr, gs], in0=f1, in1=czs[:, :], op=ALU.mult),
                     r=["fc1a", "fc1b", "czs"], w=[f"ycT{pr}_{qg}"])
        if self.dbg and "ycT" in self.dbg:
            P.dma("sp", lambda e: e.dma_start(out=self.dbg_out["ycT"][:, :, :], in_=self.ycT[:, :, :]),
                  r=[f"ycT{pr}_{qg}" for pr in range(4) for qg in range(4)], w=["dbgyc"])

    def phase4(self, l):
        with contextlib.ExitStack() as ph:
            self.ph = ph
            self._phase4(l)
            self.P.barrier()

    def _phase4(self, l):
        P = self.P
        sb = self.psb
        last = (l == 1) and self.final_norm
        x_src = self.x_in if l == 0 else self.x1
        wG = sb("wG", [128, 8, 3072], BF16)
        wAZ = sb("wAZ", [128, 8, 512], BF16)
        wBR = sb("wBR", [128, 12, 1024], BF16)
        wO = sb("wO", [128, 8, 1024], BF16)
        self.load_w(wAZ[:, :, :], "wAZ", self.w_in[l][:, C_AZ:C_AZ + 512], 512)
        for n in range(3):
            self.load_w(wBR[:, 4 * n:4 * n + 4, :], f"wBR{n}", self.w_branch[l][n], 1024)
        for n in range(3):
            self.load_w(wG[:, :, n * 1024:(n + 1) * 1024], f"wG{n}", self.w_in[l][:, C_GL + n * 1024:C_GL + (n + 1) * 1024], 1024)
        self.load_w(wO[:, :, :], "wO", self.w_out[l], 1024)
        yaraw = [sb(f"yaraw{i}", [128, 4, 512], BF16) for i in range(1)] * 2
        yag = yaraw
        azs = [sb(f"azs{i}", [128, 512], BF16) for i in range(2)]
        sg = [sb(f"sg{i}", [128, 512], F32) for i in range(3)]
        mm = sg
        mT = [sb(f"mT{i}", [128, 8, 512], BF16) for i in range(1)] * 2
        xr_ = [sb(f"xres{i}", [128, D], F32) for i in range(1)] * 2
        xo = xr_
        if last:
            fg = sb("fg", [128, D], F32)
            P.dma("sp", lambda e: e.dma_start(out=fg[:, :], in_=self.final_g.partition_broadcast(128)), w=["fg"])
            junk = self.zeros
            ss = [sb(f"ss4{i}", [128, 1], F32) for i in range(2)]
        ya = self.yaall[l]
        for tb in range(2):
            def ygather(e, tb=tb):
                src = ya.rearrange("(w r) c -> w r c", w=8)[:, :, tb * 128:tb * 128 + 7 * 256 + 128][:, :, bass.ds(P.regs["p256"], 128)]
                dst = self.yaown[l].rearrange("r (tb w c) -> w r tb c", tb=2, w=8)[:, :, tb, :]
                return e.dma_start(out=dst, in_=src)
            P.dma("sp", ygather, r=[f"yaall{l}"], w=[f"yaown_{tb}"])
        cnt = 0
        for q4 in range(4):
            gb = q4 % 2
            gs = slice(q4 * 512, (q4 + 1) * 512)
            for tt in range(4):
                t = q4 * 4 + tt
                P.dma("sp", lambda e, gb=gb, tt=tt, t=t: e.dma_start(
                    out=yaraw[gb][:, :, tt * 128:(tt + 1) * 128],
                    in_=self.yaown[l][:, t * 128:(t + 1) * 128].rearrange("(h p) c -> p h c", h=4)),
                    r=["yaown_0", "yaown_1"], w=[f"yab_{h}" for h in range(4)])
            kyr = []
            for h in range(4):
                i2 = cnt % 2
                cnt += 1
                bk = i2
                for k in range(8):
                    P.op("pe", lambda e, k=k, h=h, bk=bk, gs=gs: e.matmul(out=self.pbank[bk][:, :], lhsT=wAZ[:, k, h * 128:(h + 1) * 128],
                                                                         rhs=self.hT[:, k, gs], start=(k == 0), stop=(k == 7)),
                         r=["wAZ"] + self.k_hT, w=[f"pb{bk}"])
                P.op("act", lambda e, bk=bk, i2=i2: e.activation(out=azs[i2][:, :], in_=self.pbank[bk][:, :], func=AF.Silu),
                     r=[f"pb{bk}"], w=[f"azs{i2}"])
                P.op("pool", lambda e, h=h, i2=i2, gb=gb: e.tensor_tensor(out=yag[gb][:, h, :], in0=yaraw[gb][:, h, :], in1=azs[i2][:, :], op=ALU.mult),
                     r=kyr + [f"azs{i2}", f"yab_{h}"], w=[f"yab_{h}"])
            kyg = [f"yab_{h}" for h in range(4)]
            ysrc = [(lambda cc, gb=gb: yag[gb][:, cc, :], kyg),
                    (lambda cc, gs=gs: self.ybT[:, cc, gs], [f"ybT{t}" for t in range(NT)]),
                    (lambda cc, gs=gs: self.ycT[:, cc, gs], [f"ycT{pr}_{qg}" for pr in range(4) for qg in range(4)])]
            for dc in range(8):
                for n in range(3):
                    getter, keys = ysrc[n]
                    for cc in range(4):
                        P.op("pe", lambda e, n=n, cc=cc, dc=dc, getter=getter: e.matmul(
                            out=self.pbank[2 + n][:, :], lhsT=wBR[:, 4 * n + cc, dc * 128:(dc + 1) * 128], rhs=getter(cc),
                            start=(cc == 0), stop=(cc == 3)), r=[f"wBR{n}"] + keys, w=[f"pb{2 + n}"])
                for n in range(3):
                    for k in range(8):
                        P.op("pe", lambda e, n=n, k=k, dc=dc, gs=gs: e.matmul(
                            out=self.pbank[5 + n][:, :], lhsT=wG[:, k, n * 1024 + dc * 128:n * 1024 + (dc + 1) * 128],
                            rhs=self.hT[:, k, gs], start=(k == 0), stop=(k == 7)), r=[f"wG{n}"] + self.k_hT, w=[f"pb{5 + n}"])
                for n in range(3):
                    P.op("act", lambda e, n=n: e.activation(out=sg[n][:, :], in_=self.pbank[5 + n][:, :], func=AF.Sigmoid),
                         r=[f"pb{5 + n}"], w=[f"sg{n}"])
                    P.op("dve", lambda e, n=n: e.tensor_tensor(out=mm[n][:, :], in0=self.pbank[2 + n][:, :], in1=sg[n][:, :], op=ALU.mult),
                         r=[f"pb{2 + n}", f"sg{n}"], w=[f"sg{n}"])
                P.op("pool", lambda e: e.tensor_tensor(out=mm[0][:, :], in0=mm[0][:, :], in1=mm[1][:, :], op=ALU.add), r=["sg0", "sg1"], w=["sg0"])
                P.op("pool", lambda e, gb=gb, dc=dc: e.tensor_tensor(out=mT[gb][:, dc, :], in0=mm[0][:, :], in1=mm[2][:, :], op=ALU.add),
                     r=["sg0", "sg2"], w=[f"mT_{dc}"])
            kmT = [f"mT_{dc}" for dc in range(8)]
            for tt in range(4):
                t = q4 * 4 + tt
                xb = 0
                P.dma("sp", lambda e, t=t, xb=xb: e.dma_start(out=xr_[xb][:, :], in_=x_src[t * 128:(t + 1) * 128, :]),
                      r=(["x1w"] if l == 1 else []), w=[f"xres{xb}"])
                for eg in range(2):
                    bk = eg
                    for dc in range(8):
                        P.op("pe", lambda e, dc=dc, eg=eg, bk=bk, gb=gb, tt=tt: e.matmul(
                            out=self.pbank[bk][:, :], lhsT=mT[gb][:, dc, tt * 128:(tt + 1) * 128], rhs=wO[:, dc, eg * 512:(eg + 1) * 512],
                            start=(dc == 0), stop=(dc == 7)), r=["wO"] + kmT, w=[f"pb{bk}"])
                    P.op("dve", lambda e, eg=eg, bk=bk, xb=xb: e.tensor_tensor(out=xo[xb][:, eg * 512:(eg + 1) * 512], in0=self.pbank[bk][:, :],
                                                                              in1=xr_[xb][:, eg * 512:(eg + 1) * 512], op=ALU.add),
                         r=[f"pb{bk}", f"xres{xb}"], w=[f"xres{xb}"])
                kxo = [f"xres{xb}"]
                if not last:
                    dst = self.x1 if (l == 0 and len(self.layers) > 1) else self.out
                    P.dma("sp", lambda e, t=t, xb=xb, dst=dst: e.dma_start(out=dst[t * 128:(t + 1) * 128, :], in_=xo[xb][:, :]),
                          r=kxo, w=["x1w"])
                else:
                    P.op("act", lambda e, xb=xb: e.activation(out=junk[:, :], in_=xo[xb][:, :], func=AF.Square, accum_out=ss[xb][:, :]),
                         r=kxo, w=["zeros", f"ss4{xb}"])
                    P.op("dve", lambda e, xb=xb: e.tensor_scalar(out=ss[xb][:, :], in0=ss[xb][:, :], scalar1=1.0 / D, scalar2=EPS,
                                                                op0=ALU.mult, op1=ALU.add), r=[f"ss4{xb}"], w=[f"ss4{xb}"])
                    P.op("act", lambda e, xb=xb: e.activation(out=ss[xb][:, :], in_=ss[xb][:, :], func=AF.Sqrt), r=[f"ss4{xb}"], w=[f"ss4{xb}"])
                    P.op("dve", lambda e, xb=xb: e.reciprocal(out=ss[xb][:, :], in_=ss[xb][:, :]), r=[f"ss4{xb}"], w=[f"ss4{xb}"])
                    P.op("dve", lambda e, xb=xb: e.scalar_tensor_tensor(out=xo[xb][:, :], in0=xo[xb][:, :], scalar=ss[xb][:, 0:1], in1=fg[:, :],
                                                                       op0=ALU.mult, op1=ALU.mult), r=kxo + [f"ss4{xb}", "fg"], w=kxo)
                    P.dma("sp", lambda e, t=t, xb=xb: e.dma_start(out=self.out[t * 128:(t + 1) * 128, :], in_=xo[xb][:, :]),
                          r=kxo, w=[f"outw{t}"])

    def phase1(self, l):
        with contextlib.ExitStack() as ph:
            self.ph = ph
            self._phase1(l)
            self.P.barrier()

    def _phase1(self, l):
        P = self.P
        nc = self.nc
        sb = self.psb
        self.xt = [sb(f"xt{i}", [128, D], F32) for i in range(2)]
        self.junk = sb("junk", [128, D], BF16)
        self.ss = [sb(f"ss{i}", [128, 1], F32) for i in range(2)]
        self.rstd = [sb(f"rstd{i}", [128, 1], F32) for i in range(2)]
        self.xs = [sb(f"xs{i}", [128, D], BF16) for i in range(2)]
        self.wP1 = sb("wP1", [128, 8, 3072], BF16)
        self.qk = [sb(f"qk{i}", [128, 4, 8, 64], BF16) for i in range(2)]
        self.xr = [sb(f"xr{i}", [128, 32, 16], F32) for i in range(2)]
        self.rt = [sb(f"rt{i}", [128, 32, 16], F32) for i in range(2)]
        self.ru = [sb(f"ru{i}", [128, 32, 16], F32) for i in range(2)]
        self.vsb = [sb(f"vsb{i}", [128, 2, 512], BF16) for i in range(2)]
        self.tsb = [sb(f"tsb{i}", [128, 16, 128], BF16) for i in range(2)]
        x_src = self.x_in if l == 0 else self.x1
        for gi, c in enumerate((C_AQ, C_AK, C_CQ, C_CK, C_AV, C_CV)):
            self.load_w(self.wP1[:, :, gi * 512:(gi + 1) * 512], f"wP1_{gi}", self.w_in[l][:, c:c + 512], 512)
        wk = [f"wP1_{g}" for g in range(6)]
        P.dma("sp", lambda e: e.dma_start(out=self.gT[:, :], in_=self.norm_g[l].rearrange("(k p) -> p k", p=128),
                                          allow_slow_non_contiguous=True), w=["gT"])
        ctb = self.contrib[l]
        self.k_contrib = [f"ctb_{n}{t}" for n in ("va", "vc", "qa", "ka", "kc") for t in range(NT)] + self.k_kvpad[l]
        self.k_hT = [f"hT{t}" for t in range(NT)]
        for t in range(NT):
            b = t % 2
            xt, ss, rstd, xs, qk, xr, rt, ru, vsb, tsb = (self.xt[b], self.ss[b], self.rstd[b], self.xs[b], self.qk[b],
                                                          self.xr[b], self.rt[b], self.ru[b], self.vsb[b], self.tsb[b])
            kb = f"_{b}"
            P.dma("sp", lambda e, t=t, xt=xt: e.dma_start(out=xt[:, :], in_=x_src[t * 128:(t + 1) * 128, :]),
                  w=["xt" + kb])
            P.op("act", lambda e, xt=xt, ss=ss, junk=self.junk: e.activation(out=junk[:, :], in_=xt[:, :], func=AF.Square,
                                                           accum_out=ss[:, :]),
                 r=["xt" + kb], w=["junk", "ss" + kb])
            P.op("dve", lambda e, ss=ss, rstd=rstd: e.tensor_scalar(out=rstd[:, :], in0=ss[:, :], scalar1=1.0 / D,
                                                                  scalar2=EPS, op0=ALU.mult, op1=ALU.add),
                 r=["ss" + kb], w=["rstd" + kb])
            P.op("act", lambda e, rstd=rstd: e.activation(out=rstd[:, :], in_=rstd[:, :], func=AF.Sqrt),
                 r=["rstd" + kb], w=["rstd" + kb])
            P.op("dve", lambda e, rstd=rstd: e.reciprocal(out=rstd[:, :], in_=rstd[:, :]),
                 r=["rstd" + kb], w=["rstd" + kb])
            P.op("dve", lambda e, xt=xt, xs=xs, rstd=rstd: e.tensor_scalar(out=xs[:, :], in0=xt[:, :],
                                                                         scalar1=rstd[:, 0:1], scalar2=None,
                                                                         op0=ALU.mult),
                 r=["xt" + kb, "rstd" + kb], w=["xs" + kb])
            pT = self.pbf(0)
            for k in range(8):
                P.op("pe", lambda e, k=k, xs=xs: e.transpose(out=pT[:, k * 128:(k + 1) * 128],
                                                             in_=xs[:, k * 128:(k + 1) * 128], identity=self.ident[:, :]),
                     r=["xs" + kb, "ident"], w=["pb0"])
            hTt = self.hT[:, :, t * 128:(t + 1) * 128]
            P.op("dve", lambda e, hTt=hTt: e.tensor_tensor(out=hTt, in0=pT.rearrange("p (k t) -> p k t", k=8),
                                                          in1=bc_free(self.gT[:, :], 2, 128), op=ALU.mult),
                 r=["pb0", "gT"], w=[f"hT{t}"])
            for g in range(6):
                bank = 1 + g
                for k in range(8):
                    P.op("pe", lambda e, g=g, k=k, bank=bank, t=t, wP1=self.wP1: e.matmul(
                        out=self.pbank[bank][:, :], lhsT=self.hT[:, k, t * 128:(t + 1) * 128],
                        rhs=wP1[:, k, g * 512:(g + 1) * 512], start=(k == 0), stop=(k == 7)),
                         r=[f"hT{t}", wk[g]], w=[f"pb{bank}"])
            for g in range(4):
                src = self.pbank[1 + g][:, :].rearrange("p (h d) -> p h d", d=64)
                eng = "act" if g % 2 == 0 else "dve"
                if eng == "act":
                    P.op("act", lambda e, g=g, src=src, qk=qk: e.copy(out=qk[:, g, :, 16:64], in_=src[:, :, 16:64]),
                         r=[f"pb{1 + g}"], w=[f"qkrest{g}" + kb])
                else:
                    P.op("dve", lambda e, g=g, src=src, qk=qk: e.tensor_copy(out=qk[:, g, :, 16:64], in_=src[:, :, 16:64]),
                         r=[f"pb{1 + g}"], w=[f"qkrest{g}" + kb])
                if eng == "act":
                    P.op("act", lambda e, g=g, src=src, xr=xr: e.copy(out=xr[:, g * 8:(g + 1) * 8, :], in_=src[:, :, 0:16]),
                         r=[f"pb{1 + g}"], w=[f"xr{g}" + kb])
                else:
                    P.op("dve", lambda e, g=g, src=src, xr=xr: e.tensor_copy(out=xr[:, g * 8:(g + 1) * 8, :], in_=src[:, :, 0:16]),
                         r=[f"pb{1 + g}"], w=[f"xr{g}" + kb])
            xrk = [f"xr{g}" + kb for g in range(4)]
            csb = bc_free(self.cs[:, t, :], 1, 32)
            P.op("pool", lambda e, xr=xr, rt=rt, csb=csb: e.tensor_tensor(out=rt[:, :, :], in0=xr[:, :, :], in1=csb, op=ALU.mult),
                 r=xrk + self.k_rope, w=["rt" + kb])
            P.op("pool", lambda e, xr=xr, ru=ru, t=t: e.tensor_tensor(out=ru[:, :, 0:8], in0=xr[:, :, 8:16],
                                                                    in1=bc_free(self.sn[:, t, 0:8], 1, 32), op=ALU.mult),
                 r=xrk + self.k_rope, w=["ru_lo" + kb])
            P.op("pool", lambda e, xr=xr, ru=ru, t=t: e.tensor_tensor(out=ru[:, :, 8:16], in0=xr[:, :, 0:8],
                                                                    in1=bc_free(self.sn[:, t, 8:16], 1, 32), op=ALU.mult),
                 r=xrk + self.k_rope, w=["ru_hi" + kb])
            P.op("pool", lambda e, qk=qk, rt=rt, ru=ru: e.tensor_tensor(
                out=qk[:, :, :, 0:16], in0=rt[:, :, :].rearrange("p (g h) d -> p g h d", g=4),
                in1=ru[:, :, :].rearrange("p (g h) d -> p g h d", g=4), op=ALU.add),
                 r=["rt" + kb, "ru_lo" + kb, "ru_hi" + kb], w=["qkrot" + kb])
            P.op("act", lambda e, vsb=vsb: e.copy(out=vsb[:, 0, :], in_=self.pbank[5][:, :]), r=["pb5"], w=["vsb0" + kb])
            P.op("act", lambda e, vsb=vsb: e.copy(out=vsb[:, 1, :], in_=self.pbank[6][:, :]), r=["pb6"], w=["vsb1" + kb])
            P.dma("sp", lambda e, vsb=vsb, t=t: e.dma_start(
                out=ctb[R_VA:R_VA + 512, t * 128:(t + 1) * 128].rearrange("(h k) e -> k h e", h=4),
                in_=vsb[:, 0, :].rearrange("k (h e) -> k h e", h=4)), r=["vsb0" + kb], w=[f"ctb_va{t}"])
            vc_dst = ctb[R_VC:R_VC + 512, :].rearrange("(k a) c -> k (a c)", k=128)[:, t * 512:(t + 1) * 512]
            P.dma("sp", lambda e, vsb=vsb, vc_dst=vc_dst: e.dma_start(out=vc_dst, in_=vsb[:, 1, :]),
                  r=["vsb1" + kb], w=[f"ctb_vc{t}"])
            qkf = qk[:, :, :, :].rearrange("p g h d -> p (g h d)")
            qkk = [f"qkrest{g}" + kb for g in range(4)] + ["qkrot" + kb]
            p7 = self.pbf(7)
            for rnd in range(2):
                for j in range(8):
                    blk = rnd * 8 + j
                    P.op("pe", lambda e, blk=blk, j=j, qkf=qkf: e.transpose(
                        out=p7[:, j * 128:(j + 1) * 128], in_=qkf[:, blk * 128:(blk + 1) * 128],
                        identity=self.ident[:, :]), r=qkk + ["ident"], w=["pb7"])
                eng = "act" if rnd == 0 else "dve"
                if eng == "act":
                    P.op("act", lambda e, rnd=rnd, tsb=tsb: e.copy(out=tsb[:, rnd * 8:(rnd + 1) * 8, :],
                                                                 in_=p7.rearrange("p (j t) -> p j t", j=8)),
                         r=["pb7"], w=[f"tsb{rnd}" + kb])
                else:
                    P.op("dve", lambda e, rnd=rnd, tsb=tsb: e.tensor_copy(out=tsb[:, rnd * 8:(rnd + 1) * 8, :],
                                                                        in_=p7.rearrange("p (j t) -> p j t", j=8)),
                         r=["pb7"], w=[f"tsb{rnd}" + kb])
            P.dma("sp", lambda e, tsb=tsb, t=t: e.dma_start(
                out=ctb[R_QAT:R_QAT + 512, t * 128:(t + 1) * 128].rearrange("(h p) c -> p h c", h=4),
                in_=tsb[:, 0:4, :]), r=["tsb0" + kb], w=[f"ctb_qa{t}"])
            P.dma("sp", lambda e, tsb=tsb, t=t: e.dma_start(
                out=ctb[R_KAT:R_KAT + 512, t * 128:(t + 1) * 128].rearrange("(h p) c -> p h c", h=4),
                in_=tsb[:, 4:8, :]), r=["tsb0" + kb], w=[f"ctb_ka{t}"])
            P.op("pool", lambda e, tsb=tsb, t=t, QCT=self.QCT: e.tensor_copy(out=QCT[:, :, t * 128:(t + 1) * 128], in_=tsb[:, 8:12, :]),
                 r=["tsb1" + kb], w=[f"QCT{t}"])
            P.dma("sp", lambda e, tsb=tsb, t=t: e.dma_start(
                out=ctb[R_KCT:R_KCT + 512, t * 128:(t + 1) * 128].rearrange("(h p) c -> p h c", h=4),
                in_=tsb[:, 12:16, :]), r=["tsb1" + kb], w=[f"ctb_kc{t}"])
        if self.dbg and "contrib" in self.dbg:
            P.dma("sp", lambda e: e.dma_start(out=self.dbg_out["contrib"][:, :], in_=ctb[:, :]), r=self.k_contrib, w=["dbgc"])
        if self.dbg and "hT" in self.dbg:
            P.dma("sp", lambda e: e.dma_start(out=self.dbg_out["hT"][:, :, :], in_=self.hT[:, :, :]),
                  r=[f"hT{t}" for t in range(NT)], w=["dbgh"])


def make_in_maps(inputs):
    x = np.ascontiguousarray(inputs["x"][0])
    pos = np.ascontiguousarray(inputs["positions"][0]).astype(np.int32)
    common = {k: np.ascontiguousarray(inputs[k]) for k in
              ("norm_g", "w_in", "lam_q1", "lam_k1", "lam_q2", "lam_k2", "subln_g", "sgu_ln_g", "sgu_ln_b",
               "sgu_w", "sgu_b", "w_branch", "w_out", "final_g")}
    maps = []
    for c in range(NCORES):
        m = dict(common)
        m["x"] = x[c * SOWN:(c + 1) * SOWN]
        m["pos"] = np.ascontiguousarray(pos[c * SOWN:(c + 1) * SOWN].reshape(NT, 128).T)
        m["cidf"] = np.full((128, 1), float(c), np.float32)
        maps.append(m)
    return maps


def kernel(**inputs):
    b = Builder()
    nc = b.build()
    res = run_bass_kernel_spmd(nc, make_in_maps(inputs), core_ids=list(range(NCORES)))
    out = np.concatenate([np.asarray(r["out"]) for r in res.results], axis=0)
    return out.reshape(1, S, D).astype(np.float32)
```

```python
import contextlib
import math
import numpy as np
import ml_dtypes
import concourse.bass as bass
import concourse.mybir as mybir
from concourse.bass_utils import run_bass_kernel_spmd

F32 = mybir.dt.float32
BF16 = mybir.dt.bfloat16
I32 = mybir.dt.int32
AF = mybir.ActivationFunctionType
ALU = mybir.AluOpType
AX = mybir.AxisListType

NCORES = 8
D = 1024
S = 16384
SOWN = S // NCORES
NT = SOWN // 128
INC = 8704
EPS = 1e-6
C_AQ, C_AK, C_AV, C_AZ, C_BU, C_BV, C_BZ, C_CQ, C_CK, C_CV, C_CZ, C_GL = (
    0, 512, 1024, 1536, 2048, 2560, 3072, 3584, 4096, 4608, 5120, 5632)
R_KAT, R_QAT, R_VA, R_KCT, R_VC, R_TOT = 0, 512, 1024, 1536, 2048, 2560


class Prog:
    CE = ("pe", "act", "dve", "pool")

    def __init__(self):
        self.ops = []
        self.lastw = {}
        self.rd_c = {}
        self.rd_d = {}
        self.sp_init = None
        self.regs = {}
        self.base = set()
        self.base_pending = set()
        self.since_dma = []
        self.last_c = {}

    def barrier(self):
        self.base = set(self.last_c.values()) | set(self.since_dma)
        self.since_dma = []
        self.base_pending = {"pe", "act", "dve", "pool", "sp"}

    def _add(self, eng, fn, r, w, kind):
        w = list(w) + [k for k in r if k.startswith("pb") and k not in w]
        idx = len(self.ops)
        deps = set()
        for k in r:
            if k in self.lastw:
                deps.add(self.lastw[k])
        for k in w:
            if k in self.lastw:
                deps.add(self.lastw[k])
            deps.update(self.rd_c.get(k, {}).values())
            deps.update(self.rd_d.get(k, ()))
        if eng in self.base_pending:
            deps |= self.base
            self.base_pending.discard(eng)
        if kind == "c":
            self.last_c[eng] = idx
        else:
            self.since_dma.append(idx)
        import sys as _s
        self.ops.append(dict(eng=eng, fn=fn, deps=deps, kind=kind, line=_s._getframe(2).f_lineno))
        for k in r:
            if kind == "c":
                self.rd_c.setdefault(k, {})[eng] = idx
            else:
                self.rd_d.setdefault(k, []).append(idx)
        for k in w:
            self.lastw[k] = idx
            self.rd_c[k] = {}
            self.rd_d[k] = []
        return idx

    def op(self, eng, fn, r=(), w=()):
        return self._add(eng, fn, r, w, "c")

    def dma(self, q, fn, r=(), w=(), inc=16, semq=None):
        i = self._add(q, fn, r, w, "d")
        self.ops[i]["inc"] = inc
        self.ops[i]["semq"] = semq or q
        return i

    def emit(self, nc, st, ndma={"sp": 24, "pool": 8, "act": 4, "cc": 2}, limit=None):
        ops = self.ops if limit is None else self.ops[:limit]
        need = [False] * len(ops)
        for o in ops:
            for d in o["deps"]:
                dd = ops[d]
                if dd["kind"] == "c" and not (dd["eng"] == "pe" and o["eng"] == "pe" and o["kind"] == "c"):
                    need[d] = True
        csem = {e: st.enter_context(nc.semaphore("c_" + e)) for e in self.CE}
        dsem = {q: [st.enter_context(nc.semaphore(f"d_{q}{i}")) for i in range(n)] for q, n in ndma.items()}
        cnt = {e: 0 for e in self.CE}
        dcnt = {q: 0 for q in ndma}
        dval = {q: [0] * n for q, n in ndma.items()}
        for i, o in enumerate(ops):
            if o["kind"] == "c":
                if need[i]:
                    cnt[o["eng"]] += 1
                    o["sig"] = cnt[o["eng"]]
            else:
                q = o["semq"]
                slot = dcnt[q] % ndma[q]
                dcnt[q] += 1
                o["slot"] = slot
                o["prev"] = dval[q][slot]
                dval[q][slot] += o["inc"]
                o["val"] = dval[q][slot]
        per = {e: [] for e in ("pe", "act", "dve", "pool", "sp")}
        for i, o in enumerate(ops):
            per[o["eng"]].append(i)
        block = st.enter_context(nc.Block())

        def run(ename, e):
            waited = {}
            if ename == "sp" and self.sp_init is not None:
                self.sp_init(e)
            for i in per[ename]:
                o = ops[i]
                needw = {}
                for d in o["deps"]:
                    dd = ops[d]
                    if dd["kind"] == "c":
                        if dd["eng"] == "pe" and ename == "pe" and o["kind"] == "c":
                            continue
                        key = ("c", dd["eng"]); val = dd["sig"]
                    else:
                        key = ("d", dd["semq"], dd["slot"]); val = dd["val"]
                    if needw.get(key, 0) < val:
                        needw[key] = val
                if o["kind"] == "d" and o["prev"] > 0:
                    key = ("d", o["semq"], o["slot"])
                    if needw.get(key, 0) < o["prev"]:
                        needw[key] = o["prev"]
                for key, val in needw.items():
                    if waited.get(key, 0) < val:
                        sem = csem[key[1]] if key[0] == "c" else dsem[key[1]][key[2]]
                        e.wait_ge(sem, val)
                        waited[key] = val
                ins = o["fn"](e)
                if o["kind"] == "c":
                    if need[i]:
                        ins.then_inc(csem[ename], 1)
                else:
                    ins.then_inc(dsem[o["semq"]][o["slot"]], o["inc"])
            for qn in ([ename] + (["cc"] if ename == "pool" else [])):
                if qn in ndma:
                    for slot, v in enumerate(dval[qn]):
                        if v > 0 and waited.get(("d", qn, slot), 0) < v:
                            e.wait_ge(dsem[qn][slot], v)

        @block.tensor
        def _(e):
            run("pe", e)

        @block.scalar
        def _(e):
            run("act", e)

        @block.vector
        def _(e):
            run("dve", e)

        @block.gpsimd
        def _(e):
            run("pool", e)

        @block.sync
        def _(e):
            run("sp", e)


def bc_free(ap, pos, n):
    a = ap.unsqueeze(pos)
    shp = list(a.shape)
    shp[pos] = n
    return a.to_broadcast(shp)


class Builder:
    def __init__(self, layers=(0, 1), final_norm=True, dbg=None, stop_after=None):
        self.layers = layers
        self.final_norm = final_norm
        self.dbg = dbg
        self.stop_after = stop_after
        self.limit = None
        self.nc = bass.Bass("TRN2", target_bir_lowering=False)
        self.P = Prog()
        self.uid = 0

    def din(self, name, shape, dt):
        return self.nc.dram_tensor(name, list(shape), dt, kind="ExternalInput").ap()

    def dout(self, name, shape, dt):
        return self.nc.dram_tensor(name, list(shape), dt, kind="ExternalOutput").ap()

    def dint(self, name, shape, dt):
        return self.nc.dram_tensor(name, list(shape), dt).ap()

    def sb(self, name, shape, dt):
        return self.st.enter_context(self.nc.sbuf_tensor("sb_" + name, list(shape), dt))

    def psb(self, name, shape, dt):
        self.uid += 1
        return self.ph.enter_context(self.nc.sbuf_tensor(f"ph{self.uid}_" + name, list(shape), dt))

    def ps(self, name, shape, dt):
        return self.st.enter_context(self.nc.psum_tensor("ps_" + name, list(shape), dt))

    def build(self):
        nc = self.nc
        with contextlib.ExitStack() as st:
            self.st = st
            self.declare_io()
            self.alloc()

            def sp_init(e):
                pid = e.partition_id()
                self.P.regs["p128"] = e.snap(pid * 128)
                self.P.regs["p256"] = e.snap(pid * 256)
                self.P.regs["prow"] = e.snap(pid * R_TOT)
                self.P.regs["p1"] = e.snap(pid * 1)
            self.P.sp_init = sp_init
            self.setup_consts()
            for l in self.layers:
                self.layer(l)
            self.P.emit(nc, st, limit=self.limit)
        return nc

    def declare_io(self):
        L = 2
        self.x_in = self.din("x", [SOWN, D], F32)
        self.pos_in = self.din("pos", [128, NT], I32)
        self.cidf_in = self.din("cidf", [128, 1], F32)
        self.norm_g = self.din("norm_g", [L, D], F32)
        self.w_in = self.din("w_in", [L, D, INC], F32)
        self.lam4 = [self.din(n, [L, 64], F32) for n in ("lam_q1", "lam_k1", "lam_q2", "lam_k2")]
        self.subln_g = self.din("subln_g", [L, 128], F32)
        self.sgu_ln_g = self.din("sgu_ln_g", [L, 512], F32)
        self.sgu_ln_b = self.din("sgu_ln_b", [L, 512], F32)
        self.sgu_w = self.din("sgu_w", [L, 4, 128, 128], F32)
        self.sgu_b = self.din("sgu_b", [L, 4, 128], F32)
        self.w_branch = self.din("w_branch", [L, 3, 512, D], F32)
        self.w_out = self.din("w_out", [L, D, D], F32)
        self.final_g = self.din("final_g", [D], F32)
        self.out = self.dout("out", [SOWN, D], F32)
        self.contrib = [self.dint(f"contrib{l}", [R_TOT, 2048], BF16) for l in range(2)]
        self.kvall = [self.dint(f"kvall{l}", [(NCORES + 1) * R_TOT, 2048], BF16) for l in range(2)]
        self.contrib2 = [self.dint(f"contribb{l}", [512, 2048], BF16) for l in range(2)]
        self.yaall = [self.dint(f"yaall{l}", [NCORES * 512, 2048], BF16) for l in range(2)]
        self.x1 = self.dint("x1buf", [SOWN, D], F32)
        self.qown = [self.dint(f"qown{l}", [512, 2048], BF16) for l in range(2)]
        self.cown = [self.dint(f"cown{l}", [2, 1024, 2048], BF16) for l in range(2)]
        self.yaown = [self.dint(f"yaown{l}", [512, 2048], BF16) for l in range(2)]
        if self.dbg:
            self.dbg_out = {k: self.dout("dbg_" + k, shp, dt) for k, (shp, dt) in self.dbg.items()}

    def alloc(self):
        sb, ps = self.sb, self.ps
        self.ident = sb("ident", [128, 128], BF16)
        self.ones = sb("ones", [128, 128], BF16)
        self.onesf = sb("onesf", [128, 128], F32)
        self.zeros = sb("zeros", [128, 1024], BF16)
        self.invf = sb("invf", [128, 8], F32)
        self.posi = sb("posi", [128, NT], I32)
        self.posf = sb("posf", [128, NT], F32)
        self.cs = sb("cs", [128, NT, 16], F32)
        self.sn = sb("sn", [128, NT, 16], F32)
        self.ang = sb("ang", [128, NT, 16], F32)
        self.cidf = sb("cidf", [128, 1], F32)
        self.rk = sb("rk", [128, NT, 8], F32)
        self.rki = sb("rki", [128, NT, 8], I32)
        self.rm = sb("rm", [128, NT, 8], F32)
        self.gT = sb("gT", [128, 8], F32)
        self.hT = sb("hT", [128, 8, SOWN], BF16)
        self.ybT = sb("ybT", [128, 4, SOWN], BF16)
        self.ycT = sb("ycT", [128, 4, SOWN], BF16)
        self.maskC = sb("maskC", [128, 17, 128], BF16)
        self.relb = sb("relb", [128, 128], F32)
        self.maskA = sb("maskA", [128, 8, 128], BF16)
        self.vhalo = sb("vhalo", [128, 128], BF16)
        self.small = sb("small", [128, 64], F32)
        self.pbank = [ps(f"pb{i}", [128, 512], F32) for i in range(8)]

    def pbf(self, i):
        return self.pbank[i][:, :].bitcast(BF16)

    def setup_consts(self):
        P = self.P
        ident, ones, onesf, zeros = self.ident, self.ones, self.onesf, self.zeros
        P.op("pool", lambda e: e.memset(ones[:, :], 1.0), w=["ones"])
        P.op("pool", lambda e: e.memset(onesf[:, :], 1.0), w=["onesf"])
        P.op("pool", lambda e: e.memset(zeros[:, :], 0.0), w=["zeros"])
        P.op("pool", lambda e: e.affine_select(out=ident[:, :], in_=ones[:, :], pattern=[[1, 128]],
                                               compare_op=ALU.is_equal, fill=0.0, base=0,
                                               channel_multiplier=-1), r=["ones"], w=["ident"])
        for i in range(8):
            v = float(500000.0 ** (-i / 8.0))
            P.op("pool", lambda e, i=i, v=v: e.memset(self.invf[:, i:i + 1], v), w=[f"invf{i}"])
        invk = [f"invf{i}" for i in range(8)]
        P.dma("sp", lambda e: e.dma_start(out=self.posi[:, :], in_=self.pos_in[:, :]), w=["posi"])
        P.dma("sp", lambda e: e.dma_start(out=self.cidf[:, :], in_=self.cidf_in[:, :]), w=["cidf"])
        P.op("dve", lambda e: e.tensor_copy(out=self.posf[:, :], in_=self.posi[:, :]), r=["posi"], w=["posf"])
        for t in range(NT):
            P.op("dve", lambda e, t=t: e.tensor_scalar(out=self.ang[:, t, 0:8], in0=self.invf[:, :],
                                                      scalar1=self.posf[:, t:t + 1], scalar2=None,
                                                      op0=ALU.mult),
                 r=invk + ["posf"], w=[f"ang{t}"])
        angk = [f"ang{t}" for t in range(NT)]
        a8 = self.ang[:, :, 0:8]
        PI = math.pi
        C1 = 6.28125
        C2 = 2.0 * math.pi - C1

        def sin_of(dst, shift, tag):
            r_ = self.ang[:, :, 8:16]
            kf = self.rk[:, :, :]
            ki = self.rki[:, :, :]
            m = self.rm[:, :, :]
            P.op("dve", lambda e: e.tensor_scalar(out=r_, in0=a8, scalar1=shift, scalar2=None, op0=ALU.add),
                 r=angk + ["angs"], w=["angs"])
            P.op("dve", lambda e: e.tensor_scalar(out=kf, in0=r_, scalar1=1.0 / (2.0 * PI), scalar2=None, op0=ALU.mult),
                 r=["angs"], w=["rk"])
            P.op("dve", lambda e: e.tensor_copy(out=ki, in_=kf), r=["rk"], w=["rki"])
            P.op("dve", lambda e: e.tensor_copy(out=kf, in_=ki), r=["rki"], w=["rk"])
            P.op("dve", lambda e: e.scalar_tensor_tensor(out=r_, in0=kf, scalar=-C1, in1=r_, op0=ALU.mult, op1=ALU.add),
                 r=["rk", "angs"], w=["angs"])
            P.op("dve", lambda e: e.scalar_tensor_tensor(out=r_, in0=kf, scalar=-C2, in1=r_, op0=ALU.mult, op1=ALU.add),
                 r=["rk", "angs"], w=["angs"])
            P.op("dve", lambda e: e.tensor_scalar(out=m, in0=r_, scalar1=PI, scalar2=-2.0 * PI, op0=ALU.is_gt, op1=ALU.mult),
                 r=["angs"], w=["rm"])
            P.op("dve", lambda e: e.tensor_tensor(out=r_, in0=r_, in1=m, op=ALU.add), r=["angs", "rm"], w=["angs"])
            P.op("dve", lambda e: e.tensor_scalar(out=m, in0=r_, scalar1=-PI, scalar2=2.0 * PI, op0=ALU.is_lt, op1=ALU.mult),
                 r=["angs"], w=["rm"])
            P.op("dve", lambda e: e.tensor_tensor(out=r_, in0=r_, in1=m, op=ALU.add), r=["angs", "rm"], w=["angs"])
            P.op("dve", lambda e: e.tensor_scalar(out=r_, in0=r_, scalar1=-PI, scalar2=PI, op0=ALU.max, op1=ALU.min),
                 r=["angs"], w=["angs"])
            P.op("act", lambda e: e.activation(out=dst, in_=r_, func=AF.Sin), r=["angs"], w=[tag])

        sin_of(self.sn[:, :, 8:16], 0.0, "sn_hi")
        P.op("dve", lambda e: e.tensor_scalar(out=self.sn[:, :, 0:8], in0=self.sn[:, :, 8:16], scalar1=-1.0,
                                              scalar2=None, op0=ALU.mult), r=["sn_hi"], w=["sn_lo"])
        sin_of(self.cs[:, :, 0:8], 0.5 * PI, "cs_lo")
        P.op("dve", lambda e: e.tensor_copy(out=self.cs[:, :, 8:16], in_=self.cs[:, :, 0:8]), r=["cs_lo"], w=["cs_hi"])
        self.k_rope = ["sn_hi", "sn_lo", "cs_lo", "cs_hi"]
        self.setup_masks()

    def setup_masks(self):
        P = self.P
        relb, small = self.relb, self.small
        reli = self.sb("reli", [128, 128], I32)
        tmpa = self.sb("mtmpa", [128, 128], F32)
        tmpb = self.sb("mtmpb", [128, 128], F32)
        tmpc = self.sb("mtmpc", [128, 128], F32)
        tmpi = self.sb("mtmpi", [128, 128], I32)
        P.op("pool", lambda e: e.iota(out=reli[:, :], pattern=[[1, 128]], base=0, channel_multiplier=-1), w=["reli"])
        P.op("dve", lambda e: e.tensor_copy(out=relb[:, :], in_=reli[:, :]), r=["reli"], w=["relb"])
        def band(Dl, hi):
            c0 = float(128 * Dl)
            P.op("dve", lambda e: e.tensor_scalar(out=tmpa[:, :], in0=relb[:, :], scalar1=c0, scalar2=0.0,
                                                  op0=ALU.add, op1=ALU.is_ge), r=["relb"], w=["mta"])
            P.op("dve", lambda e: e.tensor_scalar(out=tmpb[:, :], in0=relb[:, :], scalar1=c0, scalar2=float(hi),
                                                  op0=ALU.add, op1=ALU.is_le), r=["relb"], w=["mtb"])
            P.op("dve", lambda e: e.tensor_tensor(out=tmpa[:, :], in0=tmpa[:, :], in1=tmpb[:, :], op=ALU.mult),
                 r=["mta", "mtb"], w=["mta"])

        def lattice(Dl, dil):
            c0 = float(128 * Dl + 4096)
            P.op("dve", lambda e: e.tensor_scalar(out=tmpc[:, :], in0=relb[:, :], scalar1=c0, scalar2=None,
                                                  op0=ALU.add), r=["relb"], w=["mtc"])
            P.op("dve", lambda e: e.tensor_copy(out=tmpi[:, :], in_=tmpc[:, :]), r=["mtc"], w=["mti"])
            P.op("dve", lambda e: e.tensor_single_scalar(out=tmpi[:, :], in_=tmpi[:, :], scalar=dil - 1, op=ALU.bitwise_and),
                 r=["mti"], w=["mti"])
            P.op("dve", lambda e: e.tensor_copy(out=tmpc[:, :], in_=tmpi[:, :]), r=["mti"], w=["mtc"])
            P.op("dve", lambda e: e.tensor_scalar(out=tmpb[:, :], in0=tmpc[:, :], scalar1=0.0, scalar2=None, op0=ALU.is_equal),
                 r=["mtc"], w=["mtb"])
            P.op("dve", lambda e: e.tensor_tensor(out=tmpa[:, :], in0=tmpa[:, :], in1=tmpb[:, :], op=ALU.mult),
                 r=["mta", "mtb"], w=["mta"])

        acc = self.sb("macc", [128, 128], F32)

        def one_mask(Dl):
            band(Dl, 128)
            P.op("dve", lambda e: e.tensor_copy(out=acc[:, :], in_=tmpa[:, :]), r=["mta"], w=["macc"])
            band(Dl, 512)
            lattice(Dl, 4)
            P.op("dve", lambda e: e.tensor_tensor(out=acc[:, :], in0=acc[:, :], in1=tmpa[:, :], op=ALU.add),
                 r=["mta", "macc"], w=["macc"])
            band(Dl, 2048)
            lattice(Dl, 16)
            P.op("dve", lambda e: e.tensor_tensor(out=self.maskC[:, Dl, :], in0=acc[:, :], in1=tmpa[:, :], op=ALU.add),
                 r=["mta", "macc"], w=["maskC"])

        for Dl in range(17):
            one_mask(Dl)
        for v in range(8):
            P.op("dve", lambda e, v=v: e.tensor_scalar(out=small[:, v:v + 1], in0=self.cidf[:, 0:1], scalar1=float(-v), scalar2=128.0,
                                                      op0=ALU.add, op1=ALU.mult), r=["cidf"], w=[f"cshift{v}"])
            P.op("dve", lambda e, v=v: e.tensor_scalar(out=self.maskA[:, v, :], in0=relb[:, :], scalar1=small[:, v:v + 1], scalar2=0.0,
                                                      op0=ALU.add, op1=ALU.is_ge), r=["relb", f"cshift{v}"], w=["maskA"])
        P.op("dve", lambda e: e.tensor_scalar(out=small[:, 8:9], in0=self.cidf[:, 0:1], scalar1=1.0, scalar2=None, op0=ALU.min),
             r=["cidf"], w=["hv"])
        P.op("dve", lambda e: e.tensor_scalar(out=self.vhalo[:, :], in0=self.onesf[:, :], scalar1=small[:, 8:9], scalar2=None,
                                              op0=ALU.mult), r=["onesf", "hv"], w=["vhalo"])
        for l in range(2):
            for i in range(8):
                r0 = R_KCT + i * 128
                P.dma("sp", lambda e, l=l, r0=r0: e.dma_start(out=self.kvall[l][r0:r0 + 128, :].rearrange("p (a c) -> p a c", a=2),
                                                             in_=bc_free(self.zeros[:, :], 1, 2)),
                      r=["zeros"], w=[f"kvpad{l}_{i}"])
        self.k_kvpad = [[f"kvpad{l}_{i}" for i in range(8)] for l in range(2)]

    def load_w(self, dst, key, src, ncols):
        P = self.P
        nk = dst.shape[1]
        for k in range(nk):
            for c0 in range(0, ncols, 1024):
                c1 = min(ncols, c0 + 1024)
                P.dma("pool", lambda e, k=k, c0=c0, c1=c1: e.dma_start(out=dst[:, k, c0:c1],
                                                                     in_=src[k * 128:(k + 1) * 128, c0:c1]),
                      w=[key])

    def layer(self, l):
        self.lam_init = 0.8 - 0.6 * math.exp(-0.3 * l)
        with contextlib.ExitStack() as lst:
            self.QCT = lst.enter_context(self.nc.sbuf_tensor(f"sb_QCT{l}", [128, 4, SOWN], BF16))
            done = self.layer_front(l)
        if done:
            self.phase4(l)

    def layer_front(self, l):
        self.phase1(l)
        if self.stop_after == "p1":
            return False
        self.phaseB(l)
        if self.stop_after == "pB":
            return False
        self.phaseA(l)
        if self.stop_after == "pA":
            return False
        self.phaseC(l)
        if self.stop_after == "pC":
            return False
        return True

    def allgather1(self, l):
        P = self.P
        kv = self.kvall[l]
        P.dma("pool", lambda e: e.collective_compute(
            "AllGather", ALU.bypass, replica_groups=[list(range(NCORES))],
            ins=[self.contrib[l].opt()], outs=[kv[R_TOT:, :].opt()]),
            r=self.k_contrib, w=[f"kvall{l}"], inc=1, semq="cc")

    def allgather2(self, l):
        P = self.P
        P.dma("pool", lambda e: e.collective_compute(
            "AllGather", ALU.bypass, replica_groups=[list(range(NCORES))],
            ins=[self.contrib2[l].opt()], outs=[self.yaall[l].opt()]),
            r=[f"ctb2_{h}_{b}" for h in range(4) for b in range(4)], w=[f"yaall{l}"], inc=1, semq="cc")

    def gelu_from_psum(self, src, dst, tmp1, tmp2, rkeys, wkeys, tag):
        P = self.P
        P.op("act", lambda e: e.activation(out=tmp1, in_=src, func=AF.Square), r=rkeys, w=[tag + "_t1"])
        P.op("dve", lambda e: e.tensor_scalar(out=tmp1, in0=tmp1, scalar1=0.044715, scalar2=1.0, op0=ALU.mult, op1=ALU.add),
             r=[tag + "_t1"], w=[tag + "_t1"])
        P.op("dve", lambda e: e.tensor_tensor(out=tmp2, in0=tmp1, in1=src, op=ALU.mult), r=[tag + "_t1"] + rkeys, w=[tag + "_t2"])
        P.op("act", lambda e: e.activation(out=tmp2, in_=tmp2, func=AF.Sigmoid, scale=1.5957691216057308),
             r=[tag + "_t2"], w=[tag + "_t2"])
        P.op("dve", lambda e: e.tensor_tensor(out=dst, in0=tmp2, in1=src, op=ALU.mult), r=[tag + "_t2"] + rkeys, w=wkeys)

    def phaseB(self, l):
        with contextlib.ExitStack() as ph:
            self.ph = ph
            self._phaseB(l)
            self.P.barrier()

    def _phaseB(self, l):
        P = self.P
        sb = self.psb
        wB = sb("wB", [128, 8, 1536], BF16)
        for gi, c in enumerate((C_BU, C_BV, C_BZ)):
            self.load_w(wB[:, :, gi * 512:(gi + 1) * 512], f"wB_{gi}", self.w_in[l][:, c:c + 512], 512)
        wcf = sb("wcf", [128, 4, 128], F32)
        wcb = sb("wcb", [128, 4, 128], BF16)
        wcT = sb("wcT", [128, 4, 128], BF16)
        sbb = sb("sbb", [128, 4, 128], F32)
        lng = sb("lng", [128, 512], F32)
        lnb = sb("lnb", [128, 512], F32)
        P.dma("sp", lambda e: e.dma_start(out=wcf[:, :, :], in_=self.sgu_w[l].rearrange("g t s -> t g s")), w=["wcf"])
        P.dma("sp", lambda e: e.dma_start(out=sbb[:, :, :].rearrange("p g t -> p (g t)"),
                                          in_=self.sgu_b[l].rearrange("g t -> (g t)").partition_broadcast(128)), w=["sbb"])
        P.dma("sp", lambda e: e.dma_start(out=lng[:, :], in_=self.sgu_ln_g[l].partition_broadcast(128)), w=["lng"])
        P.dma("sp", lambda e: e.dma_start(out=lnb[:, :], in_=self.sgu_ln_b[l].partition_broadcast(128)), w=["lnb"])
        for g in range(4):
            P.op("pool", lambda e, g=g: e.affine_select(out=wcb[:, g, :], in_=wcf[:, g, :], pattern=[[-1, 128]],
                                                        compare_op=ALU.is_ge, fill=0.0, base=0, channel_multiplier=1),
                 r=["wcf"], w=[f"wcb{g}"])
        p0 = self.pbf(0)
        for g in range(4):
            P.op("pe", lambda e, g=g: e.transpose(out=p0[:, g * 128:(g + 1) * 128], in_=wcb[:, g, :], identity=self.ident[:, :]),
                 r=[f"wcb{g}", "ident"], w=["pb0"])
        P.op("dve", lambda e: e.tensor_copy(out=wcT[:, :, :].rearrange("p g t -> p (g t)"), in_=p0[:, 0:512]), r=["pb0"], w=["wcT"])
        self.allgather1(l)
        uz = [sb(f"uz{i}", [128, 4, 512], BF16) for i in range(2)]
        ug = [sb(f"ug{i}", [128, 512], F32) for i in range(2)]
        t1 = [sb(f"bt1{i}", [128, 512], F32) for i in range(2)]
        t2 = [sb(f"bt2{i}", [128, 512], F32) for i in range(2)]
        t1v = [sb(f"bt1v{i}", [128, 512], F32) for i in range(2)]
        t2v = [sb(f"bt2v{i}", [128, 512], F32) for i in range(2)]
        zs = [sb(f"zs{i}", [128, 512], F32) for i in range(2)]
        vg = [sb(f"vg{i}", [128, 512], F32) for i in range(2)]
        vb = [sb(f"vb{i}", [128, 512], BF16) for i in range(2)]
        st6 = [sb(f"st6{i}", [128, 6], F32) for i in range(2)]
        mv = [sb(f"mv{i}", [128, 2], F32) for i in range(2)]
        mt = [sb(f"mt{i}", [128, 4, 128], F32) for i in range(2)]
        cnt = 0
        for q4 in range(4):
            cols = slice(q4 * 512, (q4 + 1) * 512)
            uzb = uz[q4 % 2]
            kz = f"uz_{q4 % 2}"
            for j in range(4):
                pb_u, pb_z = 1 + (j % 2) * 2, 2 + (j % 2) * 2
                i2 = cnt % 2
                cnt += 1
                for k in range(8):
                    P.op("pe", lambda e, k=k, j=j, pb_u=pb_u, cols=cols: e.matmul(out=self.pbank[pb_u][:, :], lhsT=wB[:, k, j * 128:(j + 1) * 128],
                                                                      rhs=self.hT[:, k, cols], start=(k == 0), stop=(k == 7)),
                         r=["wB_0"] + self.k_hT, w=[f"pb{pb_u}"])
                for k in range(8):
                    P.op("pe", lambda e, k=k, j=j, pb_z=pb_z, cols=cols: e.matmul(out=self.pbank[pb_z][:, :],
                                                                      lhsT=wB[:, k, 1024 + j * 128:1024 + (j + 1) * 128],
                                                                      rhs=self.hT[:, k, cols], start=(k == 0), stop=(k == 7)),
                         r=["wB_2"] + self.k_hT, w=[f"pb{pb_z}"])
                self.gelu_from_psum(self.pbank[pb_u][:, :], ug[i2][:, :], t1[i2][:, :], t2[i2][:, :], [f"pb{pb_u}"], [f"ug{i2}"], f"gu{i2}")
                P.op("act", lambda e, pb_z=pb_z, i2=i2: e.activation(out=zs[i2][:, :], in_=self.pbank[pb_z][:, :], func=AF.Silu),
                     r=[f"pb{pb_z}"], w=[f"zs{i2}"])
                P.op("dve", lambda e, j=j, i2=i2, uzb=uzb: e.tensor_tensor(out=uzb[:, j, :], in0=ug[i2][:, :], in1=zs[i2][:, :], op=ALU.mult),
                     r=[f"ug{i2}", f"zs{i2}"], w=[kz + f"_{j}"])
            if self.dbg and "B_uz" in self.dbg and q4 == 0:
                P.dma("sp", lambda e, uzb=uzb: e.dma_start(out=self.dbg_out["B_uz"][:, :, :], in_=uzb[:, :, :]),
                      r=[kz + f"_{j}" for j in range(4)], w=["dbg_uz"])
            for tt in range(4):
                t = q4 * 4 + tt
                i2 = t % 2
                pbv, pbm = 5 + i2, 7
                for k in range(8):
                    P.op("pe", lambda e, k=k, t=t, pbv=pbv: e.matmul(out=self.pbank[pbv][:, :], lhsT=self.hT[:, k, t * 128:(t + 1) * 128],
                                                                    rhs=wB[:, k, 512:1024], start=(k == 0), stop=(k == 7)),
                         r=["wB_1"] + self.k_hT, w=[f"pb{pbv}"])
                self.gelu_from_psum(self.pbank[pbv][:, :], vg[i2][:, :], t1v[i2][:, :], t2v[i2][:, :], [f"pb{pbv}"], [f"vg{i2}"], f"gv{i2}")
                P.op("dve", lambda e, i2=i2: e.bn_stats(out=st6[i2][:, :], in_=vg[i2][:, :]), r=[f"vg{i2}"], w=[f"st6{i2}"])
                P.op("dve", lambda e, i2=i2: e.bn_aggr(out=mv[i2][:, :], in_=st6[i2][:, :]), r=[f"st6{i2}"], w=[f"mv{i2}"])
                P.op("dve", lambda e, i2=i2: e.tensor_scalar(out=mv[i2][:, 1:2], in0=mv[i2][:, 1:2], scalar1=EPS, scalar2=None, op0=ALU.add),
                     r=[f"mv{i2}"], w=[f"mv{i2}"])
                P.op("act", lambda e, i2=i2: e.activation(out=mv[i2][:, 1:2], in_=mv[i2][:, 1:2], func=AF.Sqrt), r=[f"mv{i2}"], w=[f"mv{i2}"])
                P.op("dve", lambda e, i2=i2: e.reciprocal(out=mv[i2][:, 1:2], in_=mv[i2][:, 1:2]), r=[f"mv{i2}"], w=[f"mv{i2}"])
                P.op("dve", lambda e, i2=i2: e.tensor_scalar(out=vg[i2][:, :], in0=vg[i2][:, :], scalar1=mv[i2][:, 0:1], scalar2=mv[i2][:, 1:2],
                                                            op0=ALU.subtract, op1=ALU.mult), r=[f"vg{i2}", f"mv{i2}"], w=[f"vg{i2}"])
                P.op("dve", lambda e, i2=i2: e.tensor_tensor(out=vg[i2][:, :], in0=vg[i2][:, :], in1=lng[:, :], op=ALU.mult),
                     r=[f"vg{i2}", "lng"], w=[f"vg{i2}"])
                P.op("dve", lambda e, i2=i2: e.tensor_tensor(out=vb[i2][:, :], in0=vg[i2][:, :], in1=lnb[:, :], op=ALU.add),
                     r=[f"vg{i2}", "lnb"], w=[f"vb{i2}"])
                for g in range(4):
                    P.op("pe", lambda e, g=g, i2=i2: e.matmul(out=self.pbank[pbm][:, g * 128:(g + 1) * 128], lhsT=vb[i2][:, g * 128:(g + 1) * 128],
                                                             rhs=wcT[:, g, :], start=True, stop=True),
                         r=[f"vb{i2}", "wcT"], w=[f"pb{pbm}"])
                P.op("dve", lambda e, i2=i2: e.tensor_tensor(out=mt[i2][:, :, :], in0=self.pbank[pbm][:, :].rearrange("p (g t) -> p g t", g=4),
                                                            in1=sbb[:, :, :], op=ALU.add), r=[f"pb{pbm}", "sbb"], w=[f"mt{i2}"])
                if self.dbg and "B_mt" in self.dbg and t == 0:
                    P.dma("sp", lambda e, i2=i2: e.dma_start(out=self.dbg_out["B_mt"][:, :, :], in_=mt[i2][:, :, :]), r=[f"mt{i2}"], w=["dbg_mt"])
                    P.dma("sp", lambda e, i2=i2: e.dma_start(out=self.dbg_out["B_vb"][:, :], in_=vb[i2][:, :]), r=[f"vb{i2}"], w=["dbg_vb"])
                    P.dma("sp", lambda e: e.dma_start(out=self.dbg_out["B_wcT"][:, :, :], in_=wcT[:, :, :]), r=["wcT"], w=["dbg_wcT"])
                P.op("dve", lambda e, i2=i2, t=t, tt=tt, uzb=uzb: e.tensor_tensor(out=self.ybT[:, :, t * 128:(t + 1) * 128], in0=mt[i2][:, :, :],
                                                                                  in1=uzb[:, :, tt * 128:(tt + 1) * 128], op=ALU.mult),
                     r=[f"mt{i2}"] + [kz + f"_{j}" for j in range(4)], w=[f"ybT{t}"])
        if self.dbg and "ybT" in self.dbg:
            P.dma("sp", lambda e: e.dma_start(out=self.dbg_out["ybT"][:, :, :], in_=self.ybT[:, :, :]),
                  r=[f"ybT{t}" for t in range(NT)], w=["dbgyb"])

    def phaseA(self, l):
        with contextlib.ExitStack() as ph:
            self.ph = ph
            self._phaseA(l)
            self.P.barrier()

    def _phaseA(self, l):
        P = self.P
        sb = self.psb
        kv = self.kvall[l]
        kvk = [f"kvall{l}"]
        KT = sb("KT", [128, 8, 2048], BF16)
        VV = sb("VV", [128, 8, 2048], BF16)
        Q1 = [sb(f"Q1p{i}", [128, 2048], BF16) for i in range(1)] * 2
        Q2 = [sb(f"Q2p{i}", [128, 2048], BF16) for i in range(1)] * 2
        E = [sb(f"E{i}", [128, 2, 512], BF16) for i in range(4)]
        fin = [sb(f"fin{i}", [128, 512], F32) for i in range(4)]
        fin = [fin[0], fin[1], fin[0], fin[2], fin[3]]
        sqb = sb("sqb", [128, 512], BF16)
        yst = [sb(f"yst{i}", [128, 512], BF16) for i in range(2)]
        lamv = sb("lamv", [128, 4, 64], F32)
        lamt = sb("lamt", [128, 2, 64], F32)
        lams = sb("lams", [128, 4], F32)
        gcol = sb("gcol", [128, 1], F32)
        for i in range(4):
            P.dma("sp", lambda e, i=i: e.dma_start(out=lamv[:, i, :], in_=self.lam4[i][l].partition_broadcast(128)), w=[f"lamv{i}"])
        P.dma("sp", lambda e: e.dma_start(out=gcol[:, :], in_=self.subln_g[l].rearrange("(e o) -> e o", o=1)), w=["gcol"])
        for m in range(2):
            P.op("dve", lambda e, m=m: e.tensor_tensor(out=lamt[:, m, :], in0=lamv[:, 2 * m, :], in1=lamv[:, 2 * m + 1, :], op=ALU.mult),
                 r=[f"lamv{2 * m}", f"lamv{2 * m + 1}"], w=[f"lamt{m}"])
            P.op("dve", lambda e, m=m: e.reduce_sum(out=lams[:, m:m + 1], in_=lamt[:, m, :], axis=AX.X), r=[f"lamt{m}"], w=[f"lams{m}"])
            P.op("act", lambda e, m=m: e.activation(out=lams[:, m:m + 1], in_=lams[:, m:m + 1], func=AF.Exp), r=[f"lams{m}"], w=[f"lams{m}"])
        P.op("dve", lambda e: e.tensor_tensor(out=lams[:, 2:3], in0=lams[:, 1:2], in1=lams[:, 0:1], op=ALU.subtract),
             r=["lams0", "lams1"], w=["neglam"])
        li = self.lam_init
        P.op("dve", lambda e: e.tensor_scalar(out=lams[:, 2:3], in0=lams[:, 2:3], scalar1=-li, scalar2=None, op0=ALU.add),
             r=["neglam"], w=["neglam"])
        P.op("dve", lambda e: e.tensor_scalar(out=gcol[:, :], in0=gcol[:, :], scalar1=1.0 - li, scalar2=None, op0=ALU.mult),
             r=["gcol"], w=["gcol"])
        neglam = lams[:, 2:3]
        for i in range(1):
            P.op("pool", lambda e, i=i: e.memset(Q1[i][64:128, :], 0.0), w=[f"Q1z{i}"])
            P.op("pool", lambda e, i=i: e.memset(Q2[i][0:64, :], 0.0), w=[f"Q2z{i}"])
        kv4 = kv.rearrange("(o r) (a c) -> o r a c", o=NCORES + 1, a=2)
        for a in range(2):
            def qgather(e, a=a):
                src = kv4[1:NCORES + 1, R_QAT:R_QAT + 512, a, bass.ds(P.regs["p128"], 128)]
                dst = self.qown[l].rearrange("r (o a c) -> o r a c", o=8, a=2)[:, :, a, :]
                return e.dma_start(out=dst, in_=src)
            P.dma("sp", qgather, r=kvk, w=[f"qown_{a}"])
        S1b, S2b, O1b, O2b, R1b, R2b = (0, 1, 6), (2, 3, 7), 4, 5, 6, 7
        racc = [self.psb(f"racc{i}", [128, 2, 512], F32) for i in range(2)]
        tiles = []
        for h in range(4):
            for b in range(4):
                ntile = 32 * b + 32
                for kt in range(ntile):
                    tiles.append(dict(h=h, b=b, kt=kt, ntile=ntile))

        def head_loads(h):
            hb = 0
            for o in range(8):
                r0 = (o + 1) * R_TOT + R_KAT + h * 128
                P.dma("sp", lambda e, o=o, r0=r0: e.dma_start(out=KT[:, o, :], in_=kv[r0:r0 + 128, :]), r=kvk, w=[f"KT{o}"])
                r1 = (o + 1) * R_TOT + R_VA + h * 128
                P.dma("sp", lambda e, o=o, r1=r1: e.dma_start(out=VV[:, o, :], in_=kv[r1:r1 + 128, :]), r=kvk, w=[f"VV{o}"])
            P.dma("sp", lambda e, hb=hb, h=h: e.dma_start(out=Q1[hb][0:64, :], in_=self.qown[l][h * 128:h * 128 + 64, :]),
                  r=["qown_0", "qown_1"], w=[f"Q1d{hb}"])
            P.dma("sp", lambda e, hb=hb, h=h: e.dma_start(out=Q2[hb][64:128, :], in_=self.qown[l][h * 128 + 64:h * 128 + 128, :]),
                  r=["qown_0", "qown_1"], w=[f"Q2d{hb}"])

        def geom(T):
            b, kt = T["b"], T["kt"]
            u = kt - 32 * b
            n0 = 0 if u < 0 else 128 * (u // 8)
            return kt // 16, kt % 16, u, n0

        def stage1(i, T):
            h, b = T["h"], T["b"]
            hb = 0
            if T["kt"] == 0 and b == 0:
                head_loads(h)
            o, t, u, n0 = geom(T)
            sl = slice(n0, 512)
            qs = slice(b * 512 + n0, b * 512 + 512)
            s1, s2 = S1b[i % 3], S2b[i % 3]
            Eb = E[i % 4]
            ek = f"E{i % 4}"
            ksl = slice(t * 128, (t + 1) * 128)
            qk1 = [f"Q1d{hb}", f"Q1z{hb}"]
            qk2 = [f"Q2d{hb}", f"Q2z{hb}"]
            P.op("pe", lambda e: e.matmul(out=self.pbank[s1][:, sl], lhsT=KT[:, o, ksl], rhs=Q1[hb][:, qs], start=True, stop=True),
                 r=[f"KT{o}"] + qk1, w=[f"pb{s1}"])
            P.op("pe", lambda e: e.matmul(out=self.pbank[s2][:, sl], lhsT=KT[:, o, ksl], rhs=Q2[hb][:, qs], start=True, stop=True),
                 r=[f"KT{o}"] + qk2, w=[f"pb{s2}"])
            P.op("act", lambda e: e.activation(out=Eb[:, 0, sl], in_=self.pbank[s1][:, sl], func=AF.Exp, scale=0.125),
                 r=[f"pb{s1}"], w=[ek + "a"])
            P.op("act", lambda e: e.activation(out=Eb[:, 1, sl], in_=self.pbank[s2][:, sl], func=AF.Exp, scale=0.125),
                 r=[f"pb{s2}"], w=[ek + "b"])
            if u >= 0:
                v = u % 8
                P.op("pool", lambda e: e.tensor_tensor(out=Eb[:, :, n0:n0 + 128], in0=Eb[:, :, n0:n0 + 128],
                                                       in1=bc_free(self.maskA[:, v, :], 1, 2), op=ALU.mult),
                     r=[ek + "a", ek + "b", "maskA"], w=[ek + "a", ek + "b"])

        def stage2(i, T):
            h, b, kt, ntile = T["h"], T["b"], T["kt"], T["ntile"]
            o, t, u, n0 = geom(T)
            sl = slice(n0, 512)
            Eb = E[i % 4]
            ek = f"E{i % 4}"
            ksl = slice(t * 128, (t + 1) * 128)
            first, last = (kt == 0), (kt == ntile - 1)
            P.op("pe", lambda e: e.matmul(out=self.pbank[O1b][:, sl], lhsT=VV[:, o, ksl], rhs=Eb[:, 0, sl], start=first, stop=last),
                 r=[f"VV{o}", ek + "a"], w=[f"pb{O1b}"])
            P.op("pe", lambda e: e.matmul(out=self.pbank[O2b][:, sl], lhsT=VV[:, o, ksl], rhs=Eb[:, 1, sl], start=first, stop=last),
                 r=[f"VV{o}", ek + "b"], w=[f"pb{O2b}"])
            if first:
                P.op("dve", lambda e: e.memset(racc[0][:, :, :], 0.0), w=["racc0"])
                P.op("pool", lambda e: e.memset(racc[1][:, :, :], 0.0), w=["racc1"])
            if i % 3 == 2 and u < 0:
                P.op("pool", lambda e: e.tensor_tensor(out=racc[1][:, :, sl], in0=racc[1][:, :, sl], in1=Eb[:, :, sl], op=ALU.add),
                     r=[ek + "a", ek + "b", "racc1"], w=["racc1"])
            else:
                P.op("dve", lambda e: e.tensor_tensor(out=racc[0][:, :, sl], in0=racc[0][:, :, sl], in1=Eb[:, :, sl], op=ALU.add),
                     r=[ek + "a", ek + "b", "racc0"], w=["racc0"])
            if last:
                finalize(h, b)

        fsum = self.psb("fsum", [128, 2, 512], F32)

        def finalize(h, b):
            f0, f1, f2, f3, f4 = [x[:, :] for x in fin]
            P.op("dve", lambda e: e.tensor_copy(out=fsum[:, 0, :], in_=self.pbank[O1b][:, :]), r=[f"pb{O1b}"], w=["fs0"])
            P.op("dve", lambda e: e.tensor_copy(out=fsum[:, 1, :], in_=self.pbank[O2b][:, :]), r=[f"pb{O2b}"], w=["fs2"])
            for m_, bank in ((0, R1b), (1, R2b)):
                P.op("pe", lambda e, m_=m_, bank=bank: e.matmul(out=self.pbank[bank][:, :], lhsT=self.onesf[:, :], rhs=racc[0][:, m_, :],
                                                              start=True, stop=False), r=["onesf", "racc0"], w=[f"pb{bank}"])
                P.op("pe", lambda e, m_=m_, bank=bank: e.matmul(out=self.pbank[bank][:, :], lhsT=self.onesf[:, :], rhs=racc[1][:, m_, :],
                                                              start=False, stop=True), r=["onesf", "racc1"], w=[f"pb{bank}"])
            P.op("dve", lambda e: e.reciprocal(out=f0, in_=self.pbank[R1b][:, :]), r=[f"pb{R1b}"], w=["f0"])
            P.op("dve", lambda e: e.tensor_tensor(out=f1, in0=fsum[:, 0, :], in1=f0, op=ALU.mult), r=["fs0", "f0"], w=["f1"])
            P.op("dve", lambda e: e.reciprocal(out=f0, in_=self.pbank[R2b][:, :]), r=[f"pb{R2b}", "f0"], w=["f0"])
            P.op("dve", lambda e: e.tensor_tensor(out=f3, in0=fsum[:, 1, :], in1=f0, op=ALU.mult), r=["fs2", "f0"], w=["f3"])
            P.op("dve", lambda e: e.scalar_tensor_tensor(out=f4, in0=f3, scalar=neglam, in1=f1, op0=ALU.mult, op1=ALU.add),
                 r=["f3", "f1", "neglam"], w=["f4"])
            P.op("pool", lambda e: e.tensor_tensor(out=sqb[:, :], in0=f4, in1=f4, op=ALU.mult), r=["f4"], w=["sqb"])
            P.op("pe", lambda e: e.matmul(out=self.pbank[R1b][:, :], lhsT=self.ones[:, :], rhs=sqb[:, :], start=True, stop=True),
                 r=["ones", "sqb"], w=[f"pb{R1b}"])
            P.op("dve", lambda e: e.tensor_scalar(out=f0, in0=self.pbank[R1b][:, :], scalar1=1.0 / 128.0, scalar2=EPS,
                                                  op0=ALU.mult, op1=ALU.add), r=[f"pb{R1b}"], w=["f0"])
            P.op("act", lambda e: e.activation(out=f0, in_=f0, func=AF.Sqrt), r=["f0"], w=["f0"])
            P.op("dve", lambda e: e.reciprocal(out=f0, in_=f0), r=["f0"], w=["f0"])
            P.op("dve", lambda e: e.tensor_tensor(out=f1, in0=f4, in1=f0, op=ALU.mult), r=["f4", "f0"], w=["f1"])
            ys = yst[(h * 4 + b) % 2]
            ysk = f"yst{(h * 4 + b) % 2}"
            P.op("dve", lambda e: e.tensor_scalar(out=ys[:, :], in0=f1, scalar1=gcol[:, 0:1], scalar2=None, op0=ALU.mult),
                 r=["f1", "gcol"], w=[ysk])
            P.dma("sp", lambda e: e.dma_start(out=self.contrib2[l][h * 128:(h + 1) * 128, b * 512:(b + 1) * 512], in_=ys[:, :]),
                  r=[ysk], w=[f"ctb2_{h}_{b}"])

        n = len(tiles)
        SK = 2
        done2 = 0
        for step in range(n + SK):
            newhead = step < n and tiles[step]["kt"] == 0 and tiles[step]["b"] == 0
            if newhead:
                while done2 < step:
                    stage2(done2, tiles[done2])
                    done2 += 1
            if step < n:
                stage1(step, tiles[step])
            if step >= SK and done2 <= step - SK:
                stage2(done2, tiles[done2])
                done2 += 1
        while done2 < n:
            stage2(done2, tiles[done2])
            done2 += 1
        if self.dbg and "ya2" in self.dbg:
            P.dma("sp", lambda e: e.dma_start(out=self.dbg_out["ya2"][:, :], in_=self.contrib2[l][:, :]),
                  r=[f"ctb2_{h}_{b}" for h in range(4) for b in range(4)], w=["dbgya"])

    def phaseC(self, l):
        with contextlib.ExitStack() as ph:
            self.ph = ph
            self._phaseC(l)
            self.P.barrier()

    def _phaseC(self, l):
        P = self.P
        sb = self.psb
        kv = self.kvall[l]
        kvk = [f"kvall{l}"]
        wcz = sb("wcz", [128, 8, 512], BF16)
        self.load_w(wcz[:, :, :], "wcz", self.w_in[l][:, C_CZ:C_CZ + 512], 512)
        self.allgather2(l)
        KW = [sb(f"KW{i}", [128, 2, 2048], BF16) for i in range(2)]
        VW = [sb(f"VW{i}", [128, 2, 16, 128], BF16) for i in range(2)]
        QA = [sb(f"QCa{i}", [128, 2048], BF16) for i in range(2)]
        QB = [sb(f"QCb{i}", [128, 2048], BF16) for i in range(2)]
        E = [sb(f"EC{i}", [128, 512], BF16) for i in range(4)]
        fin = [sb(f"finC{i}", [128, 512], F32) for i in range(3)]
        czs = sb("czs", [128, 512], F32)
        for i in range(2):
            P.op("dve", lambda e, i=i: e.memset(QA[i][64:128, :], 0.0), w=[f"QAz{i}"])
            P.op("dve", lambda e, i=i: e.memset(QB[i][0:64, :], 0.0), w=[f"QBz{i}"])
        def cgather(e):
            kv3 = kv.rearrange("(o r) c -> o r c", o=NCORES + 1)[:, R_KCT:R_TOT, :]
            return e.dma_start(out=self.cown[l][:, :, :], in_=kv3[bass.ds(P.regs["p1"], 2), :, :])
        P.dma("sp", cgather, r=kvk, w=["cown"])
        co = self.cown[l]
        it = 0
        for pr in range(4):
            pb = pr % 2
            for hf in range(2):
                P.dma("sp", lambda e, pb=pb, hf=hf, pr=pr: e.dma_start(out=KW[pb][:, hf, :], in_=co[hf, pr * 128:(pr + 1) * 128, :]),
                      r=["cown"], w=[f"KW{pb}_{hf}"])

                def vload(e, pb=pb, hf=hf, pr=pr):
                    src = co[hf, 512:1024, :].rearrange("(k a) c -> k (a c)", k=128)
                    src = src.rearrange("k (t c) -> k t c", t=16)[:, :, pr * 128:(pr + 1) * 128]
                    return e.dma_start(out=VW[pb][:, hf, :, :], in_=src)
                P.dma("sp", vload, r=["cown"], w=[f"VW{pb}_{hf}"])
            P.op("dve", lambda e, pb=pb, pr=pr, QCT=self.QCT: e.tensor_copy(out=QA[pb][0:64, :], in_=QCT[0:64, pr, :]),
                 r=[f"QCT{t}" for t in range(NT)], w=[f"QAd{pb}"])
            P.op("dve", lambda e, pb=pb, pr=pr, QCT=self.QCT: e.tensor_copy(out=QB[pb][64:128, :], in_=QCT[64:128, pr, :]),
                 r=[f"QCT{t}" for t in range(NT)], w=[f"QBd{pb}"])
            kq = [[f"QAd{pb}", f"QAz{pb}"], [f"QBd{pb}", f"QBz{pb}"]]
            Qp = [QA[pb], QB[pb]]
            for qg in range(4):
                kts = list(range(4 * qg, 4 * qg + 20))
                units = []
                for idx, kt in enumerate(kts):
                    hf, t = kt // 16, kt % 16
                    j0 = max(0, kt - (16 + 4 * qg))
                    j1 = min(3, kt - 4 * qg)
                    D0 = 16 + 4 * qg + j0 - kt
                    nj = j1 - j0 + 1
                    for hh in range(2):
                        units.append(dict(hf=hf, t=t, D0=D0, nj=nj, hh=hh, first=(idx == 0), last=(idx == len(kts) - 1),
                                          sl=slice(j0 * 128, (j1 + 1) * 128),
                                          qs=slice(qg * 512 + j0 * 128, qg * 512 + (j1 + 1) * 128)))

                def cst1(U, itn, pb=pb, Qp=Qp, kq=kq):
                    sbk = itn % 4
                    Eb = E[itn % 4]
                    ek = f"EC{itn % 4}"
                    hf, t, sl, qs, hh, D0, nj = U["hf"], U["t"], U["sl"], U["qs"], U["hh"], U["D0"], U["nj"]
                    P.op("pe", lambda e: e.matmul(out=self.pbank[sbk][:, sl], lhsT=KW[pb][:, hf, t * 128:(t + 1) * 128], rhs=Qp[hh][:, qs],
                                                  start=True, stop=True), r=[f"KW{pb}_{hf}"] + kq[hh], w=[f"pb{sbk}"])
                    P.op("act", lambda e: e.activation(out=Eb[:, sl], in_=self.pbank[sbk][:, sl], func=AF.Exp, scale=0.125),
                         r=[f"pb{sbk}"], w=[ek])
                    P.op("dve", lambda e: e.tensor_tensor(out=Eb[:, sl].rearrange("p (j q) -> p j q", j=nj),
                                                          in0=Eb[:, sl].rearrange("p (j q) -> p j q", j=nj),
                                                          in1=self.maskC[:, D0:D0 + nj, :], op=ALU.mult), r=[ek, "maskC"], w=[ek])

                def cst2(U, itn, pb=pb):
                    Eb = E[itn % 4]
                    ek = f"EC{itn % 4}"
                    hf, t, sl, hh, first, last = U["hf"], U["t"], U["sl"], U["hh"], U["first"], U["last"]
                    ob, rb = 4 + 2 * hh, 5 + 2 * hh
                    P.op("pe", lambda e: e.matmul(out=self.pbank[ob][:, sl], lhsT=VW[pb][:, hf, t, :], rhs=Eb[:, sl], start=first, stop=last,
                                                  skip_group_check=True), r=[f"VW{pb}_{hf}", ek], w=[f"pb{ob}"])
                    lh = self.vhalo if hf == 0 else self.ones
                    P.op("pe", lambda e: e.matmul(out=self.pbank[rb][:, sl], lhsT=lh[:, :], rhs=Eb[:, sl], start=first, stop=last,
                                                  skip_group_check=True), r=["vhalo", "ones", ek], w=[f"pb{rb}"])
                nu = len(units)
                SK = 2
                for step in range(nu + SK):
                    if step < nu:
                        cst1(units[step], it + step)
                    if step >= SK:
                        cst2(units[step - SK], it + step - SK)
                it += nu
                f0, f1, f2 = [x[:, :] for x in fin]
                P.op("dve", lambda e: e.reciprocal(out=fin[0][0:64, :], in_=self.pbank[5][0:64, :]), r=["pb5"], w=["fc0a"])
                P.op("dve", lambda e: e.reciprocal(out=fin[0][64:128, :], in_=self.pbank[7][64:128, :]), r=["pb7"], w=["fc0b"])
                P.op("dve", lambda e: e.tensor_tensor(out=fin[1][0:64, :], in0=self.pbank[4][0:64, :], in1=fin[0][0:64, :], op=ALU.mult),
                     r=["pb4", "fc0a"], w=["fc1a"])
                P.op("dve", lambda e: e.tensor_tensor(out=fin[1][64:128, :], in0=self.pbank[6][64:128, :], in1=fin[0][64:128, :], op=ALU.mult),
                     r=["pb6", "fc0b"], w=["fc1b"])
                sbk = it % 4
                it += 1
                gs = slice(qg * 512, (qg + 1) * 512)
                for k in range(8):
                    P.op("pe", lambda e, k=k, pr=pr, gs=gs, sbk=sbk: e.matmul(out=self.pbank[sbk][:, :], lhsT=wcz[:, k, pr * 128:(pr + 1) * 128],
                                                                           rhs=self.hT[:, k, gs], start=(k == 0), stop=(k == 7)),
                         r=["wcz"] + self.k_hT, w=[f"pb{sbk}"])
                P.op("act", lambda e, sbk=sbk: e.activation(out=czs[:, :], in_=self.pbank[sbk][:, :], func=AF.Silu), r=[f"pb{sbk}"], w=["czs"])
                P.op("dve", lambda e, pr=pr, gs=gs: e.tensor_tensor(out=self.ycT[:, pr, gs], in0=f1, in1=czs[:, :], op=ALU.mult),
                     r=["fc1a", "fc1b", "czs"], w=[f"ycT{pr}_{qg}"])
        if self.dbg and "ycT" in self.dbg:
            P.dma("sp", lambda e: e.dma_start(out=self.dbg_out["ycT"][:, :, :], in_=self.ycT[:, :, :]),
                  r=[f"ycT{pr}_{qg}" for pr in range(4) for qg in range(4)], w=["dbgyc"])

    def phase4(self, l):
        with contextlib.ExitStack() as ph:
            self.ph = ph
            self._phase4(l)
            self.P.barrier()

    def _phase4(self, l):
        P = self.P
        sb = self.psb
        last = (l == 1) and self.final_norm
        x_src = self.x_in if l == 0 else self.x1
        wG = sb("wG", [128, 8, 3072], BF16)
        wAZ = sb("wAZ", [128, 8, 512], BF16)
        wBR = sb("wBR", [128, 12, 1024], BF16)
        wO = sb("wO", [128, 8, 1024], BF16)
        self.load_w(wAZ[:, :, :], "wAZ", self.w_in[l][:, C_AZ:C_AZ + 512], 512)
        for n in range(3):
            self.load_w(wBR[:, 4 * n:4 * n + 4, :], f"wBR{n}", self.w_branch[l][n], 1024)
        for n in range(3):
            self.load_w(wG[:, :, n * 1024:(n + 1) * 1024], f"wG{n}", self.w_in[l][:, C_GL + n * 1024:C_GL + (n + 1) * 1024], 1024)
        self.load_w(wO[:, :, :], "wO", self.w_out[l], 1024)
        yaraw = [sb(f"yaraw{i}", [128, 4, 512], BF16) for i in range(1)] * 2
        yag = yaraw
        azs = [sb(f"azs{i}", [128, 512], BF16) for i in range(2)]
        sg = [sb(f"sg{i}", [128, 512], F32) for i in range(3)]
        mm = sg
        mT = [sb(f"mT{i}", [128, 8, 512], BF16) for i in range(1)] * 2
        xr_ = [sb(f"xres{i}", [128, D], F32) for i in range(1)] * 2
        xo = xr_
        if last:
            fg = sb("fg", [128, D], F32)
            P.dma("sp", lambda e: e.dma_start(out=fg[:, :], in_=self.final_g.partition_broadcast(128)), w=["fg"])
            junk = self.zeros
            ss = [sb(f"ss4{i}", [128, 1], F32) for i in range(2)]
        ya = self.yaall[l]
        for tb in range(2):
            def ygather(e, tb=tb):
                src = ya.rearrange("(w r) c -> w r c", w=8)[:, :, tb * 128:tb * 128 + 7 * 256 + 128][:, :, bass.ds(P.regs["p256"], 128)]
                dst = self.yaown[l].rearrange("r (tb w c) -> w r tb c", tb=2, w=8)[:, :, tb, :]
                return e.dma_start(out=dst, in_=src)
            P.dma("sp", ygather, r=[f"yaall{l}"], w=[f"yaown_{tb}"])
        cnt = 0
        for q4 in range(4):
            gb = q4 % 2
            gs = slice(q4 * 512, (q4 + 1) * 512)
            for tt in range(4):
                t = q4 * 4 + tt
                P.dma("sp", lambda e, gb=gb, tt=tt, t=t: e.dma_start(
                    out=yaraw[gb][:, :, tt * 128:(tt + 1) * 128],
                    in_=self.yaown[l][:, t * 128:(t + 1) * 128].rearrange("(h p) c -> p h c", h=4)),
                    r=["yaown_0", "yaown_1"], w=[f"yab_{h}" for h in range(4)])
            kyr = []
            for h in range(4):
                i2 = cnt % 2
                cnt += 1
                bk = i2
                for k in range(8):
                    P.op("pe", lambda e, k=k, h=h, bk=bk, gs=gs: e.matmul(out=self.pbank[bk][:, :], lhsT=wAZ[:, k, h * 128:(h + 1) * 128],
                                                                         rhs=self.hT[:, k, gs], start=(k == 0), stop=(k == 7)),
                         r=["wAZ"] + self.k_hT, w=[f"pb{bk}"])
                P.op("act", lambda e, bk=bk, i2=i2: e.activation(out=azs[i2][:, :], in_=self.pbank[bk][:, :], func=AF.Silu),
                     r=[f"pb{bk}"], w=[f"azs{i2}"])
                P.op("pool", lambda e, h=h, i2=i2, gb=gb: e.tensor_tensor(out=yag[gb][:, h, :], in0=yaraw[gb][:, h, :], in1=azs[i2][:, :], op=ALU.mult),
                     r=kyr + [f"azs{i2}", f"yab_{h}"], w=[f"yab_{h}"])
            kyg = [f"yab_{h}" for h in range(4)]
            ysrc = [(lambda cc, gb=gb: yag[gb][:, cc, :], kyg),
                    (lambda cc, gs=gs: self.ybT[:, cc, gs], [f"ybT{t}" for t in range(NT)]),
                    (lambda cc, gs=gs: self.ycT[:, cc, gs], [f"ycT{pr}_{qg}" for pr in range(4) for qg in range(4)])]
            for dc in range(8):
                for n in range(3):
                    getter, keys = ysrc[n]
                    for cc in range(4):
                        P.op("pe", lambda e, n=n, cc=cc, dc=dc, getter=getter: e.matmul(
                            out=self.pbank[2 + n][:, :], lhsT=wBR[:, 4 * n + cc, dc * 128:(dc + 1) * 128], rhs=getter(cc),
                            start=(cc == 0), stop=(cc == 3)), r=[f"wBR{n}"] + keys, w=[f"pb{2 + n}"])
                for n in range(3):
                    for k in range(8):
                        P.op("pe", lambda e, n=n, k=k, dc=dc, gs=gs: e.matmul(
                            out=self.pbank[5 + n][:, :], lhsT=wG[:, k, n * 1024 + dc * 128:n * 1024 + (dc + 1) * 128],
                            rhs=self.hT[:, k, gs], start=(k == 0), stop=(k == 7)), r=[f"wG{n}"] + self.k_hT, w=[f"pb{5 + n}"])
                for n in range(3):
                    P.op("act", lambda e, n=n: e.activation(out=sg[n][:, :], in_=self.pbank[5 + n][:, :], func=AF.Sigmoid),
                         r=[f"pb{5 + n}"], w=[f"sg{n}"])
                    P.op("dve", lambda e, n=n: e.tensor_tensor(out=mm[n][:, :], in0=self.pbank[2 + n][:, :], in1=sg[n][:, :], op=ALU.mult),
                         r=[f"pb{2 + n}", f"sg{n}"], w=[f"sg{n}"])
                P.op("pool", lambda e: e.tensor_tensor(out=mm[0][:, :], in0=mm[0][:, :], in1=mm[1][:, :], op=ALU.add), r=["sg0", "sg1"], w=["sg0"])
                P.op("pool", lambda e, gb=gb, dc=dc: e.tensor_tensor(out=mT[gb][:, dc, :], in0=mm[0][:, :], in1=mm[2][:, :], op=ALU.add),
                     r=["sg0", "sg2"], w=[f"mT_{dc}"])
            kmT = [f"mT_{dc}" for dc in range(8)]
            for tt in range(4):
                t = q4 * 4 + tt
                xb = 0
                P.dma("sp", lambda e, t=t, xb=xb: e.dma_start(out=xr_[xb][:, :], in_=x_src[t * 128:(t + 1) * 128, :]),
                      r=(["x1w"] if l == 1 else []), w=[f"xres{xb}"])
                for eg in range(2):
                    bk = eg
                    for dc in range(8):
                        P.op("pe", lambda e, dc=dc, eg=eg, bk=bk, gb=gb, tt=tt: e.matmul(
                            out=self.pbank[bk][:, :], lhsT=mT[gb][:, dc, tt * 128:(tt + 1) * 128], rhs=wO[:, dc, eg * 512:(eg + 1) * 512],
                            start=(dc == 0), stop=(dc == 7)), r=["wO"] + kmT, w=[f"pb{bk}"])
                    P.op("dve", lambda e, eg=eg, bk=bk, xb=xb: e.tensor_tensor(out=xo[xb][:, eg * 512:(eg + 1) * 512], in0=self.pbank[bk][:, :],
                                                                              in1=xr_[xb][:, eg * 512:(eg + 1) * 512], op=ALU.add),
                         r=[f"pb{bk}", f"xres{xb}"], w=[f"xres{xb}"])
                kxo = [f"xres{xb}"]
                if not last:
                    dst = self.x1 if (l == 0 and len(self.layers) > 1) else self.out
                    P.dma("sp", lambda e, t=t, xb=xb, dst=dst: e.dma_start(out=dst[t * 128:(t + 1) * 128, :], in_=xo[xb][:, :]),
                          r=kxo, w=["x1w"])
                else:
                    P.op("act", lambda e, xb=xb: e.activation(out=junk[:, :], in_=xo[xb][:, :], func=AF.Square, accum_out=ss[xb][:, :]),
                         r=kxo, w=["zeros", f"ss4{xb}"])
                    P.op("dve", lambda e, xb=xb: e.tensor_scalar(out=ss[xb][:, :], in0=ss[xb][:, :], scalar1=1.0 / D, scalar2=EPS,
                                                                op0=ALU.mult, op1=ALU.add), r=[f"ss4{xb}"], w=[f"ss4{xb}"])
                    P.op("act", lambda e, xb=xb: e.activation(out=ss[xb][:, :], in_=ss[xb][:, :], func=AF.Sqrt), r=[f"ss4{xb}"], w=[f"ss4{xb}"])
                    P.op("dve", lambda e, xb=xb: e.reciprocal(out=ss[xb][:, :], in_=ss[xb][:, :]), r=[f"ss4{xb}"], w=[f"ss4{xb}"])
                    P.op("dve", lambda e, xb=xb: e.scalar_tensor_tensor(out=xo[xb][:, :], in0=xo[xb][:, :], scalar=ss[xb][:, 0:1], in1=fg[:, :],
                                                                       op0=ALU.mult, op1=ALU.mult), r=kxo + [f"ss4{xb}", "fg"], w=kxo)
                    P.dma("sp", lambda e, t=t, xb=xb: e.dma_start(out=self.out[t * 128:(t + 1) * 128, :], in_=xo[xb][:, :]),
                          r=kxo, w=[f"outw{t}"])

    def phase1(self, l):
        with contextlib.ExitStack() as ph:
            self.ph = ph
            self._phase1(l)
            self.P.barrier()

    def _phase1(self, l):
        P = self.P
        nc = self.nc
        sb = self.psb
        self.xt = [sb(f"xt{i}", [128, D], F32) for i in range(2)]
        self.junk = sb("junk", [128, D], BF16)
        self.ss = [sb(f"ss{i}", [128, 1], F32) for i in range(2)]
        self.rstd = [sb(f"rstd{i}", [128, 1], F32) for i in range(2)]
        self.xs = [sb(f"xs{i}", [128, D], BF16) for i in range(2)]
        self.wP1 = sb("wP1", [128, 8, 3072], BF16)
        self.qk = [sb(f"qk{i}", [128, 4, 8, 64], BF16) for i in range(2)]
        self.xr = [sb(f"xr{i}", [128, 32, 16], F32) for i in range(2)]
        self.rt = [sb(f"rt{i}", [128, 32, 16], F32) for i in range(2)]
        self.ru = [sb(f"ru{i}", [128, 32, 16], F32) for i in range(2)]
        self.vsb = [sb(f"vsb{i}", [128, 2, 512], BF16) for i in range(2)]
        self.tsb = [sb(f"tsb{i}", [128, 16, 128], BF16) for i in range(2)]
        x_src = self.x_in if l == 0 else self.x1
        for gi, c in enumerate((C_AQ, C_AK, C_CQ, C_CK, C_AV, C_CV)):
            self.load_w(self.wP1[:, :, gi * 512:(gi + 1) * 512], f"wP1_{gi}", self.w_in[l][:, c:c + 512], 512)
        wk = [f"wP1_{g}" for g in range(6)]
        P.dma("sp", lambda e: e.dma_start(out=self.gT[:, :], in_=self.norm_g[l].rearrange("(k p) -> p k", p=128),
                                          allow_slow_non_contiguous=True), w=["gT"])
        ctb = self.contrib[l]
        self.k_contrib = [f"ctb_{n}{t}" for n in ("va", "vc", "qa", "ka", "kc") for t in range(NT)] + self.k_kvpad[l]
        self.k_hT = [f"hT{t}" for t in range(NT)]
        for t in range(NT):
            b = t % 2
            xt, ss, rstd, xs, qk, xr, rt, ru, vsb, tsb = (self.xt[b], self.ss[b], self.rstd[b], self.xs[b], self.qk[b],
                                                          self.xr[b], self.rt[b], self.ru[b], self.vsb[b], self.tsb[b])
            kb = f"_{b}"
            P.dma("sp", lambda e, t=t, xt=xt: e.dma_start(out=xt[:, :], in_=x_src[t * 128:(t + 1) * 128, :]),
                  w=["xt" + kb])
            P.op("act", lambda e, xt=xt, ss=ss, junk=self.junk: e.activation(out=junk[:, :], in_=xt[:, :], func=AF.Square,
                                                           accum_out=ss[:, :]),
                 r=["xt" + kb], w=["junk", "ss" + kb])
            P.op("dve", lambda e, ss=ss, rstd=rstd: e.tensor_scalar(out=rstd[:, :], in0=ss[:, :], scalar1=1.0 / D,
                                                                  scalar2=EPS, op0=ALU.mult, op1=ALU.add),
                 r=["ss" + kb], w=["rstd" + kb])
            P.op("act", lambda e, rstd=rstd: e.activation(out=rstd[:, :], in_=rstd[:, :], func=AF.Sqrt),
                 r=["rstd" + kb], w=["rstd" + kb])
            P.op("dve", lambda e, rstd=rstd: e.reciprocal(out=rstd[:, :], in_=rstd[:, :]),
                 r=["rstd" + kb], w=["rstd" + kb])
            P.op("dve", lambda e, xt=xt, xs=xs, rstd=rstd: e.tensor_scalar(out=xs[:, :], in0=xt[:, :],
                                                                         scalar1=rstd[:, 0:1], scalar2=None,
                                                                         op0=ALU.mult),
                 r=["xt" + kb, "rstd" + kb], w=["xs" + kb])
            pT = self.pbf(0)
            for k in range(8):
                P.op("pe", lambda e, k=k, xs=xs: e.transpose(out=pT[:, k * 128:(k + 1) * 128],
                                                             in_=xs[:, k * 128:(k + 1) * 128], identity=self.ident[:, :]),
                     r=["xs" + kb, "ident"], w=["pb0"])
            hTt = self.hT[:, :, t * 128:(t + 1) * 128]
            P.op("dve", lambda e, hTt=hTt: e.tensor_tensor(out=hTt, in0=pT.rearrange("p (k t) -> p k t", k=8),
                                                          in1=bc_free(self.gT[:, :], 2, 128), op=ALU.mult),
                 r=["pb0", "gT"], w=[f"hT{t}"])
            for g in range(6):
                bank = 1 + g
                for k in range(8):
                    P.op("pe", lambda e, g=g, k=k, bank=bank, t=t, wP1=self.wP1: e.matmul(
                        out=self.pbank[bank][:, :], lhsT=self.hT[:, k, t * 128:(t + 1) * 128],
                        rhs=wP1[:, k, g * 512:(g + 1) * 512], start=(k == 0), stop=(k == 7)),
                         r=[f"hT{t}", wk[g]], w=[f"pb{bank}"])
            for g in range(4):
                src = self.pbank[1 + g][:, :].rearrange("p (h d) -> p h d", d=64)
                eng = "act" if g % 2 == 0 else "dve"
                if eng == "act":
                    P.op("act", lambda e, g=g, src=src, qk=qk: e.copy(out=qk[:, g, :, 16:64], in_=src[:, :, 16:64]),
                         r=[f"pb{1 + g}"], w=[f"qkrest{g}" + kb])
                else:
                    P.op("dve", lambda e, g=g, src=src, qk=qk: e.tensor_copy(out=qk[:, g, :, 16:64], in_=src[:, :, 16:64]),
                         r=[f"pb{1 + g}"], w=[f"qkrest{g}" + kb])
                if eng == "act":
                    P.op("act", lambda e, g=g, src=src, xr=xr: e.copy(out=xr[:, g * 8:(g + 1) * 8, :], in_=src[:, :, 0:16]),
                         r=[f"pb{1 + g}"], w=[f"xr{g}" + kb])
                else:
                    P.op("dve", lambda e, g=g, src=src, xr=xr: e.tensor_copy(out=xr[:, g * 8:(g + 1) * 8, :], in_=src[:, :, 0:16]),
                         r=[f"pb{1 + g}"], w=[f"xr{g}" + kb])
            xrk = [f"xr{g}" + kb for g in range(4)]
            csb = bc_free(self.cs[:, t, :], 1, 32)
            P.op("pool", lambda e, xr=xr, rt=rt, csb=csb: e.tensor_tensor(out=rt[:, :, :], in0=xr[:, :, :], in1=csb, op=ALU.mult),
                 r=xrk + self.k_rope, w=["rt" + kb])
            P.op("pool", lambda e, xr=xr, ru=ru, t=t: e.tensor_tensor(out=ru[:, :, 0:8], in0=xr[:, :, 8:16],
                                                                    in1=bc_free(self.sn[:, t, 0:8], 1, 32), op=ALU.mult),
                 r=xrk + self.k_rope, w=["ru_lo" + kb])
            P.op("pool", lambda e, xr=xr, ru=ru, t=t: e.tensor_tensor(out=ru[:, :, 8:16], in0=xr[:, :, 0:8],
                                                                    in1=bc_free(self.sn[:, t, 8:16], 1, 32), op=ALU.mult),
                 r=xrk + self.k_rope, w=["ru_hi" + kb])
            P.op("pool", lambda e, qk=qk, rt=rt, ru=ru: e.tensor_tensor(
                out=qk[:, :, :, 0:16], in0=rt[:, :, :].rearrange("p (g h) d -> p g h d", g=4),
                in1=ru[:, :, :].rearrange("p (g h) d -> p g h d", g=4), op=ALU.add),
                 r=["rt" + kb, "ru_lo" + kb, "ru_hi" + kb], w=["qkrot" + kb])
            P.op("act", lambda e, vsb=vsb: e.copy(out=vsb[:, 0, :], in_=self.pbank[5][:, :]), r=["pb5"], w=["vsb0" + kb])
            P.op("act", lambda e, vsb=vsb: e.copy(out=vsb[:, 1, :], in_=self.pbank[6][:, :]), r=["pb6"], w=["vsb1" + kb])
            P.dma("sp", lambda e, vsb=vsb, t=t: e.dma_start(
                out=ctb[R_VA:R_VA + 512, t * 128:(t + 1) * 128].rearrange("(h k) e -> k h e", h=4),
                in_=vsb[:, 0, :].rearrange("k (h e) -> k h e", h=4)), r=["vsb0" + kb], w=[f"ctb_va{t}"])
            vc_dst = ctb[R_VC:R_VC + 512, :].rearrange("(k a) c -> k (a c)", k=128)[:, t * 512:(t + 1) * 512]
            P.dma("sp", lambda e, vsb=vsb, vc_dst=vc_dst: e.dma_start(out=vc_dst, in_=vsb[:, 1, :]),
                  r=["vsb1" + kb], w=[f"ctb_vc{t}"])
            qkf = qk[:, :, :, :].rearrange("p g h d -> p (g h d)")
            qkk = [f"qkrest{g}" + kb for g in range(4)] + ["qkrot" + kb]
            p7 = self.pbf(7)
            for rnd in range(2):
                for j in range(8):
                    blk = rnd * 8 + j
                    P.op("pe", lambda e, blk=blk, j=j, qkf=qkf: e.transpose(
                        out=p7[:, j * 128:(j + 1) * 128], in_=qkf[:, blk * 128:(blk + 1) * 128],
                        identity=self.ident[:, :]), r=qkk + ["ident"], w=["pb7"])
                eng = "act" if rnd == 0 else "dve"
                if eng == "act":
                    P.op("act", lambda e, rnd=rnd, tsb=tsb: e.copy(out=tsb[:, rnd * 8:(rnd + 1) * 8, :],
                                                                 in_=p7.rearrange("p (j t) -> p j t", j=8)),
                         r=["pb7"], w=[f"tsb{rnd}" + kb])
                else:
                    P.op("dve", lambda e, rnd=rnd, tsb=tsb: e.tensor_copy(out=tsb[:, rnd * 8:(rnd + 1) * 8, :],
                                                                        in_=p7.rearrange("p (j t) -> p j t", j=8)),
                         r=["pb7"], w=[f"tsb{rnd}" + kb])
            P.dma("sp", lambda e, tsb=tsb, t=t: e.dma_start(
                out=ctb[R_QAT:R_QAT + 512, t * 128:(t + 1) * 128].rearrange("(h p) c -> p h c", h=4),
                in_=tsb[:, 0:4, :]), r=["tsb0" + kb], w=[f"ctb_qa{t}"])
            P.dma("sp", lambda e, tsb=tsb, t=t: e.dma_start(
                out=ctb[R_KAT:R_KAT + 512, t * 128:(t + 1) * 128].rearrange("(h p) c -> p h c", h=4),
                in_=tsb[:, 4:8, :]), r=["tsb0" + kb], w=[f"ctb_ka{t}"])
            P.op("pool", lambda e, tsb=tsb, t=t, QCT=self.QCT: e.tensor_copy(out=QCT[:, :, t * 128:(t + 1) * 128], in_=tsb[:, 8:12, :]),
                 r=["tsb1" + kb], w=[f"QCT{t}"])
            P.dma("sp", lambda e, tsb=tsb, t=t: e.dma_start(
                out=ctb[R_KCT:R_KCT + 512, t * 128:(t + 1) * 128].rearrange("(h p) c -> p h c", h=4),
                in_=tsb[:, 12:16, :]), r=["tsb1" + kb], w=[f"ctb_kc{t}"])
        if self.dbg and "contrib" in self.dbg:
            P.dma("sp", lambda e: e.dma_start(out=self.dbg_out["contrib"][:, :], in_=ctb[:, :]), r=self.k_contrib, w=["dbgc"])
        if self.dbg and "hT" in self.dbg:
            P.dma("sp", lambda e: e.dma_start(out=self.dbg_out["hT"][:, :, :], in_=self.hT[:, :, :]),
                  r=[f"hT{t}" for t in range(NT)], w=["dbgh"])


def make_in_maps(inputs):
    x = np.ascontiguousarray(inputs["x"][0])
    pos = np.ascontiguousarray(inputs["positions"][0]).astype(np.int32)
    common = {k: np.ascontiguousarray(inputs[k]) for k in
              ("norm_g", "w_in", "lam_q1", "lam_k1", "lam_q2", "lam_k2", "subln_g", "sgu_ln_g", "sgu_ln_b",
               "sgu_w", "sgu_b", "w_branch", "w_out", "final_g")}
    maps = []
    for c in range(NCORES):
        m = dict(common)
        m["x"] = x[c * SOWN:(c + 1) * SOWN]
        m["pos"] = np.ascontiguousarray(pos[c * SOWN:(c + 1) * SOWN].reshape(NT, 128).T)
        m["cidf"] = np.full((128, 1), float(c), np.float32)
        maps.append(m)
    return maps


def kernel(**inputs):
    b = Builder()
    nc = b.build()
    res = run_bass_kernel_spmd(nc, make_in_maps(inputs), core_ids=list(range(NCORES)))
    out = np.concatenate([np.asarray(r["out"]) for r in res.results], axis=0)
    return out.reshape(1, S, D).astype(np.float32)
```
